# Optimizing a Trainium2 kernel written in Bass

```python
import jax
import jax.numpy as jnp
from jax import lax
import numpy as np

D_MODEL = 1024
BATCH = 4
SEQ = 4096
DEPTH = 4

GRID_W = 64
CTX_LEN = 256
ROPE_BASE = 10000.0
NORM_EPS = 1e-6
NEG_INF = -1e30
Q_BLOCK = 128
WINDOW = 128
N_MOD = 6

MLA_HEADS = 8
MLA_Q_RANK = 384
MLA_KV_RANK = 256
MLA_NOPE = 64
MLA_ROPE = 32
MLA_V = 64
GA_HEADS = 8
GA_KV_HEADS = 2
GA_HEAD_DIM = 64
GA_GROUP = GA_HEADS // GA_KV_HEADS
WA_HEADS = 8
WA_KV_HEADS = 2
WA_HEAD_DIM = 64
WA_GROUP = WA_HEADS // WA_KV_HEADS
N_BRANCH = 3

PEER_HEADS = 8
PEER_N_KEYS = 128
PEER_N_EXPERTS = PEER_N_KEYS * PEER_N_KEYS
PEER_QUERY_DIM = 256
PEER_HALF = PEER_QUERY_DIM // 2
PEER_TOPK = 16
PEER_CHUNK = 128

IN_SPLITS = (MLA_Q_RANK, MLA_KV_RANK, MLA_ROPE,
             GA_HEADS * GA_HEAD_DIM, GA_KV_HEADS * GA_HEAD_DIM, GA_KV_HEADS * GA_HEAD_DIM,
             WA_HEADS * WA_HEAD_DIM, WA_KV_HEADS * WA_HEAD_DIM, WA_KV_HEADS * WA_HEAD_DIM,
             N_BRANCH * D_MODEL)
IN_COLS = (MLA_Q_RANK + MLA_KV_RANK + MLA_ROPE
           + GA_HEADS * GA_HEAD_DIM + 2 * GA_KV_HEADS * GA_HEAD_DIM
           + WA_HEADS * WA_HEAD_DIM + 2 * WA_KV_HEADS * WA_HEAD_DIM
           + N_BRANCH * D_MODEL)

kernel_name = 'hybrid_mla_gqa_swa_peer_dit'


def rmsnorm(x, g):
    xf = x.astype(jnp.float32)
    y = xf * lax.rsqrt(jnp.mean(xf * xf, axis=-1, keepdims=True) + NORM_EPS)
    return (y * g.astype(jnp.float32)).astype(x.dtype)


def axial_rope_tables(rows, rot_dim, dtype):
    r = jnp.repeat(jnp.arange(rows, dtype=jnp.float32), GRID_W)
    col = jnp.tile(jnp.arange(GRID_W, dtype=jnp.float32), rows)
    n_freq = rot_dim // 4
    inv = ROPE_BASE ** (-jnp.arange(n_freq, dtype=jnp.float32) / n_freq)
    ang = jnp.concatenate([r[:, None] * inv, col[:, None] * inv], axis=-1)
    return jnp.cos(ang).astype(dtype), jnp.sin(ang).astype(dtype)


def apply_rope(x, cos, sin):
    half = x.shape[-1] // 2
    x1, x2 = x[..., :half], x[..., half:]
    c = cos[None, :, None, :]
    s = sin[None, :, None, :]
    return jnp.concatenate([x1 * c - x2 * s, x2 * c + x1 * s], axis=-1)


def split_columns(p):
    bounds = []
    acc = 0
    for w in IN_SPLITS[:-1]:
        acc += w
        bounds.append(acc)
    return jnp.split(p, bounds, axis=-1)


def mixer_heads(p, g_cq, g_ckv, w_uq, w_ukv, g_qn, g_kn, rope_a, rope_h):
    B, T = p.shape[0], p.shape[1]
    cq, ckv, kr, qb, kb, vb, qw, kw, vw, gates = split_columns(p)
    qa = (rmsnorm(cq, g_cq) @ w_uq).reshape(B, T, MLA_HEADS, MLA_NOPE + MLA_ROPE)
    kva = (rmsnorm(ckv, g_ckv) @ w_ukv).reshape(B, T, MLA_HEADS, MLA_NOPE + MLA_V)
    qa_nope, qa_rope = qa[..., :MLA_NOPE], qa[..., MLA_NOPE:]
    ka_nope, va = kva[..., :MLA_NOPE], kva[..., MLA_NOPE:]
    kr = kr[:, :, None, :]
    qb = rmsnorm(qb.reshape(B, T, GA_HEADS, GA_HEAD_DIM), g_qn)
    kb = rmsnorm(kb.reshape(B, T, GA_KV_HEADS, GA_HEAD_DIM), g_kn)
    vb = vb.reshape(B, T, GA_KV_HEADS, GA_HEAD_DIM)
    qw = qw.reshape(B, T, WA_HEADS, WA_HEAD_DIM)
    kw = kw.reshape(B, T, WA_KV_HEADS, WA_HEAD_DIM)
    vw = vw.reshape(B, T, WA_KV_HEADS, WA_HEAD_DIM)
    if rope_a is not None:
        qa_rope = apply_rope(qa_rope, rope_a[0], rope_a[1])
        kr = apply_rope(kr, rope_a[0], rope_a[1])
        qb = apply_rope(qb, rope_h[0], rope_h[1])
        kb = apply_rope(kb, rope_h[0], rope_h[1])
        qw = apply_rope(qw, rope_h[0], rope_h[1])
        kw = apply_rope(kw, rope_h[0], rope_h[1])
    qa = jnp.concatenate([qa_nope, qa_rope], axis=-1)[:, :, :, None, :]
    ka = jnp.concatenate([ka_nope, jnp.broadcast_to(kr, (B, T, MLA_HEADS, MLA_ROPE))], axis=-1)
    qb = qb.reshape(B, T, GA_KV_HEADS, GA_GROUP, GA_HEAD_DIM)
    qw = qw.reshape(B, T, WA_KV_HEADS, WA_GROUP, WA_HEAD_DIM)
    return qa, ka, va, qb, kb, vb, qw, kw, vw, gates


def dense_latent_attention(q, k_lat, v_lat, k_ctx, v_ctx, scale):
    B, T, Hkv, G, dk = q.shape
    nb = T // Q_BLOCK
    k = jnp.concatenate([k_lat, k_ctx], axis=1)
    v = jnp.concatenate([v_lat, v_ctx], axis=1)
    qb = jnp.moveaxis(q.reshape(B, nb, Q_BLOCK, Hkv, G, dk), 1, 0)

    def one_block(q_blk):
        s = jnp.einsum('bqhgd,bkhd->bhgqk', q_blk, k).astype(jnp.float32) * scale
        p = jax.nn.softmax(s, axis=-1).astype(v.dtype)
        return jnp.einsum('bhgqk,bkhd->bqhgd', p, v)

    o = lax.map(one_block, qb)
    return jnp.moveaxis(o, 0, 1).reshape(B, T, Hkv * G * v.shape[-1])


def context_attention(q, k, v, scale, sink=None):
    B, C, Hkv, G, _ = q.shape
    s = jnp.einsum('bqhgd,bkhd->bhgqk', q, k).astype(jnp.float32) * scale
    if sink is not None:
        s_sink = jnp.broadcast_to(sink.astype(jnp.float32).reshape(1, Hkv, G, 1, 1), s.shape[:-1] + (1,))
        s = jnp.concatenate([s, s_sink], axis=-1)
    p = jax.nn.softmax(s, axis=-1)[..., :k.shape[1]].astype(v.dtype)
    o = jnp.einsum('bhgqk,bkhd->bqhgd', p, v)
    return o.reshape(B, C, Hkv * G * v.shape[-1])


def windowed_latent_attention(q, k, v, k_ctx, v_ctx, sink, scale):
    B, T, Hkv, G, d = q.shape
    nb = T // Q_BLOCK
    pad = ((0, 0), (Q_BLOCK, Q_BLOCK), (0, 0), (0, 0))

    def band(a):
        ap = jnp.pad(a, pad).reshape(B, nb + 2, Q_BLOCK, Hkv, a.shape[-1])
        return jnp.concatenate([ap[:, :-2], ap[:, 1:-1], ap[:, 2:]], axis=2)

    k_band, v_band = band(k), band(v)
    qb = q.reshape(B, nb, Q_BLOCK, Hkv, G, d)
    s_loc = jnp.einsum('bnqhgd,bnkhd->bnhgqk', qb, k_band).astype(jnp.float32) * scale
    qpos = jnp.arange(nb)[:, None] * Q_BLOCK + jnp.arange(Q_BLOCK)[None, :]
    kpos = jnp.arange(nb)[:, None] * Q_BLOCK - Q_BLOCK + jnp.arange(3 * Q_BLOCK)[None, :]
    dist = kpos[:, None, :] - qpos[:, :, None]
    valid = (jnp.abs(dist) <= WINDOW) & (kpos[:, None, :] >= 0) & (kpos[:, None, :] < T)
    s_loc = jnp.where(valid[None, :, None, None], s_loc, NEG_INF)
    s_ctx = jnp.einsum('bnqhgd,bkhd->bnhgqk', qb, k_ctx).astype(jnp.float32) * scale
    s_sink = jnp.broadcast_to(sink.astype(jnp.float32).reshape(1, 1, Hkv, G, 1, 1), s_loc.shape[:-1] + (1,))
    p = jax.nn.softmax(jnp.concatenate([s_loc, s_ctx, s_sink], axis=-1), axis=-1).astype(v.dtype)
    n_loc = 3 * Q_BLOCK
    n_ctx = k_ctx.shape[1]
    o = (jnp.einsum('bnhgqk,bnkhd->bnqhgd', p[..., :n_loc], v_band)
         + jnp.einsum('bnhgqk,bkhd->bnqhgd', p[..., n_loc:n_loc + n_ctx], v_ctx))
    return o.reshape(B, T, Hkv * G * v.shape[-1])


def merge_branches(o_a, o_b, o_w, gate_logits, w_oa, w_ob, w_ow, w_o):
    g_a, g_b, g_w = jnp.split(jax.nn.sigmoid(gate_logits), N_BRANCH, axis=-1)
    m = g_a * (o_a @ w_oa) + g_b * (o_b @ w_ob) + g_w * (o_w @ w_ow)
    return m @ w_o


def peer(h, w_pq, sub_keys, u_tab, v_tab):
    lead = h.shape[:-1]
    D = h.shape[-1]
    hf = h.reshape(-1, D)
    n = hf.shape[0]
    q = (hf @ w_pq).reshape(n, PEER_HEADS, 2, PEER_HALF)
    s = jnp.einsum('nhpd,hpkd->nhpk', q, sub_keys).astype(jnp.float32)
    s1, i1 = lax.top_k(s[:, :, 0], PEER_TOPK)
    s2, i2 = lax.top_k(s[:, :, 1], PEER_TOPK)
    cand_s = (s1[..., :, None] + s2[..., None, :]).reshape(n, PEER_HEADS, PEER_TOPK * PEER_TOPK)
    cand_i = (i1[..., :, None] * PEER_N_KEYS + i2[..., None, :]).reshape(n, PEER_HEADS, PEER_TOPK * PEER_TOPK)
    top_s, pos = lax.top_k(cand_s, PEER_TOPK)
    idx = jnp.take_along_axis(cand_i, pos, axis=-1)
    g = jax.nn.softmax(top_s, axis=-1).astype(h.dtype)
    nc = n // PEER_CHUNK

    def chunk(args):
        hc, ic, gc = args
        u = jnp.take(u_tab, ic, axis=0)
        a = jax.nn.gelu(jnp.einsum('cd,chkd->chk', hc, u), approximate=False)
        v = jnp.take(v_tab, ic, axis=0)
        return jnp.einsum('chk,chkd->cd', gc * a, v)

    out = lax.map(chunk, (hf.reshape(nc, PEER_CHUNK, D),
                          idx.reshape(nc, PEER_CHUNK, PEER_HEADS, PEER_TOPK),
                          g.reshape(nc, PEER_CHUNK, PEER_HEADS, PEER_TOPK)))
    return out.reshape(*lead, D)


def setup_inputs(seed: int = 0) -> dict:
    key = jax.random.key(seed)
    ks = jax.random.split(key, 26)
    f32 = jnp.float32
    L, D = DEPTH, D_MODEL

    def nrm(k, shape, scale):
        return jax.random.normal(k, shape, f32) * scale

    def gain(k, shape):
        return 1.0 + 0.02 * jax.random.normal(k, shape, f32)

    return {
        'x': nrm(ks[0], (BATCH, SEQ, D), 1.0),
        'c': nrm(ks[1], (BATCH, D), 1.0),
        'ctx': nrm(ks[2], (BATCH, CTX_LEN, D), 1.0),
        'c_ctx': nrm(ks[3], (D,), 1.0),
        'w_mod': nrm(ks[4], (L, D, N_MOD * D), 0.5 * D ** -0.5),
        'b_mod': nrm(ks[5], (L, N_MOD * D), 0.01),
        'g_attn': gain(ks[6], (L, D)),
        'g_ffn': gain(ks[7], (L, D)),
        'w_in': nrm(ks[8], (L, D, IN_COLS), D ** -0.5),
        'g_cq': gain(ks[9], (L, MLA_Q_RANK)),
        'g_ckv': gain(ks[10], (L, MLA_KV_RANK)),
        'w_uq': nrm(ks[11], (L, MLA_Q_RANK, MLA_HEADS * (MLA_NOPE + MLA_ROPE)), MLA_Q_RANK ** -0.5),
        'w_ukv': nrm(ks[12], (L, MLA_KV_RANK, MLA_HEADS * (MLA_NOPE + MLA_V)), MLA_KV_RANK ** -0.5),
        'g_qn': gain(ks[13], (L, GA_HEAD_DIM)),
        'g_kn': gain(ks[14], (L, GA_HEAD_DIM)),
        'sink': nrm(ks[15], (L, WA_HEADS), 0.5),
        'w_oa': nrm(ks[16], (L, MLA_HEADS * MLA_V, D), (MLA_HEADS * MLA_V) ** -0.5),
        'w_ob': nrm(ks[17], (L, GA_HEADS * GA_HEAD_DIM, D), (GA_HEADS * GA_HEAD_DIM) ** -0.5),
        'w_ow': nrm(ks[18], (L, WA_HEADS * WA_HEAD_DIM, D), (WA_HEADS * WA_HEAD_DIM) ** -0.5),
        'w_o': nrm(ks[19], (L, D, D), D ** -0.5),
        'w_pq': nrm(ks[20], (L, D, PEER_HEADS * PEER_QUERY_DIM), D ** -0.5),
        'sub_keys': nrm(ks[21], (L, PEER_HEADS, 2, PEER_N_KEYS, PEER_HALF), PEER_HALF ** -0.5),
        'peer_u': nrm(ks[22], (L, PEER_N_EXPERTS, D), D ** -0.5),
        'peer_v': nrm(ks[23], (L, PEER_N_EXPERTS, D), 0.25),
        'g_final': gain(ks[24], (D,)),
    }


def reference(x, c, ctx, c_ctx, w_mod, b_mod, g_attn, g_ffn, w_in, g_cq, g_ckv, w_uq, w_ukv,
              g_qn, g_kn, sink, w_oa, w_ob, w_ow, w_o, w_pq, sub_keys, peer_u, peer_v, g_final):
    B, T, _ = x.shape
    rows = T // GRID_W
    rope_a = axial_rope_tables(rows, MLA_ROPE, x.dtype)
    rope_h = axial_rope_tables(rows, GA_HEAD_DIM, x.dtype)
    scale_a = (MLA_NOPE + MLA_ROPE) ** -0.5
    scale_b = GA_HEAD_DIM ** -0.5
    scale_w = WA_HEAD_DIM ** -0.5
    silu_c = jax.nn.silu(c)
    silu_cc = jax.nn.silu(c_ctx)
    for l in range(DEPTH):
        last = l == DEPTH - 1
        mod_x = (silu_c @ w_mod[l] + b_mod[l])[:, None, :]
        mod_c = silu_cc @ w_mod[l] + b_mod[l]
        sh1x, sc1x, gt1x, sh2x, sc2x, gt2x = jnp.split(mod_x, N_MOD, axis=-1)
        sh1c, sc1c, gt1c, sh2c, sc2c, gt2c = jnp.split(mod_c, N_MOD, axis=-1)

        hx = rmsnorm(x, g_attn[l]) * (1 + sc1x) + sh1x
        hc = rmsnorm(ctx, g_attn[l]) * (1 + sc1c) + sh1c
        qa_x, ka_x, va_x, qb_x, kb_x, vb_x, qw_x, kw_x, vw_x, gates_x = mixer_heads(
            hx @ w_in[l], g_cq[l], g_ckv[l], w_uq[l], w_ukv[l], g_qn[l], g_kn[l], rope_a, rope_h)
        qa_c, ka_c, va_c, qb_c, kb_c, vb_c, qw_c, kw_c, vw_c, gates_c = mixer_heads(
            hc @ w_in[l], g_cq[l], g_ckv[l], w_uq[l], w_ukv[l], g_qn[l], g_kn[l], None, None)

        o_a = dense_latent_attention(qa_x, ka_x, va_x, ka_c, va_c, scale_a)
        o_b = dense_latent_attention(qb_x, kb_x, vb_x, kb_c, vb_c, scale_b)
        o_w = windowed_latent_attention(qw_x, kw_x, vw_x, kw_c, vw_c, sink[l], scale_w)
        x = x + gt1x * merge_branches(o_a, o_b, o_w, gates_x, w_oa[l], w_ob[l], w_ow[l], w_o[l])

        if not last:
            oc_a = context_attention(qa_c, ka_c, va_c, scale_a)
            oc_b = context_attention(qb_c, kb_c, vb_c, scale_b)
            oc_w = context_attention(qw_c, kw_c, vw_c, scale_w, sink[l])
            ctx = ctx + gt1c * merge_branches(oc_a, oc_b, oc_w, gates_c, w_oa[l], w_ob[l], w_ow[l], w_o[l])
            hc2 = rmsnorm(ctx, g_ffn[l]) * (1 + sc2c) + sh2c
            ctx = ctx + gt2c * peer(hc2, w_pq[l], sub_keys[l], peer_u[l], peer_v[l])

        hx2 = rmsnorm(x, g_ffn[l]) * (1 + sc2x) + sh2x
        x = x + gt2x * peer(hx2, w_pq[l], sub_keys[l], peer_u[l], peer_v[l])
    return rmsnorm(x, g_final)
```

```python
import contextlib
import os
import numpy as np
import ml_dtypes
import concourse.bass as bass
import concourse.mybir as mybir
from concourse.bass_utils import run_bass_kernel_spmd

F32 = mybir.dt.float32
BF16 = mybir.dt.bfloat16
AF = mybir.ActivationFunctionType
ALU = mybir.AluOpType
AX = mybir.AxisListType

D = 1024
KC = 8
CTX = 256
GRID_W = 64
EPS = 1e-6
NEG = -30000.0


class Sched:
    LIMIT = int(os.environ.get("SEM_LIMIT", "30000"))

    def __init__(self, nc):
        self.nc = nc
        self.engs = {'pe': nc.tensor, 'act': nc.scalar, 'dve': nc.vector, 'pool': nc.gpsimd, 'sp': nc.sync}
        self.sem = {}
        self.cnt = {}
        self.ver = {}
        self.allsems = []
        self.known = {e: {} for e in self.engs}
        self.last_w = {}
        self.readers = {}
        self.n_inst = 0
        self.n_wait = 0
        self.neng = {e: 0 for e in self.engs}
        self.dcount = {}
        self.DMA_RING = 8
        self.marks = []

    def _bump(self, name, step):
        if name not in self.sem or self.cnt[name] + step > self.LIMIT:
            self.ver[name] = self.ver.get(name, -1) + 1
            h = self.nc.alloc_semaphore(name=f"s_{name}_{self.ver[name]}")
            self.sem[name] = h
            self.cnt[name] = 0
            self.allsems.append([h, 0, name])
            self._cur = None
        self.cnt[name] += step
        for rec in self.allsems:
            if rec[0] is self.sem[name]:
                rec[1] = self.cnt[name]
        return self.sem[name], self.cnt[name]

    def op(self, eng, fn, *args, reads=(), writes=(), dma=None, **kw):
        e = self.engs[eng]
        deps = {}

        def add(d):
            if d is None:
                return
            k = id(d[0])
            if k not in deps or deps[k][1] < d[1]:
                deps[k] = d
        for r in reads:
            add(self.last_w.get(r))
        for w in writes:
            add(self.last_w.get(w))
            for d in self.readers.get(w, {}).values():
                add(d)
        kn = self.known[eng]
        for k, (h, v, own) in deps.items():
            if own == 'pe' and eng == 'pe' and dma is None:
                continue
            if kn.get(k, 0) >= v:
                continue
            e.wait_ge(h, v)
            self.n_wait += 1
            kn[k] = v
        if dma is not None:
            R = self.DMA_RING
            i = self.dcount.get(dma, 0)
            self.dcount[dma] = i + 1
            sname = f"{dma}#{i % R}"
            if sname in self.sem and self.cnt[sname] > 0:
                hp_, vp_ = self.sem[sname], self.cnt[sname]
                if kn.get(id(hp_), 0) < vp_:
                    e.wait_ge(hp_, vp_)
                    self.n_wait += 1
                    kn[id(hp_)] = vp_
        inst = fn(*args, **kw)
        if dma is None:
            self.neng[eng] += 1
        if dma is not None:
            h, v = self._bump(sname, 16)
            inst.then_inc(h, 16)
            rec = (h, v, 'dma:' + sname)
        else:
            h, v = self._bump(eng, 1)
            inst.then_inc(h, 1)
            rec = (h, v, eng)
        for w in writes:
            self.last_w[w] = rec
            self.readers[w] = {}
        for r in reads:
            self.readers.setdefault(r, {})[rec[2]] = rec
        self.n_inst += 1
        return inst

    def barrier(self, engs=None, label=""):
        self.marks.append((label, dict(self.neng)))
        for eng in (engs or self.engs):
            e = self.engs[eng]
            kn = self.known[eng]
            for h, v, nm in self.allsems:
                if nm.startswith('bg'):
                    continue
                if v > 0 and kn.get(id(h), 0) < v:
                    e.wait_ge(h, v)
                    kn[id(h)] = v
                    self.n_wait += 1
        if engs is None:
            self.last_w = {k: v for k, v in self.last_w.items() if str(k).startswith('bg:')}
            self.readers = {k: v for k, v in self.readers.items() if str(k).startswith('bg:')}


def _hp():
    return contextlib.ExitStack()


def build(TL, L, do_attn=True, do_peer=True, ctx_update=True):
    T = TL + CTX
    NT = T // 128
    NTL = TL // 128
    blocks = [(i * 512, 512, 0) for i in range(TL // 512)] + [(TL, CTX, 1)]
    nc = bass.Bass("TRN2", target_bir_lowering=False)
    s = Sched(nc)

    def din(name, shape, dt=F32):
        return nc.dram_tensor(name, list(shape), dt, kind="ExternalInput").ap()

    def dscr(name, shape, dt):
        return nc.dram_tensor(name, list(shape), dt).ap()

    xin = din("xin", [T, D])
    cT2 = din("cT2", [128, KC, 2])
    w_mod = din("w_mod", [L, D, 6 * D])
    b_modT = din("b_modT", [L, 128, 48])
    gvec = din("gvec", [L, 128, 21])
    gqk = din("gqk", [L, 64, 4])
    g_finalT = din("g_finalT", [128, KC])
    sinkb = din("sinkb", [128, L * 8])
    w_in = din("w_in", [L, D, 5280])
    w_in_sw = din("w_in_sw", [L, D, 32 + 640 + 640])
    w_uq = din("w_uq", [L, 384, 768])
    w_uq_sw = din("w_uq_sw", [L, 384, 256])
    w_ukv = din("w_ukv", [L, 256, 1024])
    w_oa = din("w_oa", [L, 512, D])
    w_ob = din("w_ob", [L, 512, D])
    w_ow = din("w_ow", [L, 512, D])
    w_o = din("w_o", [L, D, D])
    w_pq = din("w_pq", [L, D, 2048])
    skT = din("skT", [L, 128, 16, 128])
    GI = 2
    NG = 128 // GI
    peer_uG = din("peer_uG", [L, NG * 128, KC * GI * 128])
    peer_vG = din("peer_vG", [L, NG * 128, GI * D])
    ubf_d = dscr("ubf_d", [NG * 128, KC * GI * 128], BF16)
    vbf_d = dscr("vbf_d", [NG * 128, GI * D], BF16)
    rope64 = din("rope64", [2, 64, T])
    rope32 = din("rope32", [2, 32, T])
    consts = din("consts", [128, 8, 128])
    yout = nc.dram_tensor("yout", [TL // 2, D], F32, kind="ExternalOutput").ap()

    xT_d = dscr("xT_d", [D, T], F32)
    hxT_d = dscr("hxT_d", [D, T], BF16)
    qnT_d = dscr("qnT_d", [8, 64, T], BF16)
    qrT_d = dscr("qrT_d", [8, 32, T], BF16)
    knT_d = dscr("knT_d", [8, 64, T], BF16)
    krT_d = dscr("krT_d", [32, T], BF16)
    vA_d = dscr("vA_d", [T, 8, 65], BF16)
    qbT_d = dscr("qbT_d", [8, 64, T], BF16)
    kbT_d = dscr("kbT_d", [2, 64, T], BF16)
    vB_d = dscr("vB_d", [T, 2, 65], BF16)
    qwT_d = dscr("qwT_d", [8, 64, T], BF16)
    kwT_d = dscr("kwT_d", [2, 64, T], BF16)
    vW_d = dscr("vW_d", [T, 2, 65], BF16)
    gT_d = dscr("gT_d", [3 * D, T], BF16)
    oT_d = [dscr(f"oT_d{m}", [512, T], BF16) for m in range(3)]

    xT_v = xT_d.rearrange("(k p) t -> p k t", p=128)
    hxT_v = hxT_d.rearrange("(k p) t -> p k t", p=128)

    with contextlib.ExitStack() as G:
        uid = [0]

        def SB(name, shape, dt, st=G):
            uid[0] += 1
            return st.enter_context(nc.sbuf_tensor(f"{name}_u{uid[0]}", list(shape), dt))

        def PSt(name, shape, dt, st=G):
            return st.enter_context(nc.psum_tensor(name, list(shape), dt))

        PS = [PSt(f"ps{i}", [128, 512], F32) for i in range(7)]
        PSB = PSt("psb", [128, 1024], BF16)

        def dma(out, in_, reads, writes, q='ld', eng='sp'):
            fn = nc.sync.dma_start if eng == 'sp' else nc.gpsimd.dma_start
            return s.op(eng, fn, out=out, in_=in_, reads=reads, writes=writes, dma=q + eng)

        def mm(out, lhsT, rhs, start, stop, reads, writes):
            return s.op('pe', nc.tensor.matmul, out, lhsT=lhsT, rhs=rhs, start=start, stop=stop,
                        reads=reads, writes=writes)

        def act(out, in_, func, reads, writes, **kw):
            return s.op('act', nc.scalar.activation, out=out, in_=in_, func=func, reads=reads, writes=writes, **kw)

        def tt(eng, out, in0, in1, op, reads, writes):
            fn = nc.vector.tensor_tensor if eng == 'dve' else nc.gpsimd.tensor_tensor
            return s.op(eng, fn, out=out, in0=in0, in1=in1, op=op, reads=reads, writes=writes)

        def ts(eng, out, in0, s1, op0, reads, writes, s2=None, op1=None):
            fn = nc.vector.tensor_scalar if eng == 'dve' else nc.gpsimd.tensor_scalar
            kw = {}
            if op1 is not None:
                kw['op1'] = op1
            return s.op(eng, fn, out=out, in0=in0, scalar1=s1, scalar2=s2, op0=op0, reads=reads, writes=writes, **kw)

        def stt(out, in0, scalar, in1, op0, op1, reads, writes):
            return s.op('dve', nc.vector.scalar_tensor_tensor, out=out, in0=in0, scalar=scalar, in1=in1,
                        op0=op0, op1=op1, reads=reads, writes=writes)

        def cp(eng, out, in_, reads, writes):
            if eng == 'act':
                return s.op('act', nc.scalar.copy, out=out, in_=in_, reads=reads, writes=writes)
            fn = nc.vector.tensor_copy if eng == 'dve' else nc.gpsimd.tensor_copy
            return s.op(eng, fn, out=out, in_=in_, reads=reads, writes=writes)

        cst_f = SB("cst_f", [128, 8, 128], F32)
        cst_b = SB("cst_b", [128, 8, 128], BF16)
        dma(cst_f[:], consts, ['consts'], ['cst_f'])
        cp('dve', cst_b[:], cst_f[:], ['cst_f'], ['cst_b'])
        ident_f = cst_f[:, 0, :]
        ones_f = cst_f[:, 1, :]
        ident_b = cst_b[:, 0, :]
        nm_lo = cst_b[:, 2, :]
        nm_hi = cst_b[:, 3, :]
        seam_wrapL, seam_wrapR, seam_midL, seam_midR = (cst_b[:, 4 + i, :] for i in range(4))
        epsc = SB("epsc", [128, 1], F32)
        s.op('dve', nc.vector.memset, epsc[:], EPS, writes=['epsc'])
        scT = SB("scT", [128, KC, 2], F32)
        dma(scT[:], cT2, ['cT2'], ['scT'])
        act(scT[:], scT[:], AF.Silu, ['scT'], ['scT'])
        esink = SB("esink", [128, L * 8], F32)
        dma(esink[:], sinkb, ['sinkb'], ['esink'])
        act(esink[:], esink[:], AF.Exp, ['esink'], ['esink'])
        gfin = SB("gfin", [128, KC], F32)
        dma(gfin[:], g_finalT, ['gfin_d'], ['gfin'])
        modT = SB("modT", [128, 48, 2], F32)
        gsc = SB("gsc", [128, 2, KC, 2], F32)
        gv = SB("gv", [128, 21], F32)
        gq = SB("gq", [64, 4], F32)
        bmod = SB("bmod", [128, 48], F32)

        def rstd_from_ps(ps_ap, out_ap, n, rkeys, wkeys, np_=128):
            act(out_ap, ps_ap, AF.Sqrt, rkeys, wkeys, scale=1.0 / n, bias=epsc[0:np_, :])
            s.op('dve', nc.vector.reciprocal, out=out_ap, in_=out_ap, reads=wkeys, writes=wkeys)

        with _hp() as P:
            xin_t = [SB(f"xin_t{i}", [128, D], F32, P) for i in range(2)]
            xo_t = [SB(f"xo_t{i}", [128, KC, 128], F32, P) for i in range(2)]
            for j in range(NT):
                a = j % 2
                dma(xin_t[a][:], xin[j * 128:(j + 1) * 128, :], ['xin'], [f'xin_t{a}'])
                for hk in range(2):
                    pst = PS[2 * a + hk]
                    for kk in range(4):
                        k = hk * 4 + kk
                        s.op('pe', nc.tensor.transpose, pst[:, kk * 128:(kk + 1) * 128],
                             xin_t[a][:, k * 128:(k + 1) * 128], ident_f,
                             reads=[f'xin_t{a}', 'cst_f'], writes=[f'ps{2 * a + hk}'])
                    cp('dve' if hk == 0 else 'act', xo_t[a][:, hk * 4:(hk + 1) * 4, :],
                       pst[:].rearrange("p (k t) -> p k t", k=4), [f'ps{2 * a + hk}'], [f'xo_t{a}'])
                dma(xT_v[:, :, j * 128:(j + 1) * 128], xo_t[a][:], [f'xo_t{a}'], ['xT_d'], q='st', eng='pool')
        s.barrier()

        def load_x_and_norm(P_tiles, blk, sub, hx_out, hx_key):
            t0, nt, isc = blk
            xs, sq, rstd, tmpf = P_tiles['xs'], P_tiles['sq'], P_tiles['rstd'], P_tiles['tmpf']
            dma(xs[:, :, 0:nt], xT_v[:, :, t0:t0 + nt], ['xT_d'], ['xs'])
            for k in range(KC):
                a = k % 2
                act(sq[a][:, 0:nt], xs[:, k, 0:nt], AF.Square, ['xs'], [f'sq{a}'])
                mm(PS[6][:, 0:nt], ones_f, sq[a][:, 0:nt], k == 0, k == KC - 1, [f'sq{a}', 'cst_f'], ['ps6'])
            rstd_from_ps(PS[6][:, 0:nt], rstd[:, 0:nt], D, ['ps6', 'epsc'], ['rstd'])
            shoff = 0 if sub == 0 else 24
            for k in range(KC):
                a = k % 2
                stt(tmpf[a][:, 0:nt], xs[:, k, 0:nt], gsc[:, sub, k, isc:isc + 1], rstd[:, 0:nt], ALU.mult, ALU.mult,
                    ['xs', 'gsc', 'rstd'], [f'tmpf{a}'])
                act(hx_out[:, k, 0:nt], tmpf[a][:, 0:nt], AF.Identity, [f'tmpf{a}', 'modT'], [hx_key],
                    bias=modT[:, shoff + k, isc:isc + 1])

        def norm_tiles(P):
            return {'xs': SB("xs", [128, KC, 512], F32, P),
                    'sq': [SB(f"sq{i}", [128, 512], F32, P) for i in range(2)],
                    'rstd': SB("rstd", [128, 512], F32, P),
                    'tmpf': [SB(f"tmpf{i}", [128, 512], F32, P) for i in range(2)]}

        def load_w_bf16(dst, src, key, srckey):
            dma(dst, src, [srckey], [key], q='w', eng='pool')

        for l in range(L):
            last = (l == L - 1)
            with _hp() as P:
                wm = [SB(f"wm{i}", [128, KC, 512], F32, P) for i in range(2)]
                if do_peer:
                    nrow = NG * 128
                    for q8 in range(8):
                        r0, r1 = q8 * nrow // 8, (q8 + 1) * nrow // 8
                        dma(ubf_d[r0:r1, :], peer_uG[l][r0:r1, :], ['peer_uG', 'bg:ld'], ['bg:ubf'], q='bg', eng='pool')
                        dma(vbf_d[r0:r1, :], peer_vG[l][r0:r1, :], ['peer_vG', 'bg:ld'], ['bg:vbf'], q='bg', eng='pool')
                dma(bmod[:], b_modT[l], ['b_modT'], ['bmod'])
                dma(gv[:], gvec[l], ['gvec'], ['gv'])
                dma(gq[:], gqk[l], ['gqk'], ['gq'])
                for nb in range(12):
                    a = nb % 2
                    dma(wm[a][:], w_mod[l].rearrange("(k p) n -> p k n", p=128)[:, :, nb * 512:(nb + 1) * 512],
                        ['w_mod'], [f'wm{a}'])
                    for c in range(4):
                        n = nb * 4 + c
                        for k in range(KC):
                            mm(PS[a][:, c * 2:c * 2 + 2], wm[a][:, k, c * 128:(c + 1) * 128], scT[:, k, :],
                               k == 0, k == KC - 1, [f'wm{a}', 'scT'], [f'ps{a}'])
                        ts('dve', modT[:, n, :], PS[a][:, c * 2:c * 2 + 2], bmod[:, n:n + 1], ALU.add,
                           [f'ps{a}', 'bmod'], ['modT'])
                for sub in range(2):
                    scoff = 8 if sub == 0 else 32
                    goff = 0 if sub == 0 else 8
                    for v in range(2):
                        ts('dve', gsc[:, sub, :, v], modT[:, scoff:scoff + 8, v], 1.0, ALU.add,
                           ['modT'], ['gsc'])
                        tt('dve', gsc[:, sub, :, v], gsc[:, sub, :, v], gv[:, goff:goff + 8], ALU.mult,
                           ['gsc', 'gv'], ['gsc'])
            s.barrier()

            if do_attn:
                with _hp() as P:
                    NTl = norm_tiles(P)
                    hxs = [SB(f"hxs{i}", [128, KC, 512], BF16, P) for i in range(2)]
                    for bi, blk in enumerate(blocks):
                        t0, nt, isc = blk
                        a = bi % 2
                        load_x_and_norm(NTl, blk, 0, hxs[a], f'hxs{a}')
                        dma(hxT_v[:, :, t0:t0 + nt], hxs[a][:, :, 0:nt], [f'hxs{a}'], ['hxT_d'], q='st', eng='pool')
                s.barrier()

                with _hp() as P:
                    wA = SB("wA", [128, KC, 704], BF16, P)
                    wuq = SB("wuq", [128, 3, 1024], BF16, P)
                    wukv = SB("wukv", [128, 2, 1024], BF16, P)
                    wiv = w_in[l].rearrange("(k p) n -> p k n", p=128)
                    load_w_bf16(wA[:, :, 0:672], wiv[:, :, 0:672], 'wA', 'w_in')
                    load_w_bf16(wA[:, :, 672:704], w_in_sw[l].rearrange("(k p) n -> p k n", p=128)[:, :, 0:32], 'wA', 'w_in_sw')
                    load_w_bf16(wuq[:, :, 0:768], w_uq[l].rearrange("(k p) n -> p k n", p=128), 'wuq', 'w_uq')
                    load_w_bf16(wuq[:, :, 768:1024], w_uq_sw[l].rearrange("(k p) n -> p k n", p=128), 'wuq', 'w_uq_sw')
                    load_w_bf16(wukv[:], w_ukv[l].rearrange("(k p) n -> p k n", p=128), 'wukv', 'w_ukv')
                    hxb = [SB(f"hxb{i}", [128, KC, 512], BF16, P) for i in range(2)]
                    r32 = [SB(f"r32_{i}", [32, 2, 512], F32, P) for i in range(2)]
                    cqf = SB("cqf", [128, 5, 512], F32, P)
                    cqn = SB("cqn", [128, 5, 512], BF16, P)
                    sqa = [SB(f"sqa{i}", [128, 512], F32, P) for i in range(2)]
                    rsa = SB("rsa", [128, 2, 512], F32, P)
                    t32 = [SB(f"t32_{i}", [32, 512], F32, P) for i in range(2)]
                    o32 = [SB(f"o32_{i}", [32, 512], BF16, P) for i in range(2)]
                    o64 = [SB(f"o64_{i}", [64, 512], BF16, P) for i in range(2)]
                    vst = [SB(f"vst{i}", [128, 8, 65], BF16, P) for i in range(2)]
                    for i in range(2):
                        s.op('pool', nc.gpsimd.memset, vst[i][:], 1.0, writes=[f'vst{i}'])
                    ev = 0
                    for bi, blk in enumerate(blocks):
                        t0, nt, isc = blk
                        a = bi % 2
                        dma(hxb[a][:, :, 0:nt], hxT_v[:, :, t0:t0 + nt], ['hxT_d'], [f'hxb{a}'])
                        dma(r32[a][:, :, 0:nt], rope32.rearrange("c p t -> p c t")[:, :, t0:t0 + nt], ['rope32'], [f'r32_{a}'])
                        for c in range(5):
                            pb = c % 2
                            for k in range(KC):
                                mm(PS[pb][:, 0:nt], wA[:, k, c * 128:(c + 1) * 128], hxb[a][:, k, 0:nt], k == 0, k == KC - 1,
                                   ['wA', f'hxb{a}'], [f'ps{pb}'])
                            cp('dve', cqf[:, c, 0:nt], PS[pb][:, 0:nt], [f'ps{pb}'], ['cqf'])
                        for grp, (c0, c1, n) in enumerate([(0, 3, 384), (3, 5, 256)]):
                            for c in range(c0, c1):
                                sa = c % 2
                                act(sqa[sa][:, 0:nt], cqf[:, c, 0:nt], AF.Square, ['cqf'], [f'sqa{sa}'])
                                mm(PS[2 + grp][:, 0:nt], ones_f, sqa[sa][:, 0:nt], c == c0, c == c1 - 1,
                                   [f'sqa{sa}', 'cst_f'], [f'ps{2 + grp}'])
                            rstd_from_ps(PS[2 + grp][:, 0:nt], rsa[:, grp, 0:nt], n, [f'ps{2 + grp}', 'epsc'], ['rsa'])
                            for c in range(c0, c1):
                                stt(cqn[:, c, 0:nt], cqf[:, c, 0:nt], gv[:, 16 + c:17 + c], rsa[:, grp, 0:nt], ALU.mult, ALU.mult,
                                    ['cqf', 'gv', 'rsa'], ['cqn'])
                        for k in range(KC):
                            mm(PS[4][0:32, 0:nt], wA[:, k, 640:672], hxb[a][:, k, 0:nt], k == 0, k == KC - 1, ['wA', f'hxb{a}'], ['ps4'])
                        for k in range(KC):
                            mm(PS[5][0:32, 0:nt], wA[:, k, 672:704], hxb[a][:, k, 0:nt], k == 0, k == KC - 1, ['wA', f'hxb{a}'], ['ps5'])
                        e = ev % 2; ev += 1
                        tt('dve', t32[0][:, 0:nt], PS[4][0:32, 0:nt], r32[a][:, 0, 0:nt], ALU.mult, ['ps4', f'r32_{a}'], ['t32_0'])
                        tt('dve', t32[1][:, 0:nt], PS[5][0:32, 0:nt], r32[a][:, 1, 0:nt], ALU.mult, ['ps5', f'r32_{a}'], ['t32_1'])
                        tt('pool', o32[e][:, 0:nt], t32[0][:, 0:nt], t32[1][:, 0:nt], ALU.add, ['t32_0', 't32_1'], [f'o32_{e}'])
                        dma(krT_d[:, t0:t0 + nt], o32[e][:, 0:nt], [f'o32_{e}'], ['krT_d'], q='st', eng='pool')
                        for h in range(8):
                            pb = h % 2
                            for c in range(3):
                                mm(PS[pb][0:64, 0:nt], wuq[:, c, h * 96:h * 96 + 64], cqn[:, c, 0:nt], c == 0, c == 2, ['wuq', 'cqn'], [f'ps{pb}'])
                            e = h % 2
                            cp('act', o64[e][:, 0:nt], PS[pb][0:64, 0:nt], [f'ps{pb}'], [f'o64_{e}'])
                            dma(qnT_d[h, :, t0:t0 + nt], o64[e][:, 0:nt], [f'o64_{e}'], ['qnT_d'], q='st', eng='pool')
                            for c in range(3):
                                mm(PS[4][0:32, 0:nt], wuq[:, c, h * 96 + 64:h * 96 + 96], cqn[:, c, 0:nt], c == 0, c == 2, ['wuq', 'cqn'], ['ps4'])
                            for c in range(3):
                                mm(PS[5][0:32, 0:nt], wuq[:, c, 768 + h * 32:768 + h * 32 + 32], cqn[:, c, 0:nt], c == 0, c == 2, ['wuq', 'cqn'], ['ps5'])
                            e = ev % 2; ev += 1
                            tt('dve', t32[0][:, 0:nt], PS[4][0:32, 0:nt], r32[a][:, 0, 0:nt], ALU.mult, ['ps4', f'r32_{a}'], ['t32_0'])
                            tt('dve', t32[1][:, 0:nt], PS[5][0:32, 0:nt], r32[a][:, 1, 0:nt], ALU.mult, ['ps5', f'r32_{a}'], ['t32_1'])
                            tt('pool', o32[e][:, 0:nt], t32[0][:, 0:nt], t32[1][:, 0:nt], ALU.add, ['t32_0', 't32_1'], [f'o32_{e}'])
                            dma(qrT_d[h, :, t0:t0 + nt], o32[e][:, 0:nt], [f'o32_{e}'], ['qrT_d'], q='st', eng='pool')
                            pb = 2 + h % 2
                            for c in range(2):
                                mm(PS[pb][0:64, 0:nt], wukv[:, c, h * 128:h * 128 + 64], cqn[:, 3 + c, 0:nt], c == 0, c == 1, ['wukv', 'cqn'], [f'ps{pb}'])
                            e2 = (h + 1) % 2
                            cp('act', o64[e2][:, 0:nt], PS[pb][0:64, 0:nt], [f'ps{pb}'], [f'o64_{e2}'])
                            dma(knT_d[h, :, t0:t0 + nt], o64[e2][:, 0:nt], [f'o64_{e2}'], ['knT_d'], q='st', eng='pool')
                        for j in range(nt // 128):
                            va = j % 2
                            for c in range(2):
                                mm(PS[6][:, :].rearrange("p (h d) -> p h d", h=8), cqn[:, 3 + c, j * 128:(j + 1) * 128],
                                   wukv[:, c, :].rearrange("p (h d) -> p h d", h=8)[:, :, 64:128], c == 0, c == 1, ['wukv', 'cqn'], ['ps6'])
                            cp('dve', vst[va][:, :, 0:64], PS[6][:, :].rearrange("p (h d) -> p h d", h=8), ['ps6'], [f'vst{va}'])
                            dma(vA_d[t0 + j * 128:t0 + (j + 1) * 128, :, :], vst[va][:], [f'vst{va}'], ['vA_d'], q='st', eng='pool')
                s.barrier()

                for mix in range(2):
                    with _hp() as P:
                        col0 = 672 if mix == 0 else 1440
                        sw0 = 32 if mix == 0 else 672
                        qT_dst, kT_dst, v_dst = (qbT_d, kbT_d, vB_d) if mix == 0 else (qwT_d, kwT_d, vW_d)
                        wq = SB("wq", [128, KC, 1408], BF16, P)
                        wiv = w_in[l].rearrange("(k p) n -> p k n", p=128)
                        load_w_bf16(wq[:, :, 0:768], wiv[:, :, col0:col0 + 768], 'wq', 'w_in')
                        load_w_bf16(wq[:, :, 768:1408], w_in_sw[l].rearrange("(k p) n -> p k n", p=128)[:, :, sw0:sw0 + 640], 'wq', 'w_in_sw')
                        hxb = [SB(f"hxb{i}", [128, KC, 512], BF16, P) for i in range(2)]
                        r64 = [SB(f"r64_{i}", [64, 2, 512], F32, P) for i in range(2)]
                        qf = [SB(f"qf{i}", [64, 512], F32, P) for i in range(2)]
                        sq6 = [SB(f"sq6_{i}", [64, 512], F32, P) for i in range(2)]
                        rs6 = [SB(f"rs6_{i}", [64, 512], F32, P) for i in range(2)]
                        t64 = [SB(f"t64_{i}", [64, 512], F32, P) for i in range(2)]
                        o64 = [SB(f"o64_{i}", [64, 512], BF16, P) for i in range(2)]
                        vst = [SB(f"vst{i}", [128, 2, 65], BF16, P) for i in range(2)]
                        for i in range(2):
                            s.op('pool', nc.gpsimd.memset, vst[i][:], 1.0, writes=[f'vst{i}'])
                        ev = 0
                        for bi, blk in enumerate(blocks):
                            t0, nt, isc = blk
                            a = bi % 2
                            dma(hxb[a][:, :, 0:nt], hxT_v[:, :, t0:t0 + nt], ['hxT_d'], [f'hxb{a}'])
                            dma(r64[a][:, :, 0:nt], rope64.rearrange("c p t -> p c t")[:, :, t0:t0 + nt], ['rope64'], [f'r64_{a}'])
                            for hh in range(10):
                                isk = hh >= 8
                                c_main = (hh * 64) if not isk else (512 + (hh - 8) * 64)
                                c_sw = (768 + hh * 64) if not isk else (768 + 512 + (hh - 8) * 64)
                                e = ev % 2; ev += 1
                                pa, pb = 2 * e, 2 * e + 1
                                for k in range(KC):
                                    mm(PS[pa][0:64, 0:nt], wq[:, k, c_main:c_main + 64], hxb[a][:, k, 0:nt], k == 0, k == KC - 1, ['wq', f'hxb{a}'], [f'ps{pa}'])
                                for k in range(KC):
                                    mm(PS[pb][0:64, 0:nt], wq[:, k, c_sw:c_sw + 64], hxb[a][:, k, 0:nt], k == 0, k == KC - 1, ['wq', f'hxb{a}'], [f'ps{pb}'])
                                if mix == 0:
                                    gcol = 2 if isk else 0
                                    act(sq6[e][:, 0:nt], PS[pa][0:64, 0:nt], AF.Square, [f'ps{pa}'], [f'sq6_{e}'])
                                    mm(PS[4 + e][0:64, 0:nt], ones_f[0:64, 0:64], sq6[e][:, 0:nt], True, True, [f'sq6_{e}', 'cst_f'], [f'ps{4 + e}'])
                                    rstd_from_ps(PS[4 + e][0:64, 0:nt], rs6[e][:, 0:nt], 64, [f'ps{4 + e}', 'epsc'], [f'rs6_{e}'], np_=64)
                                    stt(qf[e][:, 0:nt], PS[pa][0:64, 0:nt], gq[:, gcol:gcol + 1], r64[a][:, 0, 0:nt], ALU.mult, ALU.mult,
                                        [f'ps{pa}', 'gq', f'r64_{a}'], [f'qf{e}'])
                                    stt(t64[e][:, 0:nt], PS[pb][0:64, 0:nt], gq[:, gcol + 1:gcol + 2], r64[a][:, 1, 0:nt], ALU.mult, ALU.mult,
                                        [f'ps{pb}', 'gq', f'r64_{a}'], [f't64_{e}'])
                                    tt('pool', t64[e][:, 0:nt], t64[e][:, 0:nt], qf[e][:, 0:nt], ALU.add, [f't64_{e}', f'qf{e}'], [f't64_{e}'])
                                    tt('pool', o64[e][:, 0:nt], t64[e][:, 0:nt], rs6[e][:, 0:nt], ALU.mult, [f't64_{e}', f'rs6_{e}'], [f'o64_{e}'])
                                else:
                                    tt('dve', qf[e][:, 0:nt], PS[pa][0:64, 0:nt], r64[a][:, 0, 0:nt], ALU.mult, [f'ps{pa}', f'r64_{a}'], [f'qf{e}'])
                                    tt('dve', t64[e][:, 0:nt], PS[pb][0:64, 0:nt], r64[a][:, 1, 0:nt], ALU.mult, [f'ps{pb}', f'r64_{a}'], [f't64_{e}'])
                                    tt('pool', o64[e][:, 0:nt], t64[e][:, 0:nt], qf[e][:, 0:nt], ALU.add, [f't64_{e}', f'qf{e}'], [f'o64_{e}'])
                                dst = qT_dst[hh] if not isk else kT_dst[hh - 8]
                                dma(dst[:, t0:t0 + nt], o64[e][:, 0:nt], [f'o64_{e}'], ['qk_d'], q='st', eng='pool')
                            for j in range(nt // 128):
                                va = j % 2
                                for k in range(KC):
                                    mm(PS[6][:, 0:128], hxb[a][:, k, j * 128:(j + 1) * 128], wq[:, k, 640:768], k == 0, k == KC - 1, ['wq', f'hxb{a}'], ['ps6'])
                                cp('act', vst[va][:, :, 0:64], PS[6][:, 0:128].rearrange("p (h d) -> p h d", h=2), ['ps6'], [f'vst{va}'])
                                dma(v_dst[t0 + j * 128:t0 + (j + 1) * 128, :, :], vst[va][:], [f'vst{va}'], ['v_d'], q='st', eng='pool')
                    s.barrier()

                with _hp() as P:
                    wg = SB("wg", [128, KC, 3072], BF16, P)
                    wiv = w_in[l].rearrange("(k p) n -> p k n", p=128)
                    for q4 in range(4):
                        load_w_bf16(wg[:, :, q4 * 768:(q4 + 1) * 768], wiv[:, :, 2208 + q4 * 768:2208 + (q4 + 1) * 768], 'wg', 'w_in')
                    hxb = [SB(f"hxb{i}", [128, KC, 512], BF16, P) for i in range(2)]
                    og = [SB(f"og{i}", [128, 4, 512], BF16, P) for i in range(2)]
                    gT_v = gT_d.rearrange("(c p) t -> p c t", p=128)
                    ev = 0
                    for bi, blk in enumerate(blocks):
                        t0, nt, isc = blk
                        a = bi % 2
                        dma(hxb[a][:, :, 0:nt], hxT_v[:, :, t0:t0 + nt], ['hxT_d'], [f'hxb{a}'])
                        for c4 in range(6):
                            e = ev % 2; ev += 1
                            for cc in range(4):
                                c = c4 * 4 + cc
                                pb = c % 4
                                for k in range(KC):
                                    mm(PS[pb][:, 0:nt], wg[:, k, c * 128:(c + 1) * 128], hxb[a][:, k, 0:nt], k == 0, k == KC - 1, ['wg', f'hxb{a}'], [f'ps{pb}'])
                                act(og[e][:, cc, 0:nt], PS[pb][:, 0:nt], AF.Sigmoid, [f'ps{pb}'], [f'og{e}'])
                            dma(gT_v[:, c4 * 4:(c4 + 1) * 4, t0:t0 + nt], og[e][:, :, 0:nt], [f'og{e}'], ['gT_d'], q='st', eng='pool')
                s.barrier()

                with _hp() as P:
                    kn_sb = [SB(f"kn_sb{i}", [64, T], BF16, P) for i in range(2)]
                    kr_sb = SB("kr_sb", [32, T], BF16, P)
                    v_sb = [SB(f"v_sb{i}", [128, NT, 65], BF16, P) for i in range(2)]
                    qn_sb = [SB(f"qn_sb{i}", [64, 512], BF16, P) for i in range(2)]
                    qr_sb = [SB(f"qr_sb{i}", [32, 512], BF16, P) for i in range(2)]
                    p_sb = [SB(f"p_sb{i}", [128, 512], BF16, P) for i in range(4)]
                    rz = [SB(f"rz{i}", [128, 4], F32, P) for i in range(2)]
                    o_sb = [SB(f"o_sb{i}", [128, 4, 64], BF16, P) for i in range(2)]
                    oT_sb = [SB(f"oT_sb{i}", [64, 512], BF16, P) for i in range(2)]
                    dma(kr_sb[:], krT_d, ['krT_d'], ['kr_sb'])
                    cnt = {'kv': 0, 'q': 0, 'p': 0, 'o': 0, 's': 0}
                    OB = [PS[2], PS[3], PS[4], PS[5]]

                    def attn_unit(qparts, kparts, vt, nq, ktiles, scale, sink_col, out_dst):
                        nj = nq // 128
                        nk = len(ktiles)
                        pis = {}
                        LA = 2
                        for ki in range(nk + LA):
                            if ki < nk:
                                kt, mask = ktiles[ki]
                                sb_i = (0, 1, 6)[cnt['s'] % 3]; cnt['s'] += 1
                                sps = PS[sb_i]
                                nparts = len(qparts) + (1 if mask is not None else 0)
                                for pi, ((qap, qk), (kap, kk)) in enumerate(zip(qparts, kparts)):
                                    mm(sps[:, 0:nq], kap[:, kt * 128:(kt + 1) * 128], qap[:, 0:nq], pi == 0, pi == nparts - 1,
                                       [qk, kk], [f'ps{sb_i}'])
                                if mask is not None:
                                    mm(sps[:, 0:nq], ident_b, mask, False, True, ['cst_b'], [f'ps{sb_i}'])
                                pi_ = cnt['p'] % 4; cnt['p'] += 1
                                pis[ki] = pi_
                                act(p_sb[pi_][:, 0:nq], sps[:, 0:nq], AF.Exp, [f'ps{sb_i}'], [f'p_sb{pi_}'], scale=scale)
                            if ki >= LA:
                                kp = ki - LA
                                ktp = ktiles[kp][0]
                                pp_ = pis[kp]
                                for j in range(nj):
                                    mm(OB[j][:, 0:65], p_sb[pp_][:, j * 128:(j + 1) * 128], vt[0][:, ktp, :], kp == 0, kp == nk - 1,
                                       [f'p_sb{pp_}', vt[1]], [f'ps{2 + j}'])
                        oi = cnt['o'] % 2; cnt['o'] += 1
                        for j in range(nj):
                            if sink_col is not None:
                                ts('dve', rz[oi][:, j:j + 1], OB[j][:, 64:65], esink[:, sink_col:sink_col + 1], ALU.add,
                                   [f'ps{2 + j}', 'esink'], [f'rz{oi}'])
                                s.op('dve', nc.vector.reciprocal, out=rz[oi][:, j:j + 1], in_=rz[oi][:, j:j + 1], reads=[f'rz{oi}'], writes=[f'rz{oi}'])
                            else:
                                s.op('dve', nc.vector.reciprocal, out=rz[oi][:, j:j + 1], in_=OB[j][:, 64:65], reads=[f'ps{2 + j}'], writes=[f'rz{oi}'])
                            ts('dve', o_sb[oi][:, j, :], OB[j][:, 0:64], rz[oi][:, j:j + 1], ALU.mult, [f'ps{2 + j}', f'rz{oi}'], [f'o_sb{oi}'])
                            s.op('pe', nc.tensor.transpose, PSB[0:64, j * 128:(j + 1) * 128], o_sb[oi][:, j, :], ident_b,
                                 reads=[f'o_sb{oi}', 'cst_b'], writes=['psb'])
                        cp('act', oT_sb[oi][:, 0:nq], PSB[0:64, 0:nq], ['psb'], [f'oT_sb{oi}'])
                        dma(out_dst, oT_sb[oi][:, 0:nq], [f'oT_sb{oi}'], ['oT_d'], q='st', eng='pool')

                    mixers = [(0, qnT_d, knT_d, vA_d, 96 ** -0.5), (1, qbT_d, kbT_d, vB_d, 0.125), (2, qwT_d, kwT_d, vW_d, 0.125)]
                    for (m, qd, kd, vd, scale) in mixers:
                        for h in range(8):
                            kvh = h if m == 0 else h // 4
                            if m == 0 or h % 4 == 0:
                                kvi = cnt['kv'] % 2; cnt['kv'] += 1
                                dma(kn_sb[kvi][:], kd[kvh], ['qk_d', 'knT_d'], [f'kn_sb{kvi}'])
                                dma(v_sb[kvi][:], vd[:, kvh, :].rearrange("(n p) c -> p n c", p=128), ['v_d', 'vA_d'], [f'v_sb{kvi}'])
                            kparts = [(kn_sb[kvi], f'kn_sb{kvi}')]
                            if m == 0:
                                kparts.append((kr_sb, 'kr_sb'))
                            vt = (v_sb[kvi], f'v_sb{kvi}')
                            for bi, blk in enumerate(blocks):
                                t0, nt, isc = blk
                                if last and (isc or t0 >= TL // 2):
                                    continue
                                qi = cnt['q'] % 2; cnt['q'] += 1
                                dma(qn_sb[qi][:, 0:nt], qd[h, :, t0:t0 + nt], ['qk_d', 'qnT_d'], [f'qn_sb{qi}'])
                                qparts = [(qn_sb[qi], f'qn_sb{qi}')]
                                if m == 0:
                                    dma(qr_sb[qi][:, 0:nt], qrT_d[h, :, t0:t0 + nt], ['qrT_d'], [f'qr_sb{qi}'])
                                    qparts.append((qr_sb[qi], f'qr_sb{qi}'))
                                ctx_tiles = [(NTL, None), (NTL + 1, None)]
                                dst = oT_d[m][h * 64:(h + 1) * 64, t0:t0 + nt]
                                sink_col = (l * 8 + h) if m == 2 else None
                                if last and (isc or t0 >= TL // 2):
                                    continue
                                if isc:
                                    attn_unit(qparts, kparts, vt, nt, ctx_tiles, scale, sink_col, dst)
                                elif m < 2:
                                    attn_unit(qparts, kparts, vt, nt, [(kt, None) for kt in range(NT)], scale, sink_col, dst)
                                else:
                                    for j in range(nt // 128):
                                        n = t0 // 128 + j
                                        kts = []
                                        lm = seam_wrapL if n == 0 else (seam_midL if n == NTL // 2 else nm_lo)
                                        rm = seam_wrapR if n == NTL - 1 else (seam_midR if n == NTL // 2 - 1 else nm_hi)
                                        kts.append(((n - 1) % NTL, lm))
                                        kts.append((n, None))
                                        kts.append(((n + 1) % NTL, rm))
                                        kts += ctx_tiles
                                        qp = [(qparts[0][0][:, j * 128:(j + 1) * 128], qparts[0][1])]
                                        attn_unit(qp, kparts, vt, 128, kts, scale, sink_col,
                                                  oT_d[m][h * 64:(h + 1) * 64, t0 + j * 128:t0 + (j + 1) * 128])
                s.barrier()

                with _hp() as P:
                    wo3 = [SB(f"wo3_{m}", [128, 4, D], BF16, P) for m in range(3)]
                    wo = SB("wo", [128, KC, D], BF16, P)
                    for m, wsrc in enumerate([w_oa, w_ob, w_ow]):
                        load_w_bf16(wo3[m][:], wsrc[l].rearrange("(k p) n -> p k n", p=128), f'wo3_{m}', 'w_o3')
                    load_w_bf16(wo[:], w_o[l].rearrange("(k p) n -> p k n", p=128), 'wo', 'w_o')
                    oTb = [[SB(f"oTb{m}_{i}", [128, 4, 512], BF16, P) for i in range(2)] for m in range(3)]
                    gtb = [SB(f"gtb{i}", [128, 24, 512], BF16, P) for i in range(2)]
                    xs2 = [SB(f"xs2_{i}", [128, KC, 512], F32, P) for i in range(2)]
                    mT = SB("mT", [128, KC, 512], BF16, P)
                    acc = [SB(f"acc{i}", [128, 512], F32, P) for i in range(2)]
                    tmp = [SB(f"tmp{i}", [128, 512], F32, P) for i in range(2)]
                    gT_v = gT_d.rearrange("(c p) t -> p c t", p=128)
                    for bi, blk in enumerate(blocks):
                        t0, nt, isc = blk
                        if (isc and (last or not ctx_update)) or (last and t0 >= TL // 2):
                            continue
                        a = bi % 2
                        for m in range(3):
                            dma(oTb[m][a][:, :, 0:nt], oT_d[m].rearrange("(c p) t -> p c t", p=128)[:, :, t0:t0 + nt], ['oT_d'], [f'oTb{m}_{a}'])
                        dma(gtb[a][:, :, 0:nt], gT_v[:, :, t0:t0 + nt], ['gT_d'], [f'gtb{a}'])
                        dma(xs2[a][:, :, 0:nt], xT_v[:, :, t0:t0 + nt], ['xT_d'], [f'xs2_{a}'])
                        for dc in range(KC):
                            e = dc % 2
                            for m in range(3):
                                pb = m
                                for c in range(4):
                                    mm(PS[pb][:, 0:nt], wo3[m][:, c, dc * 128:(dc + 1) * 128], oTb[m][a][:, c, 0:nt], c == 0, c == 3,
                                       [f'wo3_{m}', f'oTb{m}_{a}'], [f'ps{pb}'])
                            tt('dve', acc[e][:, 0:nt], PS[0][:, 0:nt], gtb[a][:, dc, 0:nt], ALU.mult, ['ps0', f'gtb{a}'], [f'acc{e}'])
                            tt('dve', tmp[e][:, 0:nt], PS[1][:, 0:nt], gtb[a][:, 8 + dc, 0:nt], ALU.mult, ['ps1', f'gtb{a}'], [f'tmp{e}'])
                            tt('pool', acc[e][:, 0:nt], acc[e][:, 0:nt], tmp[e][:, 0:nt], ALU.add, [f'acc{e}', f'tmp{e}'], [f'acc{e}'])
                            tt('dve', tmp[e][:, 0:nt], PS[2][:, 0:nt], gtb[a][:, 16 + dc, 0:nt], ALU.mult, ['ps2', f'gtb{a}'], [f'tmp{e}'])
                            tt('pool', mT[:, dc, 0:nt], acc[e][:, 0:nt], tmp[e][:, 0:nt], ALU.add, [f'acc{e}', f'tmp{e}'], ['mT'])
                        for dc in range(KC):
                            pb = 4 + dc % 2
                            for k in range(KC):
                                mm(PS[pb][:, 0:nt], wo[:, k, dc * 128:(dc + 1) * 128], mT[:, k, 0:nt], k == 0, k == KC - 1, ['wo', 'mT'], [f'ps{pb}'])
                            stt(xs2[a][:, dc, 0:nt], PS[pb][:, 0:nt], modT[:, 16 + dc, isc:isc + 1], xs2[a][:, dc, 0:nt], ALU.mult, ALU.add,
                                [f'ps{pb}', 'modT', f'xs2_{a}'], [f'xs2_{a}'])
                        dma(xT_v[:, :, t0:t0 + nt], xs2[a][:, :, 0:nt], [f'xs2_{a}'], ['xT_d'], q='st', eng='pool')
                s.barrier()

            if do_peer:
                with _hp() as P:
                    NTl = norm_tiles(P)
                    xs = NTl['xs']
                    wpq = [SB(f"wpq{i}", [128, KC, 512], BF16, P) for i in range(2)]
                    skt = SB("skt", [128, 16, 128], BF16, P)
                    load_w_bf16(skt[:], skT[l], 'skt', 'skT')
                    hx2 = SB("hx2", [128, KC, 512], BF16, P)
                    qT = xs[:].bitcast(BF16).rearrange("p k (a t) -> p (k a) t", a=2)
                    Ssb = SB("Ssb", [128, 16, 128], F32, P)
                    Stmp = SB("Stmp", [128, 128], F32, P)
                    T16 = SB("T16", [128, 16, 16], F32, P)
                    Eall = SB("Eall", [128, 4, 16, 128], F32, P)
                    ET16 = SB("ET16", [128, 16, 16], F32, P)
                    cand = SB("cand", [128, 8, 256], BF16, P)
                    ctmp = SB("ctmp", [128, 256], BF16, P)
                    CT16 = SB("CT16", [128, 8, 16], F32, P)
                    theta = SB("theta", [128, 4, 8], F32, P)
                    zz = SB("zz", [128, 8], F32, P)
                    Dm = SB("Dm", [128, 4, 8, 128], BF16, P)
                    Pp = [SB(f"Pp{i}", [128, 2, GI, 128], BF16, P) for i in range(4)]
                    Gm = [SB(f"Gm{i}", [128, 4, 8, GI * 128], BF16, P) for i in range(2)]
                    UT = [SB(f"UT{i}", [128, KC, GI * 128], BF16, P) for i in range(2)]
                    Vg = [SB(f"Vg{i}", [128, GI, D], BF16, P) for i in range(3)]
                    a_sb = [SB(f"a_sb{i}", [128, GI, 512], F32, P) for i in range(2)]
                    GaT = [SB(f"GaT{i}", [128, GI, 512], BF16, P) for i in range(2)]
                    accp = SB("accp", [128, KC, 512], F32, P)
                    ubf_v = ubf_d.rearrange("(g p) (k e) -> g p k e", p=128, k=KC)
                    vbf_v = vbf_d.rearrange("(g p) (c d) -> g p c d", p=128, c=GI)
                    gcount = 0
                    pcount = [0]
                    mcount = [0]
                    mk = [SB(f"mk{i}", [128, GI * 128], BF16, P) for i in range(2)]
                    slot = {}
                    for bi, blk in enumerate(blocks):
                        t0, nt, isc = blk
                        if (isc and (last or not ctx_update)) or (last and t0 >= TL // 2):
                            continue
                        ntl = nt // 128
                        load_x_and_norm(NTl, blk, 1, hx2, 'hx2')
                        for hp in range(16):
                            pb = hp % 2
                            wi = (hp // 4) % 2
                            if hp % 4 == 0:
                                load_w_bf16(wpq[wi][:], w_pq[l].rearrange("(k p) n -> p k n", p=128)[:, :, hp * 128:(hp + 4) * 128], f'wpq{wi}', 'w_pq')
                            for k in range(KC):
                                mm(PS[pb][:, 0:nt], wpq[wi][:, k, (hp % 4) * 128:(hp % 4 + 1) * 128], hx2[:, k, 0:nt], k == 0, k == KC - 1, [f'wpq{wi}', 'hx2'], [f'ps{pb}'])
                            cp('act' if hp % 2 else 'dve', qT[:, hp, 0:nt], PS[pb][:, 0:nt], [f'ps{pb}', 'xs'], ['xs'])
                        for j in range(ntl):
                            for q4 in range(4):
                                pb = 2 + q4 % 2
                                for i4 in range(4):
                                    hp = q4 * 4 + i4
                                    mm(PS[pb][:, i4 * 128:(i4 + 1) * 128], qT[:, hp, j * 128:(j + 1) * 128], skt[:, hp, :], True, True,
                                       ['xs', 'skt'], [f'ps{pb}'])
                                cp('act', Ssb[:, q4 * 4:(q4 + 1) * 4, :], PS[pb][:, :].rearrange("p (a b) -> p a b", a=4), [f'ps{pb}'], ['Ssb'])
                            for hp in range(16):
                                s.op('dve', nc.vector.max, out=T16[:, hp, 0:8], in_=Ssb[:, hp, :], reads=['Ssb'], writes=['T16'])
                                s.op('dve', nc.vector.match_replace, out=Stmp[:], in_to_replace=T16[:, hp, 0:8], in_values=Ssb[:, hp, :],
                                     imm_value=-1e30, reads=['T16', 'Ssb'], writes=['Stmp'])
                                s.op('dve', nc.vector.max, out=T16[:, hp, 8:16], in_=Stmp[:], reads=['Stmp'], writes=['T16'])
                            tt('dve', Ssb[:], Ssb[:], T16[:, :, 0:1].to_broadcast([128, 16, 128]), ALU.subtract, ['Ssb', 'T16'], ['Ssb'])
                            act(Eall[:, j, :, :], Ssb[:], AF.Exp, ['Ssb'], ['Eall'])
                            tt('dve', ET16[:], T16[:], T16[:, :, 0:1].to_broadcast([128, 16, 16]), ALU.subtract, ['T16'], ['ET16'])
                            act(ET16[:], ET16[:], AF.Exp, ['ET16'], ['ET16'])
                            e4 = ET16[:].rearrange("p (h two) k -> p h two k", two=2)
                            for h in range(8):
                                tt('pool', cand[:, h, :].rearrange("p (a b) -> p a b", a=16),
                                   e4[:, h, 0, :].unsqueeze(2).to_broadcast([128, 16, 16]),
                                   e4[:, h, 1, :].unsqueeze(1).to_broadcast([128, 16, 16]), ALU.mult, ['ET16'], ['cand'])
                                s.op('dve', nc.vector.max, out=CT16[:, h, 0:8], in_=cand[:, h, :], reads=['cand'], writes=['CT16'])
                                s.op('dve', nc.vector.match_replace, out=ctmp[:], in_to_replace=CT16[:, h, 0:8], in_values=cand[:, h, :],
                                     imm_value=-1e30, reads=['CT16', 'cand'], writes=['ctmp'])
                                s.op('dve', nc.vector.max, out=CT16[:, h, 8:16], in_=ctmp[:], reads=['ctmp'], writes=['CT16'])
                            cp('dve', theta[:, j, :], CT16[:, :, 15], ['CT16'], ['theta'])
                            s.op('dve', nc.vector.tensor_reduce, out=zz[:], in_=CT16[:], axis=AX.X, op=ALU.add, reads=['CT16'], writes=['zz'])
                            s.op('dve', nc.vector.reciprocal, out=zz[:], in_=zz[:], reads=['zz'], writes=['zz'])
                            for h in range(8):
                                ts('dve', Dm[:, j, h, :], ident_f, zz[:, h:h + 1], ALU.mult, ['cst_f', 'zz'], ['Dm'])
                        for it in range(NG + 2):
                            g = it
                            if g < NG:
                                ui = gcount % 2
                                vi = gcount % 3
                                gi = gcount % 2
                                gcount += 1
                                slot[g] = (ui, vi, gi)
                                dma(UT[ui][:], ubf_v[g], ['bg:ubf'], [f'UT{ui}'])
                                dma(Vg[vi][:], vbf_v[g], ['bg:vbf'], [f'Vg{vi}'])
                                for j in range(ntl):
                                    for h2 in range(4):
                                        pi_ = pcount[0] % 4; pcount[0] += 1
                                        tt('pool', Pp[pi_][:],
                                           Eall[:, j, 4 * h2:4 * h2 + 3:2, g * GI:(g + 1) * GI].unsqueeze(3).to_broadcast([128, 2, GI, 128]),
                                           Eall[:, j, 4 * h2 + 1:4 * h2 + 4:2, :].unsqueeze(2).to_broadcast([128, 2, GI, 128]),
                                           ALU.mult, ['Eall'], [f'Pp{pi_}'])
                                        on_pool = False
                                        for hh in range(2):
                                            h = 2 * h2 + hh
                                            pv = Pp[pi_][:, hh, :, :].rearrange("p a b -> p (a b)")
                                            if on_pool:
                                                mi = mcount[0] % 2; mcount[0] += 1
                                                tt('pool', mk[mi][:], pv, theta[:, j, h:h + 1].to_broadcast([128, GI * 128]), ALU.is_ge,
                                                   [f'Pp{pi_}', 'theta'], [f'mk{mi}'])
                                                tt('pool', Gm[gi][:, j, h, :], mk[mi][:], pv, ALU.mult, [f'mk{mi}', f'Pp{pi_}'], [f'Gm{gi}_{j}_{h}'])
                                            else:
                                                stt(Gm[gi][:, j, h, :], pv, theta[:, j, h:h + 1], pv, ALU.is_ge, ALU.mult,
                                                    [f'Pp{pi_}', 'theta'], [f'Gm{gi}_{j}_{h}'])
                                for c in range(GI):
                                    ab = c % 2
                                    for k in range(KC):
                                        mm(PS[ab][:, 0:nt], UT[ui][:, k, c * 128:(c + 1) * 128], hx2[:, k, 0:nt], k == 0, k == KC - 1,
                                           [f'UT{ui}', 'hx2'], [f'ps{ab}'])
                                    act(a_sb[gi][:, c, 0:nt], PS[ab][:, 0:nt], AF.Gelu, [f'ps{ab}'], [f'a_sb{gi}_{c}'])
                            g1 = it - 1
                            if 0 <= g1 < NG:
                                ui, vi, gi = slot[g1]
                                for c in range(GI):
                                    gb = 2 + c % 2
                                    for j in range(ntl):
                                        for h in range(8):
                                            mm(PS[gb][:, j * 128:(j + 1) * 128], Gm[gi][:, j, h, c * 128:(c + 1) * 128], Dm[:, j, h, :], h == 0, h == 7,
                                               [f'Gm{gi}_{j}_{h}', 'Dm'], [f'ps{gb}'])
                                    tt('dve', GaT[gi][:, c, 0:nt], PS[gb][:, 0:nt], a_sb[gi][:, c, 0:nt], ALU.mult,
                                       [f'ps{gb}', f'a_sb{gi}_{c}'], [f'GaT{gi}'])
                            g2 = it - 2
                            if 0 <= g2 < NG:
                                ui, vi, gi = slot[g2]
                                for dc in range(KC):
                                    ob = 4 + dc % 2
                                    for c in range(GI):
                                        mm(PS[ob][:, 0:nt], Vg[vi][:, c, dc * 128:(dc + 1) * 128], GaT[gi][:, c, 0:nt], c == 0, c == GI - 1,
                                           [f'Vg{vi}', f'GaT{gi}'], [f'ps{ob}'])
                                    if g2 == 0:
                                        cp('act', accp[:, dc, 0:nt], PS[ob][:, 0:nt], [f'ps{ob}'], [f'accp{dc}'])
                                    else:
                                        tt('dve', accp[:, dc, 0:nt], PS[ob][:, 0:nt], accp[:, dc, 0:nt], ALU.add, [f'ps{ob}', f'accp{dc}'], [f'accp{dc}'])
                        dma(xs[:, :, 0:nt], xT_v[:, :, t0:t0 + nt], ['xT_d'], ['xs'])
                        for dc in range(KC):
                            stt(xs[:, dc, 0:nt], accp[:, dc, 0:nt], modT[:, 40 + dc, isc:isc + 1], xs[:, dc, 0:nt], ALU.mult, ALU.add,
                                [f'accp{dc}', 'modT', 'xs'], ['xs'])
                        dma(xT_v[:, :, t0:t0 + nt], xs[:, :, 0:nt], ['xs'], ['xT_d'], q='st', eng='pool')
                s.barrier()

        with _hp() as P:
            xs = SB("xsf", [128, KC, 512], F32, P)
            sq = [SB(f"sqf{i}", [128, 512], F32, P) for i in range(2)]
            rstd = SB("rstdf", [128, 512], F32, P)
            yT = SB("yT", [128, KC, 512], F32, P)
            yo = [SB(f"yo{i}", [128, D], F32, P) for i in range(2)]
            oc = 0
            for bi, blk in enumerate(blocks):
                t0, nt, isc = blk
                if isc or t0 >= TL // 2:
                    continue
                dma(xs[:, :, 0:nt], xT_v[:, :, t0:t0 + nt], ['xT_d'], ['xsf'])
                for k in range(KC):
                    a = k % 2
                    act(sq[a][:, 0:nt], xs[:, k, 0:nt], AF.Square, ['xsf'], [f'sqf{a}'])
                    mm(PS[6][:, 0:nt], ones_f, sq[a][:, 0:nt], k == 0, k == KC - 1, [f'sqf{a}', 'cst_f'], ['ps6'])
                rstd_from_ps(PS[6][:, 0:nt], rstd[:, 0:nt], D, ['ps6', 'epsc'], ['rstdf'])
                for k in range(KC):
                    stt(yT[:, k, 0:nt], xs[:, k, 0:nt], gfin[:, k:k + 1], rstd[:, 0:nt], ALU.mult, ALU.mult,
                        ['xsf', 'gfin', 'rstdf'], ['yT'])
                for j in range(nt // 128):
                    a = oc % 2; oc += 1
                    for hk in range(2):
                        pb = 2 * a + hk
                        for kk in range(4):
                            k = hk * 4 + kk
                            s.op('pe', nc.tensor.transpose, PS[pb][:, kk * 128:(kk + 1) * 128],
                                 yT[:, k, j * 128:(j + 1) * 128], ident_f, reads=['yT', 'cst_f'], writes=[f'ps{pb}'])
                        cp('dve' if hk == 0 else 'act', yo[a][:, hk * 512:(hk + 1) * 512], PS[pb][:, :], [f'ps{pb}'], [f'yo{a}'])
                    dma(yout[t0 + j * 128:t0 + (j + 1) * 128, :], yo[a][:], [f'yo{a}'], ['yout'], q='st', eng='pool')
        s.barrier(['sp', 'pool'])
    return nc, s


def _rope_tables(TL, rot_dim):
    rows = TL // GRID_W
    r = np.repeat(np.arange(rows, dtype=np.float32), GRID_W)
    col = np.tile(np.arange(GRID_W, dtype=np.float32), rows)
    n_freq = rot_dim // 4
    inv = (np.float32(10000.0) ** (-np.arange(n_freq, dtype=np.float32) / np.float32(n_freq))).astype(np.float32)
    ang = np.concatenate([r[:, None] * inv, col[:, None] * inv], axis=-1).astype(np.float32)
    cos = np.cos(ang).astype(np.float32)
    sin = np.sin(ang).astype(np.float32)
    half = rot_dim // 2
    T = TL + CTX
    tab = np.zeros((2, rot_dim, T), np.float32)
    tab[0, :, TL:] = 1.0
    tab[0, :half, :TL] = cos.T
    tab[0, half:, :TL] = cos.T
    tab[1, :half, :TL] = -sin.T
    tab[1, half:, :TL] = sin.T
    return tab


def _swap_heads(w, hd):
    n = w.shape[-1] // hd
    w4 = w.reshape(w.shape[:-1] + (n, 2, hd // 2))
    return np.ascontiguousarray(w4[..., ::-1, :]).reshape(w.shape)


def _consts(rank):
    c = np.zeros((128, 8, 128), np.float32)
    c[:, 0, :] = np.eye(128, dtype=np.float32)
    c[:, 1, :] = 1.0
    jj = np.arange(128)[:, None]
    ii = np.arange(128)[None, :]
    c[:, 2, :] = np.where(jj < ii, NEG, 0.0)
    c[:, 3, :] = np.where(jj > ii, NEG, 0.0)
    c[:, 4, :] = c[:, 2, :] if rank == 1 else NEG
    c[:, 5, :] = c[:, 3, :] if rank == 1 else NEG
    c[:, 6, :] = c[:, 2, :] if rank == 0 else NEG
    c[:, 7, :] = c[:, 3, :] if rank == 0 else NEG
    return c


def prep_shared(inp, TL, L):
    f = lambda a: np.ascontiguousarray(np.asarray(a, dtype=np.float32))
    w_in = f(inp['w_in'])[:L]
    sh = {}
    sh['w_mod'] = f(inp['w_mod'])[:L]
    sh['b_modT'] = f(f(inp['b_mod'])[:L].reshape(L, 48, 128).transpose(0, 2, 1))
    gvec = np.zeros((L, 128, 21), np.float32)
    gvec[:, :, 0:8] = f(inp['g_attn'])[:L].reshape(L, 8, 128).transpose(0, 2, 1)
    gvec[:, :, 8:16] = f(inp['g_ffn'])[:L].reshape(L, 8, 128).transpose(0, 2, 1)
    gvec[:, :, 16:19] = f(inp['g_cq'])[:L].reshape(L, 3, 128).transpose(0, 2, 1)
    gvec[:, :, 19:21] = f(inp['g_ckv'])[:L].reshape(L, 2, 128).transpose(0, 2, 1)
    sh['gvec'] = gvec
    gqk = np.zeros((L, 64, 4), np.float32)
    gqn = f(inp['g_qn'])[:L]; gkn = f(inp['g_kn'])[:L]
    gqk[:, :, 0] = gqn; gqk[:, :, 1] = _swap_heads(gqn, 64)
    gqk[:, :, 2] = gkn; gqk[:, :, 3] = _swap_heads(gkn, 64)
    sh['gqk'] = gqk
    sh['g_finalT'] = f(f(inp['g_final']).reshape(8, 128).T)
    sh['sinkb'] = f(np.broadcast_to(f(inp['sink'])[:L].reshape(1, L * 8), (128, L * 8)))
    sh['w_in'] = w_in
    kr_sw = _swap_heads(w_in[:, :, 640:672], 32)
    b_sw = _swap_heads(w_in[:, :, 672:672 + 640], 64)
    w_sw = _swap_heads(w_in[:, :, 1440:1440 + 640], 64)
    sh['w_in_sw'] = f(np.concatenate([kr_sw, b_sw, w_sw], axis=-1))
    w_uq = f(inp['w_uq'])[:L]
    sh['w_uq'] = w_uq
    rope_cols = w_uq.reshape(L, 384, 8, 96)[:, :, :, 64:96]
    sh['w_uq_sw'] = f(_swap_heads(f(rope_cols).reshape(L, 384, 256), 32))
    sh['w_ukv'] = f(inp['w_ukv'])[:L]
    for k in ['w_oa', 'w_ob', 'w_ow', 'w_o', 'w_pq']:
        sh[k] = f(inp[k])[:L]
    GI = 2
    NG = 128 // GI
    pu = f(inp['peer_u'])[:L].reshape(L, NG, GI * 128, 8, 128)
    sh['peer_uG'] = f(pu.transpose(0, 1, 4, 3, 2)).reshape(L, NG * 128, 8 * GI * 128)
    pv = f(inp['peer_v'])[:L].reshape(L, NG, GI, 128, 1024)
    sh['peer_vG'] = f(pv.transpose(0, 1, 3, 2, 4)).reshape(L, NG * 128, GI * 1024)
    sk = f(inp['sub_keys'])[:L].reshape(L, 16, 128, 128)
    sh['skT'] = f(sk.transpose(0, 3, 1, 2))
    sh['rope64'] = _rope_tables(TL, 64)
    sh['rope32'] = _rope_tables(TL, 32)
    return sh


def _roll_lat(a, rank, TL, axis):
    if rank == 0:
        return np.ascontiguousarray(a)
    idx = np.concatenate([np.arange(TL // 2, TL), np.arange(0, TL // 2), np.arange(TL, a.shape[axis])])
    return np.ascontiguousarray(np.take(a, idx, axis=axis))


def prep_core(inp, sh, b, rank=0):
    d = dict(sh)
    TL = inp['x'].shape[1]
    xin = np.concatenate([np.asarray(inp['x'][b], np.float32), np.asarray(inp['ctx'][b], np.float32)], axis=0)
    d['xin'] = _roll_lat(xin, rank, TL, 0)
    d['rope64'] = _roll_lat(sh['rope64'], rank, TL, 2)
    d['rope32'] = _roll_lat(sh['rope32'], rank, TL, 2)
    d['consts'] = _consts(rank)
    cT2 = np.zeros((128, 8, 2), np.float32)
    cT2[:, :, 0] = np.asarray(inp['c'][b], np.float32).reshape(8, 128).T
    cT2[:, :, 1] = np.asarray(inp['c_ctx'], np.float32).reshape(8, 128).T
    d['cT2'] = cT2
    return d


_CACHE = {}


def kernel(**inputs):
    B, TL, _ = inputs['x'].shape
    L = inputs['w_mod'].shape[0]
    key = (TL, L)
    if key not in _CACHE:
        _CACHE[key] = build(TL, L)
    nc, _ = _CACHE[key]
    sh = prep_shared(inputs, TL, L)
    in_maps = [prep_core(inputs, sh, c // 2, c % 2) for c in range(8)]
    res = run_bass_kernel_spmd(nc, in_maps, core_ids=list(range(8)))
    out = np.concatenate([np.asarray(res.results[c]['yout'], dtype=np.float32) for c in range(2 * B)], axis=0)
    return out.reshape(B, TL, D)
```

```python
import contextlib
import os
import numpy as np
import ml_dtypes
import concourse.bass as bass
import concourse.mybir as mybir
from concourse.bass_utils import run_bass_kernel_spmd

F32 = mybir.dt.float32
BF16 = mybir.dt.bfloat16
AF = mybir.ActivationFunctionType
ALU = mybir.AluOpType
AX = mybir.AxisListType

D = 1024
KC = 8
CTX = 256
GRID_W = 64
EPS = 1e-6
NEG = -30000.0


class Sched:
    LIMIT = int(os.environ.get("SEM_LIMIT", "30000"))

    def __init__(self, nc):
        self.nc = nc
        self.engs = {'pe': nc.tensor, 'act': nc.scalar, 'dve': nc.vector, 'pool': nc.gpsimd, 'sp': nc.sync}
        self.sem = {}
        self.cnt = {}
        self.ver = {}
        self.allsems = []
        self.known = {e: {} for e in self.engs}
        self.last_w = {}
        self.readers = {}
        self.n_inst = 0
        self.n_wait = 0
        self.neng = {e: 0 for e in self.engs}
        self.dcount = {}
        self.DMA_RING = 8
        self.marks = []

    def _bump(self, name, step):
        if name not in self.sem or self.cnt[name] + step > self.LIMIT:
            self.ver[name] = self.ver.get(name, -1) + 1
            h = self.nc.alloc_semaphore(name=f"s_{name}_{self.ver[name]}")
            self.sem[name] = h
            self.cnt[name] = 0
            self.allsems.append([h, 0, name])
            self._cur = None
        self.cnt[name] += step
        for rec in self.allsems:
            if rec[0] is self.sem[name]:
                rec[1] = self.cnt[name]
        return self.sem[name], self.cnt[name]

    def op(self, eng, fn, *args, reads=(), writes=(), dma=None, **kw):
        e = self.engs[eng]
        deps = {}

        def add(d):
            if d is None:
                return
            k = id(d[0])
            if k not in deps or deps[k][1] < d[1]:
                deps[k] = d
        for r in reads:
            add(self.last_w.get(r))
        for w in writes:
            add(self.last_w.get(w))
            for d in self.readers.get(w, {}).values():
                add(d)
        kn = self.known[eng]
        for k, (h, v, own) in deps.items():
            if own == 'pe' and eng == 'pe' and dma is None:
                continue
            if kn.get(k, 0) >= v:
                continue
            e.wait_ge(h, v)
            self.n_wait += 1
            kn[k] = v
        if dma is not None:
            R = self.DMA_RING
            i = self.dcount.get(dma, 0)
            self.dcount[dma] = i + 1
            sname = f"{dma}#{i % R}"
            if sname in self.sem and self.cnt[sname] > 0:
                hp_, vp_ = self.sem[sname], self.cnt[sname]
                if kn.get(id(hp_), 0) < vp_:
                    e.wait_ge(hp_, vp_)
                    self.n_wait += 1
                    kn[id(hp_)] = vp_
        inst = fn(*args, **kw)
        if dma is None:
            self.neng[eng] += 1
        if dma is not None:
            h, v = self._bump(sname, 16)
            inst.then_inc(h, 16)
            rec = (h, v, 'dma:' + sname)
        else:
            h, v = self._bump(eng, 1)
            inst.then_inc(h, 1)
            rec = (h, v, eng)
        for w in writes:
            self.last_w[w] = rec
            self.readers[w] = {}
        for r in reads:
            self.readers.setdefault(r, {})[rec[2]] = rec
        self.n_inst += 1
        return inst

    def barrier(self, engs=None, label=""):
        self.marks.append((label, dict(self.neng)))
        for eng in (engs or self.engs):
            e = self.engs[eng]
            kn = self.known[eng]
            for h, v, nm in self.allsems:
                if nm.startswith('bg'):
                    continue
                if v > 0 and kn.get(id(h), 0) < v:
                    e.wait_ge(h, v)
                    kn[id(h)] = v
                    self.n_wait += 1
        if engs is None:
            self.last_w = {k: v for k, v in self.last_w.items() if str(k).startswith('bg:')}
            self.readers = {k: v for k, v in self.readers.items() if str(k).startswith('bg:')}


def _hp():
    return contextlib.ExitStack()


def build(TL, L, do_attn=True, do_peer=True, ctx_update=True):
    T = TL + CTX
    NT = T // 128
    NTL = TL // 128
    blocks = [(i * 512, 512, 0) for i in range(TL // 512)] + [(TL, CTX, 1)]
    nc = bass.Bass("TRN2", target_bir_lowering=False)
    s = Sched(nc)

    def din(name, shape, dt=F32):
        return nc.dram_tensor(name, list(shape), dt, kind="ExternalInput").ap()

    def dscr(name, shape, dt):
        return nc.dram_tensor(name, list(shape), dt).ap()

    xin = din("xin", [T, D])
    cT2 = din("cT2", [128, KC, 2])
    w_mod = din("w_mod", [L, D, 6 * D])
    b_modT = din("b_modT", [L, 128, 48])
    gvec = din("gvec", [L, 128, 21])
    gqk = din("gqk", [L, 64, 4])
    g_finalT = din("g_finalT", [128, KC])
    sinkb = din("sinkb", [128, L * 8])
    w_in = din("w_in", [L, D, 5280])
    w_in_sw = din("w_in_sw", [L, D, 32 + 640 + 640])
    w_uq = din("w_uq", [L, 384, 768])
    w_uq_sw = din("w_uq_sw", [L, 384, 256])
    w_ukv = din("w_ukv", [L, 256, 1024])
    w_oa = din("w_oa", [L, 512, D])
    w_ob = din("w_ob", [L, 512, D])
    w_ow = din("w_ow", [L, 512, D])
    w_o = din("w_o", [L, D, D])
    w_pq = din("w_pq", [L, D, 2048])
    skT = din("skT", [L, 128, 16, 128])
    GI = 2
    NG = 128 // GI
    peer_uG = din("peer_uG", [L, NG * 128, KC * GI * 128])
    peer_vG = din("peer_vG", [L, NG * 128, GI * D])
    ubf_d = dscr("ubf_d", [NG * 128, KC * GI * 128], BF16)
    vbf_d = dscr("vbf_d", [NG * 128, GI * D], BF16)
    rope64 = din("rope64", [2, 64, T])
    rope32 = din("rope32", [2, 32, T])
    consts = din("consts", [128, 8, 128])
    yout = nc.dram_tensor("yout", [TL // 2, D], F32, kind="ExternalOutput").ap()

    xT_d = dscr("xT_d", [D, T], F32)
    hxT_d = dscr("hxT_d", [D, T], BF16)
    qnT_d = dscr("qnT_d", [8, 64, T], BF16)
    qrT_d = dscr("qrT_d", [8, 32, T], BF16)
    knT_d = dscr("knT_d", [8, 64, T], BF16)
    krT_d = dscr("krT_d", [32, T], BF16)
    vA_d = dscr("vA_d", [T, 8, 65], BF16)
    qbT_d = dscr("qbT_d", [8, 64, T], BF16)
    kbT_d = dscr("kbT_d", [2, 64, T], BF16)
    vB_d = dscr("vB_d", [T, 2, 65], BF16)
    qwT_d = dscr("qwT_d", [8, 64, T], BF16)
    kwT_d = dscr("kwT_d", [2, 64, T], BF16)
    vW_d = dscr("vW_d", [T, 2, 65], BF16)
    gT_d = dscr("gT_d", [3 * D, T], BF16)
    oT_d = [dscr(f"oT_d{m}", [512, T], BF16) for m in range(3)]

    xT_v = xT_d.rearrange("(k p) t -> p k t", p=128)
    hxT_v = hxT_d.rearrange("(k p) t -> p k t", p=128)

    with contextlib.ExitStack() as G:
        uid = [0]

        def SB(name, shape, dt, st=G):
            uid[0] += 1
            return st.enter_context(nc.sbuf_tensor(f"{name}_u{uid[0]}", list(shape), dt))

        def PSt(name, shape, dt, st=G):
            return st.enter_context(nc.psum_tensor(name, list(shape), dt))

        PS = [PSt(f"ps{i}", [128, 512], F32) for i in range(7)]
        PSB = PSt("psb", [128, 1024], BF16)

        def dma(out, in_, reads, writes, q='ld', eng='sp'):
            fn = nc.sync.dma_start if eng == 'sp' else nc.gpsimd.dma_start
            return s.op(eng, fn, out=out, in_=in_, reads=reads, writes=writes, dma=q + eng)

        def mm(out, lhsT, rhs, start, stop, reads, writes):
            return s.op('pe', nc.tensor.matmul, out, lhsT=lhsT, rhs=rhs, start=start, stop=stop,
                        reads=reads, writes=writes)

        def act(out, in_, func, reads, writes, **kw):
            return s.op('act', nc.scalar.activation, out=out, in_=in_, func=func, reads=reads, writes=writes, **kw)

        def tt(eng, out, in0, in1, op, reads, writes):
            fn = nc.vector.tensor_tensor if eng == 'dve' else nc.gpsimd.tensor_tensor
            return s.op(eng, fn, out=out, in0=in0, in1=in1, op=op, reads=reads, writes=writes)

        def ts(eng, out, in0, s1, op0, reads, writes, s2=None, op1=None):
            fn = nc.vector.tensor_scalar if eng == 'dve' else nc.gpsimd.tensor_scalar
            kw = {}
            if op1 is not None:
                kw['op1'] = op1
            return s.op(eng, fn, out=out, in0=in0, scalar1=s1, scalar2=s2, op0=op0, reads=reads, writes=writes, **kw)

        def stt(out, in0, scalar, in1, op0, op1, reads, writes):
            return s.op('dve', nc.vector.scalar_tensor_tensor, out=out, in0=in0, scalar=scalar, in1=in1,
                        op0=op0, op1=op1, reads=reads, writes=writes)

        def cp(eng, out, in_, reads, writes):
            if eng == 'act':
                return s.op('act', nc.scalar.copy, out=out, in_=in_, reads=reads, writes=writes)
            fn = nc.vector.tensor_copy if eng == 'dve' else nc.gpsimd.tensor_copy
            return s.op(eng, fn, out=out, in_=in_, reads=reads, writes=writes)

        cst_f = SB("cst_f", [128, 8, 128], F32)
        cst_b = SB("cst_b", [128, 8, 128], BF16)
        dma(cst_f[:], consts, ['consts'], ['cst_f'])
        cp('dve', cst_b[:], cst_f[:], ['cst_f'], ['cst_b'])
        ident_f = cst_f[:, 0, :]
        ones_f = cst_f[:, 1, :]
        ident_b = cst_b[:, 0, :]
        nm_lo = cst_b[:, 2, :]
        nm_hi = cst_b[:, 3, :]
        seam_wrapL, seam_wrapR, seam_midL, seam_midR = (cst_b[:, 4 + i, :] for i in range(4))
        epsc = SB("epsc", [128, 1], F32)
        s.op('dve', nc.vector.memset, epsc[:], EPS, writes=['epsc'])
        scT = SB("scT", [128, KC, 2], F32)
        dma(scT[:], cT2, ['cT2'], ['scT'])
        act(scT[:], scT[:], AF.Silu, ['scT'], ['scT'])
        esink = SB("esink", [128, L * 8], F32)
        dma(esink[:], sinkb, ['sinkb'], ['esink'])
        act(esink[:], esink[:], AF.Exp, ['esink'], ['esink'])
        gfin = SB("gfin", [128, KC], F32)
        dma(gfin[:], g_finalT, ['gfin_d'], ['gfin'])
        modT = SB("modT", [128, 48, 2], F32)
        gsc = SB("gsc", [128, 2, KC, 2], F32)
        gv = SB("gv", [128, 21], F32)
        gq = SB("gq", [64, 4], F32)
        bmod = SB("bmod", [128, 48], F32)

        def rstd_from_ps(ps_ap, out_ap, n, rkeys, wkeys, np_=128):
            act(out_ap, ps_ap, AF.Sqrt, rkeys, wkeys, scale=1.0 / n, bias=epsc[0:np_, :])
            s.op('dve', nc.vector.reciprocal, out=out_ap, in_=out_ap, reads=wkeys, writes=wkeys)

        with _hp() as P:
            xin_t = [SB(f"xin_t{i}", [128, D], F32, P) for i in range(2)]
            xo_t = [SB(f"xo_t{i}", [128, KC, 128], F32, P) for i in range(2)]
            for j in range(NT):
                a = j % 2
                dma(xin_t[a][:], xin[j * 128:(j + 1) * 128, :], ['xin'], [f'xin_t{a}'])
                for hk in range(2):
                    pst = PS[2 * a + hk]
                    for kk in range(4):
                        k = hk * 4 + kk
                        s.op('pe', nc.tensor.transpose, pst[:, kk * 128:(kk + 1) * 128],
                             xin_t[a][:, k * 128:(k + 1) * 128], ident_f,
                             reads=[f'xin_t{a}', 'cst_f'], writes=[f'ps{2 * a + hk}'])
                    cp('dve' if hk == 0 else 'act', xo_t[a][:, hk * 4:(hk + 1) * 4, :],
                       pst[:].rearrange("p (k t) -> p k t", k=4), [f'ps{2 * a + hk}'], [f'xo_t{a}'])
                dma(xT_v[:, :, j * 128:(j + 1) * 128], xo_t[a][:], [f'xo_t{a}'], ['xT_d'], q='st', eng='pool')
        s.barrier()

        def load_x_and_norm(P_tiles, blk, sub, hx_out, hx_key):
            t0, nt, isc = blk
            xs, sq, rstd, tmpf = P_tiles['xs'], P_tiles['sq'], P_tiles['rstd'], P_tiles['tmpf']
            dma(xs[:, :, 0:nt], xT_v[:, :, t0:t0 + nt], ['xT_d'], ['xs'])
            for k in range(KC):
                a = k % 2
                act(sq[a][:, 0:nt], xs[:, k, 0:nt], AF.Square, ['xs'], [f'sq{a}'])
                mm(PS[6][:, 0:nt], ones_f, sq[a][:, 0:nt], k == 0, k == KC - 1, [f'sq{a}', 'cst_f'], ['ps6'])
            rstd_from_ps(PS[6][:, 0:nt], rstd[:, 0:nt], D, ['ps6', 'epsc'], ['rstd'])
            shoff = 0 if sub == 0 else 24
            for k in range(KC):
                a = k % 2
                stt(tmpf[a][:, 0:nt], xs[:, k, 0:nt], gsc[:, sub, k, isc:isc + 1], rstd[:, 0:nt], ALU.mult, ALU.mult,
                    ['xs', 'gsc', 'rstd'], [f'tmpf{a}'])
                act(hx_out[:, k, 0:nt], tmpf[a][:, 0:nt], AF.Identity, [f'tmpf{a}', 'modT'], [hx_key],
                    bias=modT[:, shoff + k, isc:isc + 1])

        def norm_tiles(P):
            return {'xs': SB("xs", [128, KC, 512], F32, P),
                    'sq': [SB(f"sq{i}", [128, 512], F32, P) for i in range(2)],
                    'rstd': SB("rstd", [128, 512], F32, P),
                    'tmpf': [SB(f"tmpf{i}", [128, 512], F32, P) for i in range(2)]}

        def load_w_bf16(dst, src, key, srckey):
            dma(dst, src, [srckey], [key], q='w', eng='pool')

        for l in range(L):
            last = (l == L - 1)
            with _hp() as P:
                wm = [SB(f"wm{i}", [128, KC, 512], F32, P) for i in range(2)]
                if do_peer:
                    nrow = NG * 128
                    for q8 in range(8):
                        r0, r1 = q8 * nrow // 8, (q8 + 1) * nrow // 8
                        dma(ubf_d[r0:r1, :], peer_uG[l][r0:r1, :], ['peer_uG', 'bg:ld'], ['bg:ubf'], q='bg', eng='pool')
                        dma(vbf_d[r0:r1, :], peer_vG[l][r0:r1, :], ['peer_vG', 'bg:ld'], ['bg:vbf'], q='bg', eng='pool')
                dma(bmod[:], b_modT[l], ['b_modT'], ['bmod'])
                dma(gv[:], gvec[l], ['gvec'], ['gv'])
                dma(gq[:], gqk[l], ['gqk'], ['gq'])
                for nb in range(12):
                    a = nb % 2
                    dma(wm[a][:], w_mod[l].rearrange("(k p) n -> p k n", p=128)[:, :, nb * 512:(nb + 1) * 512],
                        ['w_mod'], [f'wm{a}'])
                    for c in range(4):
                        n = nb * 4 + c
                        for k in range(KC):
                            mm(PS[a][:, c * 2:c * 2 + 2], wm[a][:, k, c * 128:(c + 1) * 128], scT[:, k, :],
                               k == 0, k == KC - 1, [f'wm{a}', 'scT'], [f'ps{a}'])
                        ts('dve', modT[:, n, :], PS[a][:, c * 2:c * 2 + 2], bmod[:, n:n + 1], ALU.add,
                           [f'ps{a}', 'bmod'], ['modT'])
                for sub in range(2):
                    scoff = 8 if sub == 0 else 32
                    goff = 0 if sub == 0 else 8
                    for v in range(2):
                        ts('dve', gsc[:, sub, :, v], modT[:, scoff:scoff + 8, v], 1.0, ALU.add,
                           ['modT'], ['gsc'])
                        tt('dve', gsc[:, sub, :, v], gsc[:, sub, :, v], gv[:, goff:goff + 8], ALU.mult,
                           ['gsc', 'gv'], ['gsc'])
            s.barrier()

            if do_attn:
                with _hp() as P:
                    NTl = norm_tiles(P)
                    hxs = [SB(f"hxs{i}", [128, KC, 512], BF16, P) for i in range(2)]
                    for bi, blk in enumerate(blocks):
                        t0, nt, isc = blk
                        a = bi % 2
                        load_x_and_norm(NTl, blk, 0, hxs[a], f'hxs{a}')
                        dma(hxT_v[:, :, t0:t0 + nt], hxs[a][:, :, 0:nt], [f'hxs{a}'], ['hxT_d'], q='st', eng='pool')
                s.barrier()

                with _hp() as P:
                    wA = SB("wA", [128, KC, 704], BF16, P)
                    wuq = SB("wuq", [128, 3, 1024], BF16, P)
                    wukv = SB("wukv", [128, 2, 1024], BF16, P)
                    wiv = w_in[l].rearrange("(k p) n -> p k n", p=128)
                    load_w_bf16(wA[:, :, 0:672], wiv[:, :, 0:672], 'wA', 'w_in')
                    load_w_bf16(wA[:, :, 672:704], w_in_sw[l].rearrange("(k p) n -> p k n", p=128)[:, :, 0:32], 'wA', 'w_in_sw')
                    load_w_bf16(wuq[:, :, 0:768], w_uq[l].rearrange("(k p) n -> p k n", p=128), 'wuq', 'w_uq')
                    load_w_bf16(wuq[:, :, 768:1024], w_uq_sw[l].rearrange("(k p) n -> p k n", p=128), 'wuq', 'w_uq_sw')
                    load_w_bf16(wukv[:], w_ukv[l].rearrange("(k p) n -> p k n", p=128), 'wukv', 'w_ukv')
                    hxb = [SB(f"hxb{i}", [128, KC, 512], BF16, P) for i in range(2)]
                    r32 = [SB(f"r32_{i}", [32, 2, 512], F32, P) for i in range(2)]
                    cqf = SB("cqf", [128, 5, 512], F32, P)
                    cqn = SB("cqn", [128, 5, 512], BF16, P)
                    sqa = [SB(f"sqa{i}", [128, 512], F32, P) for i in range(2)]
                    rsa = SB("rsa", [128, 2, 512], F32, P)
                    t32 = [SB(f"t32_{i}", [32, 512], F32, P) for i in range(2)]
                    o32 = [SB(f"o32_{i}", [32, 512], BF16, P) for i in range(2)]
                    o64 = [SB(f"o64_{i}", [64, 512], BF16, P) for i in range(2)]
                    vst = [SB(f"vst{i}", [128, 8, 65], BF16, P) for i in range(2)]
                    for i in range(2):
                        s.op('pool', nc.gpsimd.memset, vst[i][:], 1.0, writes=[f'vst{i}'])
                    ev = 0
                    for bi, blk in enumerate(blocks):
                        t0, nt, isc = blk
                        a = bi % 2
                        dma(hxb[a][:, :, 0:nt], hxT_v[:, :, t0:t0 + nt], ['hxT_d'], [f'hxb{a}'])
                        dma(r32[a][:, :, 0:nt], rope32.rearrange("c p t -> p c t")[:, :, t0:t0 + nt], ['rope32'], [f'r32_{a}'])
                        for c in range(5):
                            pb = c % 2
                            for k in range(KC):
                                mm(PS[pb][:, 0:nt], wA[:, k, c * 128:(c + 1) * 128], hxb[a][:, k, 0:nt], k == 0, k == KC - 1,
                                   ['wA', f'hxb{a}'], [f'ps{pb}'])
                            cp('dve', cqf[:, c, 0:nt], PS[pb][:, 0:nt], [f'ps{pb}'], ['cqf'])
                        for grp, (c0, c1, n) in enumerate([(0, 3, 384), (3, 5, 256)]):
                            for c in range(c0, c1):
                                sa = c % 2
                                act(sqa[sa][:, 0:nt], cqf[:, c, 0:nt], AF.Square, ['cqf'], [f'sqa{sa}'])
                                mm(PS[2 + grp][:, 0:nt], ones_f, sqa[sa][:, 0:nt], c == c0, c == c1 - 1,
                                   [f'sqa{sa}', 'cst_f'], [f'ps{2 + grp}'])
                            rstd_from_ps(PS[2 + grp][:, 0:nt], rsa[:, grp, 0:nt], n, [f'ps{2 + grp}', 'epsc'], ['rsa'])
                            for c in range(c0, c1):
                                stt(cqn[:, c, 0:nt], cqf[:, c, 0:nt], gv[:, 16 + c:17 + c], rsa[:, grp, 0:nt], ALU.mult, ALU.mult,
                                    ['cqf', 'gv', 'rsa'], ['cqn'])
                        for k in range(KC):
                            mm(PS[4][0:32, 0:nt], wA[:, k, 640:672], hxb[a][:, k, 0:nt], k == 0, k == KC - 1, ['wA', f'hxb{a}'], ['ps4'])
                        for k in range(KC):
                            mm(PS[5][0:32, 0:nt], wA[:, k, 672:704], hxb[a][:, k, 0:nt], k == 0, k == KC - 1, ['wA', f'hxb{a}'], ['ps5'])
                        e = ev % 2; ev += 1
                        tt('dve', t32[0][:, 0:nt], PS[4][0:32, 0:nt], r32[a][:, 0, 0:nt], ALU.mult, ['ps4', f'r32_{a}'], ['t32_0'])
                        tt('dve', t32[1][:, 0:nt], PS[5][0:32, 0:nt], r32[a][:, 1, 0:nt], ALU.mult, ['ps5', f'r32_{a}'], ['t32_1'])
                        tt('pool', o32[e][:, 0:nt], t32[0][:, 0:nt], t32[1][:, 0:nt], ALU.add, ['t32_0', 't32_1'], [f'o32_{e}'])
                        dma(krT_d[:, t0:t0 + nt], o32[e][:, 0:nt], [f'o32_{e}'], ['krT_d'], q='st', eng='pool')
                        for h in range(8):
                            pb = h % 2
                            for c in range(3):
                                mm(PS[pb][0:64, 0:nt], wuq[:, c, h * 96:h * 96 + 64], cqn[:, c, 0:nt], c == 0, c == 2, ['wuq', 'cqn'], [f'ps{pb}'])
                            e = h % 2
                            cp('act', o64[e][:, 0:nt], PS[pb][0:64, 0:nt], [f'ps{pb}'], [f'o64_{e}'])
                            dma(qnT_d[h, :, t0:t0 + nt], o64[e][:, 0:nt], [f'o64_{e}'], ['qnT_d'], q='st', eng='pool')
                            for c in range(3):
                                mm(PS[4][0:32, 0:nt], wuq[:, c, h * 96 + 64:h * 96 + 96], cqn[:, c, 0:nt], c == 0, c == 2, ['wuq', 'cqn'], ['ps4'])
                            for c in range(3):
                                mm(PS[5][0:32, 0:nt], wuq[:, c, 768 + h * 32:768 + h * 32 + 32], cqn[:, c, 0:nt], c == 0, c == 2, ['wuq', 'cqn'], ['ps5'])
                            e = ev % 2; ev += 1
                            tt('dve', t32[0][:, 0:nt], PS[4][0:32, 0:nt], r32[a][:, 0, 0:nt], ALU.mult, ['ps4', f'r32_{a}'], ['t32_0'])
                            tt('dve', t32[1][:, 0:nt], PS[5][0:32, 0:nt], r32[a][:, 1, 0:nt], ALU.mult, ['ps5', f'r32_{a}'], ['t32_1'])
                            tt('pool', o32[e][:, 0:nt], t32[0][:, 0:nt], t32[1][:, 0:nt], ALU.add, ['t32_0', 't32_1'], [f'o32_{e}'])
                            dma(qrT_d[h, :, t0:t0 + nt], o32[e][:, 0:nt], [f'o32_{e}'], ['qrT_d'], q='st', eng='pool')
                            pb = 2 + h % 2
                            for c in range(2):
                                mm(PS[pb][0:64, 0:nt], wukv[:, c, h * 128:h * 128 + 64], cqn[:, 3 + c, 0:nt], c == 0, c == 1, ['wukv', 'cqn'], [f'ps{pb}'])
                            e2 = (h + 1) % 2
                            cp('act', o64[e2][:, 0:nt], PS[pb][0:64, 0:nt], [f'ps{pb}'], [f'o64_{e2}'])
                            dma(knT_d[h, :, t0:t0 + nt], o64[e2][:, 0:nt], [f'o64_{e2}'], ['knT_d'], q='st', eng='pool')
                        for j in range(nt // 128):
                            va = j % 2
                            for c in range(2):
                                mm(PS[6][:, :].rearrange("p (h d) -> p h d", h=8), cqn[:, 3 + c, j * 128:(j + 1) * 128],
                                   wukv[:, c, :].rearrange("p (h d) -> p h d", h=8)[:, :, 64:128], c == 0, c == 1, ['wukv', 'cqn'], ['ps6'])
                            cp('dve', vst[va][:, :, 0:64], PS[6][:, :].rearrange("p (h d) -> p h d", h=8), ['ps6'], [f'vst{va}'])
                            dma(vA_d[t0 + j * 128:t0 + (j + 1) * 128, :, :], vst[va][:], [f'vst{va}'], ['vA_d'], q='st', eng='pool')
                s.barrier()

                for mix in range(2):
                    with _hp() as P:
                        col0 = 672 if mix == 0 else 1440
                        sw0 = 32 if mix == 0 else 672
                        qT_dst, kT_dst, v_dst = (qbT_d, kbT_d, vB_d) if mix == 0 else (qwT_d, kwT_d, vW_d)
                        wq = SB("wq", [128, KC, 1408], BF16, P)
                        wiv = w_in[l].rearrange("(k p) n -> p k n", p=128)
                        load_w_bf16(wq[:, :, 0:768], wiv[:, :, col0:col0 + 768], 'wq', 'w_in')
                        load_w_bf16(wq[:, :, 768:1408], w_in_sw[l].rearrange("(k p) n -> p k n", p=128)[:, :, sw0:sw0 + 640], 'wq', 'w_in_sw')
                        hxb = [SB(f"hxb{i}", [128, KC, 512], BF16, P) for i in range(2)]
                        r64 = [SB(f"r64_{i}", [64, 2, 512], F32, P) for i in range(2)]
                        qf = [SB(f"qf{i}", [64, 512], F32, P) for i in range(2)]
                        sq6 = [SB(f"sq6_{i}", [64, 512], F32, P) for i in range(2)]
                        rs6 = [SB(f"rs6_{i}", [64, 512], F32, P) for i in range(2)]
                        t64 = [SB(f"t64_{i}", [64, 512], F32, P) for i in range(2)]
                        o64 = [SB(f"o64_{i}", [64, 512], BF16, P) for i in range(2)]
                        vst = [SB(f"vst{i}", [128, 2, 65], BF16, P) for i in range(2)]
                        for i in range(2):
                            s.op('pool', nc.gpsimd.memset, vst[i][:], 1.0, writes=[f'vst{i}'])
                        ev = 0
                        for bi, blk in enumerate(blocks):
                            t0, nt, isc = blk
                            a = bi % 2
                            dma(hxb[a][:, :, 0:nt], hxT_v[:, :, t0:t0 + nt], ['hxT_d'], [f'hxb{a}'])
                            dma(r64[a][:, :, 0:nt], rope64.rearrange("c p t -> p c t")[:, :, t0:t0 + nt], ['rope64'], [f'r64_{a}'])
                            for hh in range(10):
                                isk = hh >= 8
                                c_main = (hh * 64) if not isk else (512 + (hh - 8) * 64)
                                c_sw = (768 + hh * 64) if not isk else (768 + 512 + (hh - 8) * 64)
                                e = ev % 2; ev += 1
                                pa, pb = 2 * e, 2 * e + 1
                                for k in range(KC):
                                    mm(PS[pa][0:64, 0:nt], wq[:, k, c_main:c_main + 64], hxb[a][:, k, 0:nt], k == 0, k == KC - 1, ['wq', f'hxb{a}'], [f'ps{pa}'])
                                for k in range(KC):
                                    mm(PS[pb][0:64, 0:nt], wq[:, k, c_sw:c_sw + 64], hxb[a][:, k, 0:nt], k == 0, k == KC - 1, ['wq', f'hxb{a}'], [f'ps{pb}'])
                                if mix == 0:
                                    gcol = 2 if isk else 0
                                    act(sq6[e][:, 0:nt], PS[pa][0:64, 0:nt], AF.Square, [f'ps{pa}'], [f'sq6_{e}'])
                                    mm(PS[4 + e][0:64, 0:nt], ones_f[0:64, 0:64], sq6[e][:, 0:nt], True, True, [f'sq6_{e}', 'cst_f'], [f'ps{4 + e}'])
                                    rstd_from_ps(PS[4 + e][0:64, 0:nt], rs6[e][:, 0:nt], 64, [f'ps{4 + e}', 'epsc'], [f'rs6_{e}'], np_=64)
                                    stt(qf[e][:, 0:nt], PS[pa][0:64, 0:nt], gq[:, gcol:gcol + 1], r64[a][:, 0, 0:nt], ALU.mult, ALU.mult,
                                        [f'ps{pa}', 'gq', f'r64_{a}'], [f'qf{e}'])
                                    stt(t64[e][:, 0:nt], PS[pb][0:64, 0:nt], gq[:, gcol + 1:gcol + 2], r64[a][:, 1, 0:nt], ALU.mult, ALU.mult,
                                        [f'ps{pb}', 'gq', f'r64_{a}'], [f't64_{e}'])
                                    tt('pool', t64[e][:, 0:nt], t64[e][:, 0:nt], qf[e][:, 0:nt], ALU.add, [f't64_{e}', f'qf{e}'], [f't64_{e}'])
                                    tt('pool', o64[e][:, 0:nt], t64[e][:, 0:nt], rs6[e][:, 0:nt], ALU.mult, [f't64_{e}', f'rs6_{e}'], [f'o64_{e}'])
                                else:
                                    tt('dve', qf[e][:, 0:nt], PS[pa][0:64, 0:nt], r64[a][:, 0, 0:nt], ALU.mult, [f'ps{pa}', f'r64_{a}'], [f'qf{e}'])
                                    tt('dve', t64[e][:, 0:nt], PS[pb][0:64, 0:nt], r64[a][:, 1, 0:nt], ALU.mult, [f'ps{pb}', f'r64_{a}'], [f't64_{e}'])
                                    tt('pool', o64[e][:, 0:nt], t64[e][:, 0:nt], qf[e][:, 0:nt], ALU.add, [f't64_{e}', f'qf{e}'], [f'o64_{e}'])
                                dst = qT_dst[hh] if not isk else kT_dst[hh - 8]
                                dma(dst[:, t0:t0 + nt], o64[e][:, 0:nt], [f'o64_{e}'], ['qk_d'], q='st', eng='pool')
                            for j in range(nt // 128):
                                va = j % 2
                                for k in range(KC):
                                    mm(PS[6][:, 0:128], hxb[a][:, k, j * 128:(j + 1) * 128], wq[:, k, 640:768], k == 0, k == KC - 1, ['wq', f'hxb{a}'], ['ps6'])
                                cp('act', vst[va][:, :, 0:64], PS[6][:, 0:128].rearrange("p (h d) -> p h d", h=2), ['ps6'], [f'vst{va}'])
                                dma(v_dst[t0 + j * 128:t0 + (j + 1) * 128, :, :], vst[va][:], [f'vst{va}'], ['v_d'], q='st', eng='pool')
                    s.barrier()

                with _hp() as P:
                    wg = SB("wg", [128, KC, 3072], BF16, P)
                    wiv = w_in[l].rearrange("(k p) n -> p k n", p=128)
                    for q4 in range(4):
                        load_w_bf16(wg[:, :, q4 * 768:(q4 + 1) * 768], wiv[:, :, 2208 + q4 * 768:2208 + (q4 + 1) * 768], 'wg', 'w_in')
                    hxb = [SB(f"hxb{i}", [128, KC, 512], BF16, P) for i in range(2)]
                    og = [SB(f"og{i}", [128, 4, 512], BF16, P) for i in range(2)]
                    gT_v = gT_d.rearrange("(c p) t -> p c t", p=128)
                    ev = 0
                    for bi, blk in enumerate(blocks):
                        t0, nt, isc = blk
                        a = bi % 2
                        dma(hxb[a][:, :, 0:nt], hxT_v[:, :, t0:t0 + nt], ['hxT_d'], [f'hxb{a}'])
                        for c4 in range(6):
                            e = ev % 2; ev += 1
                            for cc in range(4):
                                c = c4 * 4 + cc
                                pb = c % 4
                                for k in range(KC):
                                    mm(PS[pb][:, 0:nt], wg[:, k, c * 128:(c + 1) * 128], hxb[a][:, k, 0:nt], k == 0, k == KC - 1, ['wg', f'hxb{a}'], [f'ps{pb}'])
                                act(og[e][:, cc, 0:nt], PS[pb][:, 0:nt], AF.Sigmoid, [f'ps{pb}'], [f'og{e}'])
                            dma(gT_v[:, c4 * 4:(c4 + 1) * 4, t0:t0 + nt], og[e][:, :, 0:nt], [f'og{e}'], ['gT_d'], q='st', eng='pool')
                s.barrier()

                with _hp() as P:
                    kn_sb = [SB(f"kn_sb{i}", [64, T], BF16, P) for i in range(2)]
                    kr_sb = SB("kr_sb", [32, T], BF16, P)
                    v_sb = [SB(f"v_sb{i}", [128, NT, 65], BF16, P) for i in range(2)]
                    qn_sb = [SB(f"qn_sb{i}", [64, 512], BF16, P) for i in range(2)]
                    qr_sb = [SB(f"qr_sb{i}", [32, 512], BF16, P) for i in range(2)]
                    p_sb = [SB(f"p_sb{i}", [128, 512], BF16, P) for i in range(4)]
                    rz = [SB(f"rz{i}", [128, 4], F32, P) for i in range(2)]
                    o_sb = [SB(f"o_sb{i}", [128, 4, 64], BF16, P) for i in range(2)]
                    oT_sb = [SB(f"oT_sb{i}", [64, 512], BF16, P) for i in range(2)]
                    dma(kr_sb[:], krT_d, ['krT_d'], ['kr_sb'])
                    cnt = {'kv': 0, 'q': 0, 'p': 0, 'o': 0, 's': 0}
                    OB = [PS[2], PS[3], PS[4], PS[5]]

                    def attn_unit(qparts, kparts, vt, nq, ktiles, scale, sink_col, out_dst):
                        nj = nq // 128
                        nk = len(ktiles)
                        pis = {}
                        LA = 2
                        for ki in range(nk + LA):
                            if ki < nk:
                                kt, mask = ktiles[ki]
                                sb_i = (0, 1, 6)[cnt['s'] % 3]; cnt['s'] += 1
                                sps = PS[sb_i]
                                nparts = len(qparts) + (1 if mask is not None else 0)
                                for pi, ((qap, qk), (kap, kk)) in enumerate(zip(qparts, kparts)):
                                    mm(sps[:, 0:nq], kap[:, kt * 128:(kt + 1) * 128], qap[:, 0:nq], pi == 0, pi == nparts - 1,
                                       [qk, kk], [f'ps{sb_i}'])
                                if mask is not None:
                                    mm(sps[:, 0:nq], ident_b, mask, False, True, ['cst_b'], [f'ps{sb_i}'])
                                pi_ = cnt['p'] % 4; cnt['p'] += 1
                                pis[ki] = pi_
                                act(p_sb[pi_][:, 0:nq], sps[:, 0:nq], AF.Exp, [f'ps{sb_i}'], [f'p_sb{pi_}'], scale=scale)
                            if ki >= LA:
                                kp = ki - LA
                                ktp = ktiles[kp][0]
                                pp_ = pis[kp]
                                for j in range(nj):
                                    mm(OB[j][:, 0:65], p_sb[pp_][:, j * 128:(j + 1) * 128], vt[0][:, ktp, :], kp == 0, kp == nk - 1,
                                       [f'p_sb{pp_}', vt[1]], [f'ps{2 + j}'])
                        oi = cnt['o'] % 2; cnt['o'] += 1
                        for j in range(nj):
                            if sink_col is not None:
                                ts('dve', rz[oi][:, j:j + 1], OB[j][:, 64:65], esink[:, sink_col:sink_col + 1], ALU.add,
                                   [f'ps{2 + j}', 'esink'], [f'rz{oi}'])
                                s.op('dve', nc.vector.reciprocal, out=rz[oi][:, j:j + 1], in_=rz[oi][:, j:j + 1], reads=[f'rz{oi}'], writes=[f'rz{oi}'])
                            else:
                                s.op('dve', nc.vector.reciprocal, out=rz[oi][:, j:j + 1], in_=OB[j][:, 64:65], reads=[f'ps{2 + j}'], writes=[f'rz{oi}'])
                            ts('dve', o_sb[oi][:, j, :], OB[j][:, 0:64], rz[oi][:, j:j + 1], ALU.mult, [f'ps{2 + j}', f'rz{oi}'], [f'o_sb{oi}'])
                            s.op('pe', nc.tensor.transpose, PSB[0:64, j * 128:(j + 1) * 128], o_sb[oi][:, j, :], ident_b,
                                 reads=[f'o_sb{oi}', 'cst_b'], writes=['psb'])
                        cp('act', oT_sb[oi][:, 0:nq], PSB[0:64, 0:nq], ['psb'], [f'oT_sb{oi}'])
                        dma(out_dst, oT_sb[oi][:, 0:nq], [f'oT_sb{oi}'], ['oT_d'], q='st', eng='pool')

                    mixers = [(0, qnT_d, knT_d, vA_d, 96 ** -0.5), (1, qbT_d, kbT_d, vB_d, 0.125), (2, qwT_d, kwT_d, vW_d, 0.125)]
                    for (m, qd, kd, vd, scale) in mixers:
                        for h in range(8):
                            kvh = h if m == 0 else h // 4
                            if m == 0 or h % 4 == 0:
                                kvi = cnt['kv'] % 2; cnt['kv'] += 1
                                dma(kn_sb[kvi][:], kd[kvh], ['qk_d', 'knT_d'], [f'kn_sb{kvi}'])
                                dma(v_sb[kvi][:], vd[:, kvh, :].rearrange("(n p) c -> p n c", p=128), ['v_d', 'vA_d'], [f'v_sb{kvi}'])
                            kparts = [(kn_sb[kvi], f'kn_sb{kvi}')]
                            if m == 0:
                                kparts.append((kr_sb, 'kr_sb'))
                            vt = (v_sb[kvi], f'v_sb{kvi}')
                            for bi, blk in enumerate(blocks):
                                t0, nt, isc = blk
                                if last and (isc or t0 >= TL // 2):
                                    continue
                                qi = cnt['q'] % 2; cnt['q'] += 1
                                dma(qn_sb[qi][:, 0:nt], qd[h, :, t0:t0 + nt], ['qk_d', 'qnT_d'], [f'qn_sb{qi}'])
                                qparts = [(qn_sb[qi], f'qn_sb{qi}')]
                                if m == 0:
                                    dma(qr_sb[qi][:, 0:nt], qrT_d[h, :, t0:t0 + nt], ['qrT_d'], [f'qr_sb{qi}'])
                                    qparts.append((qr_sb[qi], f'qr_sb{qi}'))
                                ctx_tiles = [(NTL, None), (NTL + 1, None)]
                                dst = oT_d[m][h * 64:(h + 1) * 64, t0:t0 + nt]
                                sink_col = (l * 8 + h) if m == 2 else None
                                if last and (isc or t0 >= TL // 2):
                                    continue
                                if isc:
                                    attn_unit(qparts, kparts, vt, nt, ctx_tiles, scale, sink_col, dst)
                                elif m < 2:
                                    attn_unit(qparts, kparts, vt, nt, [(kt, None) for kt in range(NT)], scale, sink_col, dst)
                                else:
                                    for j in range(nt // 128):
                                        n = t0 // 128 + j
                                        kts = []
                                        lm = seam_wrapL if n == 0 else (seam_midL if n == NTL // 2 else nm_lo)
                                        rm = seam_wrapR if n == NTL - 1 else (seam_midR if n == NTL // 2 - 1 else nm_hi)
                                        kts.append(((n - 1) % NTL, lm))
                                        kts.append((n, None))
                                        kts.append(((n + 1) % NTL, rm))
                                        kts += ctx_tiles
                                        qp = [(qparts[0][0][:, j * 128:(j + 1) * 128], qparts[0][1])]
                                        attn_unit(qp, kparts, vt, 128, kts, scale, sink_col,
                                                  oT_d[m][h * 64:(h + 1) * 64, t0 + j * 128:t0 + (j + 1) * 128])
                s.barrier()

                with _hp() as P:
                    wo3 = [SB(f"wo3_{m}", [128, 4, D], BF16, P) for m in range(3)]
                    wo = SB("wo", [128, KC, D], BF16, P)
                    for m, wsrc in enumerate([w_oa, w_ob, w_ow]):
                        load_w_bf16(wo3[m][:], wsrc[l].rearrange("(k p) n -> p k n", p=128), f'wo3_{m}', 'w_o3')
                    load_w_bf16(wo[:], w_o[l].rearrange("(k p) n -> p k n", p=128), 'wo', 'w_o')
                    oTb = [[SB(f"oTb{m}_{i}", [128, 4, 512], BF16, P) for i in range(2)] for m in range(3)]
                    gtb = [SB(f"gtb{i}", [128, 24, 512], BF16, P) for i in range(2)]
                    xs2 = [SB(f"xs2_{i}", [128, KC, 512], F32, P) for i in range(2)]
                    mT = SB("mT", [128, KC, 512], BF16, P)
                    acc = [SB(f"acc{i}", [128, 512], F32, P) for i in range(2)]
                    tmp = [SB(f"tmp{i}", [128, 512], F32, P) for i in range(2)]
                    gT_v = gT_d.rearrange("(c p) t -> p c t", p=128)
                    for bi, blk in enumerate(blocks):
                        t0, nt, isc = blk
                        if (isc and (last or not ctx_update)) or (last and t0 >= TL // 2):
                            continue
                        a = bi % 2
                        for m in range(3):
                            dma(oTb[m][a][:, :, 0:nt], oT_d[m].rearrange("(c p) t -> p c t", p=128)[:, :, t0:t0 + nt], ['oT_d'], [f'oTb{m}_{a}'])
                        dma(gtb[a][:, :, 0:nt], gT_v[:, :, t0:t0 + nt], ['gT_d'], [f'gtb{a}'])
                        dma(xs2[a][:, :, 0:nt], xT_v[:, :, t0:t0 + nt], ['xT_d'], [f'xs2_{a}'])
                        for dc in range(KC):
                            e = dc % 2
                            for m in range(3):
                                pb = m
                                for c in range(4):
                                    mm(PS[pb][:, 0:nt], wo3[m][:, c, dc * 128:(dc + 1) * 128], oTb[m][a][:, c, 0:nt], c == 0, c == 3,
                                       [f'wo3_{m}', f'oTb{m}_{a}'], [f'ps{pb}'])
                            tt('dve', acc[e][:, 0:nt], PS[0][:, 0:nt], gtb[a][:, dc, 0:nt], ALU.mult, ['ps0', f'gtb{a}'], [f'acc{e}'])
                            tt('dve', tmp[e][:, 0:nt], PS[1][:, 0:nt], gtb[a][:, 8 + dc, 0:nt], ALU.mult, ['ps1', f'gtb{a}'], [f'tmp{e}'])
                            tt('pool', acc[e][:, 0:nt], acc[e][:, 0:nt], tmp[e][:, 0:nt], ALU.add, [f'acc{e}', f'tmp{e}'], [f'acc{e}'])
                            tt('dve', tmp[e][:, 0:nt], PS[2][:, 0:nt], gtb[a][:, 16 + dc, 0:nt], ALU.mult, ['ps2', f'gtb{a}'], [f'tmp{e}'])
                            tt('pool', mT[:, dc, 0:nt], acc[e][:, 0:nt], tmp[e][:, 0:nt], ALU.add, [f'acc{e}', f'tmp{e}'], ['mT'])
                        for dc in range(KC):
                            pb = 4 + dc % 2
                            for k in range(KC):
                                mm(PS[pb][:, 0:nt], wo[:, k, dc * 128:(dc + 1) * 128], mT[:, k, 0:nt], k == 0, k == KC - 1, ['wo', 'mT'], [f'ps{pb}'])
                            stt(xs2[a][:, dc, 0:nt], PS[pb][:, 0:nt], modT[:, 16 + dc, isc:isc + 1], xs2[a][:, dc, 0:nt], ALU.mult, ALU.add,
                                [f'ps{pb}', 'modT', f'xs2_{a}'], [f'xs2_{a}'])
                        dma(xT_v[:, :, t0:t0 + nt], xs2[a][:, :, 0:nt], [f'xs2_{a}'], ['xT_d'], q='st', eng='pool')
                s.barrier()

            if do_peer:
                with _hp() as P:
                    NTl = norm_tiles(P)
                    xs = NTl['xs']
                    wpq = [SB(f"wpq{i}", [128, KC, 512], BF16, P) for i in range(2)]
                    skt = SB("skt", [128, 16, 128], BF16, P)
                    load_w_bf16(skt[:], skT[l], 'skt', 'skT')
                    hx2 = SB("hx2", [128, KC, 512], BF16, P)
                    qT = xs[:].bitcast(BF16).rearrange("p k (a t) -> p (k a) t", a=2)
                    Ssb = SB("Ssb", [128, 16, 128], F32, P)
                    Stmp = SB("Stmp", [128, 128], F32, P)
                    T16 = SB("T16", [128, 16, 16], F32, P)
                    Eall = SB("Eall", [128, 4, 16, 128], F32, P)
                    ET16 = SB("ET16", [128, 16, 16], F32, P)
                    cand = SB("cand", [128, 8, 256], BF16, P)
                    ctmp = SB("ctmp", [128, 256], BF16, P)
                    CT16 = SB("CT16", [128, 8, 16], F32, P)
                    theta = SB("theta", [128, 4, 8], F32, P)
                    zz = SB("zz", [128, 8], F32, P)
                    Dm = SB("Dm", [128, 4, 8, 128], BF16, P)
                    Pp = [SB(f"Pp{i}", [128, 2, GI, 128], BF16, P) for i in range(4)]
                    Gm = [SB(f"Gm{i}", [128, 4, 8, GI * 128], BF16, P) for i in range(2)]
                    UT = [SB(f"UT{i}", [128, KC, GI * 128], BF16, P) for i in range(2)]
                    Vg = [SB(f"Vg{i}", [128, GI, D], BF16, P) for i in range(3)]
                    a_sb = [SB(f"a_sb{i}", [128, GI, 512], F32, P) for i in range(2)]
                    GaT = [SB(f"GaT{i}", [128, GI, 512], BF16, P) for i in range(2)]
                    accp = SB("accp", [128, KC, 512], F32, P)
                    ubf_v = ubf_d.rearrange("(g p) (k e) -> g p k e", p=128, k=KC)
                    vbf_v = vbf_d.rearrange("(g p) (c d) -> g p c d", p=128, c=GI)
                    gcount = 0
                    pcount = [0]
                    mcount = [0]
                    fcount = [0]
                    ftmp = [SB(f"ftmp{i}", [128, 512], F32, P) for i in range(3)]
                    slot = {}
                    for bi, blk in enumerate(blocks):
                        t0, nt, isc = blk
                        if (isc and (last or not ctx_update)) or (last and t0 >= TL // 2):
                            continue
                        ntl = nt // 128
                        load_x_and_norm(NTl, blk, 1, hx2, 'hx2')
                        for hp in range(16):
                            pb = hp % 2
                            wi = (hp // 4) % 2
                            if hp % 4 == 0:
                                load_w_bf16(wpq[wi][:], w_pq[l].rearrange("(k p) n -> p k n", p=128)[:, :, hp * 128:(hp + 4) * 128], f'wpq{wi}', 'w_pq')
                            for k in range(KC):
                                mm(PS[pb][:, 0:nt], wpq[wi][:, k, (hp % 4) * 128:(hp % 4 + 1) * 128], hx2[:, k, 0:nt], k == 0, k == KC - 1, [f'wpq{wi}', 'hx2'], [f'ps{pb}'])
                            cp('act' if hp % 2 else 'dve', qT[:, hp, 0:nt], PS[pb][:, 0:nt], [f'ps{pb}', 'xs'], ['xs'])
                        for j in range(ntl):
                            for q4 in range(4):
                                pb = 2 + q4 % 2
                                for i4 in range(4):
                                    hp = q4 * 4 + i4
                                    mm(PS[pb][:, i4 * 128:(i4 + 1) * 128], qT[:, hp, j * 128:(j + 1) * 128], skt[:, hp, :], True, True,
                                       ['xs', 'skt'], [f'ps{pb}'])
                                cp('act', Ssb[:, q4 * 4:(q4 + 1) * 4, :], PS[pb][:, :].rearrange("p (a b) -> p a b", a=4), [f'ps{pb}'], ['Ssb'])
                            for hp in range(16):
                                s.op('dve', nc.vector.max, out=T16[:, hp, 0:8], in_=Ssb[:, hp, :], reads=['Ssb'], writes=['T16'])
                                s.op('dve', nc.vector.match_replace, out=Stmp[:], in_to_replace=T16[:, hp, 0:8], in_values=Ssb[:, hp, :],
                                     imm_value=-1e30, reads=['T16', 'Ssb'], writes=['Stmp'])
                                s.op('dve', nc.vector.max, out=T16[:, hp, 8:16], in_=Stmp[:], reads=['Stmp'], writes=['T16'])
                            tt('dve', Ssb[:], Ssb[:], T16[:, :, 0:1].to_broadcast([128, 16, 128]), ALU.subtract, ['Ssb', 'T16'], ['Ssb'])
                            act(Eall[:, j, :, :], Ssb[:], AF.Exp, ['Ssb'], ['Eall'])
                            tt('dve', ET16[:], T16[:], T16[:, :, 0:1].to_broadcast([128, 16, 16]), ALU.subtract, ['T16'], ['ET16'])
                            act(ET16[:], ET16[:], AF.Exp, ['ET16'], ['ET16'])
                            e4 = ET16[:].rearrange("p (h two) k -> p h two k", two=2)
                            for h in range(8):
                                tt('pool', cand[:, h, :].rearrange("p (a b) -> p a b", a=16),
                                   e4[:, h, 0, :].unsqueeze(2).to_broadcast([128, 16, 16]),
                                   e4[:, h, 1, :].unsqueeze(1).to_broadcast([128, 16, 16]), ALU.mult, ['ET16'], ['cand'])
                                s.op('dve', nc.vector.max, out=CT16[:, h, 0:8], in_=cand[:, h, :], reads=['cand'], writes=['CT16'])
                                s.op('dve', nc.vector.match_replace, out=ctmp[:], in_to_replace=CT16[:, h, 0:8], in_values=cand[:, h, :],
                                     imm_value=-1e30, reads=['CT16', 'cand'], writes=['ctmp'])
                                s.op('dve', nc.vector.max, out=CT16[:, h, 8:16], in_=ctmp[:], reads=['ctmp'], writes=['CT16'])
                            cp('dve', theta[:, j, :], CT16[:, :, 15], ['CT16'], ['theta'])
                            s.op('dve', nc.vector.tensor_reduce, out=zz[:], in_=CT16[:], axis=AX.X, op=ALU.add, reads=['CT16'], writes=['zz'])
                            s.op('dve', nc.vector.reciprocal, out=zz[:], in_=zz[:], reads=['zz'], writes=['zz'])
                            for h in range(8):
                                ts('dve', Dm[:, j, h, :], ident_f, zz[:, h:h + 1], ALU.mult, ['cst_f', 'zz'], ['Dm'])
                        for it in range(NG + 2):
                            g = it
                            if g < NG:
                                ui = gcount % 2
                                vi = gcount % 3
                                gi = gcount % 2
                                gcount += 1
                                slot[g] = (ui, vi, gi)
                                dma(UT[ui][:], ubf_v[g], ['bg:ubf'], [f'UT{ui}'])
                                dma(Vg[vi][:], vbf_v[g], ['bg:vbf'], [f'Vg{vi}'])
                                for j in range(ntl):
                                    for h2 in range(4):
                                        pi_ = pcount[0] % 4; pcount[0] += 1
                                        tt('pool', Pp[pi_][:],
                                           Eall[:, j, 4 * h2:4 * h2 + 3:2, g * GI:(g + 1) * GI].unsqueeze(3).to_broadcast([128, 2, GI, 128]),
                                           Eall[:, j, 4 * h2 + 1:4 * h2 + 4:2, :].unsqueeze(2).to_broadcast([128, 2, GI, 128]),
                                           ALU.mult, ['Eall'], [f'Pp{pi_}'])
                                        on_pool = False
                                        for hh in range(2):
                                            h = 2 * h2 + hh
                                            pv = Pp[pi_][:, hh, :, :].rearrange("p a b -> p (a b)")
                                            if on_pool:
                                                mi = mcount[0] % 2; mcount[0] += 1
                                                tt('pool', mk[mi][:], pv, theta[:, j, h:h + 1].to_broadcast([128, GI * 128]), ALU.is_ge,
                                                   [f'Pp{pi_}', 'theta'], [f'mk{mi}'])
                                                tt('pool', Gm[gi][:, j, h, :], mk[mi][:], pv, ALU.mult, [f'mk{mi}', f'Pp{pi_}'], [f'Gm{gi}_{j}_{h}'])
                                            else:
                                                stt(Gm[gi][:, j, h, :], pv, theta[:, j, h:h + 1], pv, ALU.is_ge, ALU.mult,
                                                    [f'Pp{pi_}', 'theta'], [f'Gm{gi}_{j}_{h}'])
                                for c in range(GI):
                                    ab = c % 2
                                    for k in range(KC):
                                        mm(PS[ab][:, 0:nt], UT[ui][:, k, c * 128:(c + 1) * 128], hx2[:, k, 0:nt], k == 0, k == KC - 1,
                                           [f'UT{ui}', 'hx2'], [f'ps{ab}'])
                                    act(a_sb[gi][:, c, 0:nt], PS[ab][:, 0:nt], AF.Gelu, [f'ps{ab}'], [f'a_sb{gi}_{c}'])
                            g1 = it - 1
                            if 0 <= g1 < NG:
                                ui, vi, gi = slot[g1]
                                for c in range(GI):
                                    gb = 2 + c % 2
                                    for j in range(ntl):
                                        for h in range(8):
                                            mm(PS[gb][:, j * 128:(j + 1) * 128], Gm[gi][:, j, h, c * 128:(c + 1) * 128], Dm[:, j, h, :], h == 0, h == 7,
                                               [f'Gm{gi}_{j}_{h}', 'Dm'], [f'ps{gb}'])
                                    tt('dve', GaT[gi][:, c, 0:nt], PS[gb][:, 0:nt], a_sb[gi][:, c, 0:nt], ALU.mult,
                                       [f'ps{gb}', f'a_sb{gi}_{c}'], [f'GaT{gi}'])
                            g2 = it - 2
                            if 0 <= g2 < NG:
                                ui, vi, gi = slot[g2]
                                for dc in range(KC):
                                    ob = 4 + dc % 2
                                    for c in range(GI):
                                        mm(PS[ob][:, 0:nt], Vg[vi][:, c, dc * 128:(dc + 1) * 128], GaT[gi][:, c, 0:nt], c == 0, c == GI - 1,
                                           [f'Vg{vi}', f'GaT{gi}'], [f'ps{ob}'])
                                    if g2 == 0:
                                        cp('act', accp[:, dc, 0:nt], PS[ob][:, 0:nt], [f'ps{ob}'], [f'accp{dc}'])
                                    elif dc < 6:
                                        fi = fcount[0] % 3; fcount[0] += 1
                                        cp('act', ftmp[fi][:, 0:nt], PS[ob][:, 0:nt], [f'ps{ob}'], [f'ftmp{fi}'])
                                        tt('pool', accp[:, dc, 0:nt], ftmp[fi][:, 0:nt], accp[:, dc, 0:nt], ALU.add, [f'ftmp{fi}', f'accp{dc}'], [f'accp{dc}'])
                                    else:
                                        tt('dve', accp[:, dc, 0:nt], PS[ob][:, 0:nt], accp[:, dc, 0:nt], ALU.add, [f'ps{ob}', f'accp{dc}'], [f'accp{dc}'])
                        dma(xs[:, :, 0:nt], xT_v[:, :, t0:t0 + nt], ['xT_d'], ['xs'])
                        for dc in range(KC):
                            stt(xs[:, dc, 0:nt], accp[:, dc, 0:nt], modT[:, 40 + dc, isc:isc + 1], xs[:, dc, 0:nt], ALU.mult, ALU.add,
                                [f'accp{dc}', 'modT', 'xs'], ['xs'])
                        dma(xT_v[:, :, t0:t0 + nt], xs[:, :, 0:nt], ['xs'], ['xT_d'], q='st', eng='pool')
                s.barrier()

        with _hp() as P:
            xs = SB("xsf", [128, KC, 512], F32, P)
            sq = [SB(f"sqf{i}", [128, 512], F32, P) for i in range(2)]
            rstd = SB("rstdf", [128, 512], F32, P)
            yT = SB("yT", [128, KC, 512], F32, P)
            yo = [SB(f"yo{i}", [128, D], F32, P) for i in range(2)]
            oc = 0
            for bi, blk in enumerate(blocks):
                t0, nt, isc = blk
                if isc or t0 >= TL // 2:
                    continue
                dma(xs[:, :, 0:nt], xT_v[:, :, t0:t0 + nt], ['xT_d'], ['xsf'])
                for k in range(KC):
                    a = k % 2
                    act(sq[a][:, 0:nt], xs[:, k, 0:nt], AF.Square, ['xsf'], [f'sqf{a}'])
                    mm(PS[6][:, 0:nt], ones_f, sq[a][:, 0:nt], k == 0, k == KC - 1, [f'sqf{a}', 'cst_f'], ['ps6'])
                rstd_from_ps(PS[6][:, 0:nt], rstd[:, 0:nt], D, ['ps6', 'epsc'], ['rstdf'])
                for k in range(KC):
                    stt(yT[:, k, 0:nt], xs[:, k, 0:nt], gfin[:, k:k + 1], rstd[:, 0:nt], ALU.mult, ALU.mult,
                        ['xsf', 'gfin', 'rstdf'], ['yT'])
                for j in range(nt // 128):
                    a = oc % 2; oc += 1
                    for hk in range(2):
                        pb = 2 * a + hk
                        for kk in range(4):
                            k = hk * 4 + kk
                            s.op('pe', nc.tensor.transpose, PS[pb][:, kk * 128:(kk + 1) * 128],
                                 yT[:, k, j * 128:(j + 1) * 128], ident_f, reads=['yT', 'cst_f'], writes=[f'ps{pb}'])
                        cp('dve' if hk == 0 else 'act', yo[a][:, hk * 512:(hk + 1) * 512], PS[pb][:, :], [f'ps{pb}'], [f'yo{a}'])
                    dma(yout[t0 + j * 128:t0 + (j + 1) * 128, :], yo[a][:], [f'yo{a}'], ['yout'], q='st', eng='pool')
        s.barrier(['sp', 'pool'])
    return nc, s


def _rope_tables(TL, rot_dim):
    rows = TL // GRID_W
    r = np.repeat(np.arange(rows, dtype=np.float32), GRID_W)
    col = np.tile(np.arange(GRID_W, dtype=np.float32), rows)
    n_freq = rot_dim // 4
    inv = (np.float32(10000.0) ** (-np.arange(n_freq, dtype=np.float32) / np.float32(n_freq))).astype(np.float32)
    ang = np.concatenate([r[:, None] * inv, col[:, None] * inv], axis=-1).astype(np.float32)
    cos = np.cos(ang).astype(np.float32)
    sin = np.sin(ang).astype(np.float32)
    half = rot_dim // 2
    T = TL + CTX
    tab = np.zeros((2, rot_dim, T), np.float32)
    tab[0, :, TL:] = 1.0
    tab[0, :half, :TL] = cos.T
    tab[0, half:, :TL] = cos.T
    tab[1, :half, :TL] = -sin.T
    tab[1, half:, :TL] = sin.T
    return tab


def _swap_heads(w, hd):
    n = w.shape[-1] // hd
    w4 = w.reshape(w.shape[:-1] + (n, 2, hd // 2))
    return np.ascontiguousarray(w4[..., ::-1, :]).reshape(w.shape)


def _consts(rank):
    c = np.zeros((128, 8, 128), np.float32)
    c[:, 0, :] = np.eye(128, dtype=np.float32)
    c[:, 1, :] = 1.0
    jj = np.arange(128)[:, None]
    ii = np.arange(128)[None, :]
    c[:, 2, :] = np.where(jj < ii, NEG, 0.0)
    c[:, 3, :] = np.where(jj > ii, NEG, 0.0)
    c[:, 4, :] = c[:, 2, :] if rank == 1 else NEG
    c[:, 5, :] = c[:, 3, :] if rank == 1 else NEG
    c[:, 6, :] = c[:, 2, :] if rank == 0 else NEG
    c[:, 7, :] = c[:, 3, :] if rank == 0 else NEG
    return c


def prep_shared(inp, TL, L):
    f = lambda a: np.ascontiguousarray(np.asarray(a, dtype=np.float32))
    w_in = f(inp['w_in'])[:L]
    sh = {}
    sh['w_mod'] = f(inp['w_mod'])[:L]
    sh['b_modT'] = f(f(inp['b_mod'])[:L].reshape(L, 48, 128).transpose(0, 2, 1))
    gvec = np.zeros((L, 128, 21), np.float32)
    gvec[:, :, 0:8] = f(inp['g_attn'])[:L].reshape(L, 8, 128).transpose(0, 2, 1)
    gvec[:, :, 8:16] = f(inp['g_ffn'])[:L].reshape(L, 8, 128).transpose(0, 2, 1)
    gvec[:, :, 16:19] = f(inp['g_cq'])[:L].reshape(L, 3, 128).transpose(0, 2, 1)
    gvec[:, :, 19:21] = f(inp['g_ckv'])[:L].reshape(L, 2, 128).transpose(0, 2, 1)
    sh['gvec'] = gvec
    gqk = np.zeros((L, 64, 4), np.float32)
    gqn = f(inp['g_qn'])[:L]; gkn = f(inp['g_kn'])[:L]
    gqk[:, :, 0] = gqn; gqk[:, :, 1] = _swap_heads(gqn, 64)
    gqk[:, :, 2] = gkn; gqk[:, :, 3] = _swap_heads(gkn, 64)
    sh['gqk'] = gqk
    sh['g_finalT'] = f(f(inp['g_final']).reshape(8, 128).T)
    sh['sinkb'] = f(np.broadcast_to(f(inp['sink'])[:L].reshape(1, L * 8), (128, L * 8)))
    sh['w_in'] = w_in
    kr_sw = _swap_heads(w_in[:, :, 640:672], 32)
    b_sw = _swap_heads(w_in[:, :, 672:672 + 640], 64)
    w_sw = _swap_heads(w_in[:, :, 1440:1440 + 640], 64)
    sh['w_in_sw'] = f(np.concatenate([kr_sw, b_sw, w_sw], axis=-1))
    w_uq = f(inp['w_uq'])[:L]
    sh['w_uq'] = w_uq
    rope_cols = w_uq.reshape(L, 384, 8, 96)[:, :, :, 64:96]
    sh['w_uq_sw'] = f(_swap_heads(f(rope_cols).reshape(L, 384, 256), 32))
    sh['w_ukv'] = f(inp['w_ukv'])[:L]
    for k in ['w_oa', 'w_ob', 'w_ow', 'w_o', 'w_pq']:
        sh[k] = f(inp[k])[:L]
    GI = 2
    NG = 128 // GI
    pu = f(inp['peer_u'])[:L].reshape(L, NG, GI * 128, 8, 128)
    sh['peer_uG'] = f(pu.transpose(0, 1, 4, 3, 2)).reshape(L, NG * 128, 8 * GI * 128)
    pv = f(inp['peer_v'])[:L].reshape(L, NG, GI, 128, 1024)
    sh['peer_vG'] = f(pv.transpose(0, 1, 3, 2, 4)).reshape(L, NG * 128, GI * 1024)
    sk = f(inp['sub_keys'])[:L].reshape(L, 16, 128, 128)
    sh['skT'] = f(sk.transpose(0, 3, 1, 2))
    sh['rope64'] = _rope_tables(TL, 64)
    sh['rope32'] = _rope_tables(TL, 32)
    return sh


def _roll_lat(a, rank, TL, axis):
    if rank == 0:
        return np.ascontiguousarray(a)
    idx = np.concatenate([np.arange(TL // 2, TL), np.arange(0, TL // 2), np.arange(TL, a.shape[axis])])
    return np.ascontiguousarray(np.take(a, idx, axis=axis))


def prep_core(inp, sh, b, rank=0):
    d = dict(sh)
    TL = inp['x'].shape[1]
    xin = np.concatenate([np.asarray(inp['x'][b], np.float32), np.asarray(inp['ctx'][b], np.float32)], axis=0)
    d['xin'] = _roll_lat(xin, rank, TL, 0)
    d['rope64'] = _roll_lat(sh['rope64'], rank, TL, 2)
    d['rope32'] = _roll_lat(sh['rope32'], rank, TL, 2)
    d['consts'] = _consts(rank)
    cT2 = np.zeros((128, 8, 2), np.float32)
    cT2[:, :, 0] = np.asarray(inp['c'][b], np.float32).reshape(8, 128).T
    cT2[:, :, 1] = np.asarray(inp['c_ctx'], np.float32).reshape(8, 128).T
    d['cT2'] = cT2
    return d


_CACHE = {}


def kernel(**inputs):
    B, TL, _ = inputs['x'].shape
    L = inputs['w_mod'].shape[0]
    key = (TL, L)
    if key not in _CACHE:
        _CACHE[key] = build(TL, L)
    nc, _ = _CACHE[key]
    sh = prep_shared(inputs, TL, L)
    in_maps = [prep_core(inputs, sh, c // 2, c % 2) for c in range(8)]
    res = run_bass_kernel_spmd(nc, in_maps, core_ids=list(range(8)))
    out = np.concatenate([np.asarray(res.results[c]['yout'], dtype=np.float32) for c in range(2 * B)], axis=0)
    return out.reshape(B, TL, D)
```

```python
import contextlib
import os
import numpy as np
import ml_dtypes
import concourse.bass as bass
import concourse.mybir as mybir
from concourse.bass_utils import run_bass_kernel_spmd

F32 = mybir.dt.float32
BF16 = mybir.dt.bfloat16
AF = mybir.ActivationFunctionType
ALU = mybir.AluOpType
AX = mybir.AxisListType

D = 1024
KC = 8
CTX = 256
GRID_W = 64
EPS = 1e-6
NEG = -30000.0


class Sched:
    LIMIT = int(os.environ.get("SEM_LIMIT", "30000"))

    def __init__(self, nc):
        self.nc = nc
        self.engs = {'pe': nc.tensor, 'act': nc.scalar, 'dve': nc.vector, 'pool': nc.gpsimd, 'sp': nc.sync}
        self.sem = {}
        self.cnt = {}
        self.ver = {}
        self.allsems = []
        self.known = {e: {} for e in self.engs}
        self.last_w = {}
        self.readers = {}
        self.n_inst = 0
        self.n_wait = 0
        self.neng = {e: 0 for e in self.engs}
        self.dcount = {}
        self.DMA_RING = 8
        self.marks = []

    def _bump(self, name, step):
        if name not in self.sem or self.cnt[name] + step > self.LIMIT:
            self.ver[name] = self.ver.get(name, -1) + 1
            h = self.nc.alloc_semaphore(name=f"s_{name}_{self.ver[name]}")
            self.sem[name] = h
            self.cnt[name] = 0
            self.allsems.append([h, 0, name])
            self._cur = None
        self.cnt[name] += step
        for rec in self.allsems:
            if rec[0] is self.sem[name]:
                rec[1] = self.cnt[name]
        return self.sem[name], self.cnt[name]

    def op(self, eng, fn, *args, reads=(), writes=(), dma=None, **kw):
        e = self.engs[eng]
        deps = {}

        def add(d):
            if d is None:
                return
            k = id(d[0])
            if k not in deps or deps[k][1] < d[1]:
                deps[k] = d
        for r in reads:
            add(self.last_w.get(r))
        for w in writes:
            add(self.last_w.get(w))
            for d in self.readers.get(w, {}).values():
                add(d)
        kn = self.known[eng]
        for k, (h, v, own) in deps.items():
            if own == 'pe' and eng == 'pe' and dma is None:
                continue
            if kn.get(k, 0) >= v:
                continue
            e.wait_ge(h, v)
            self.n_wait += 1
            kn[k] = v
        if dma is not None:
            R = self.DMA_RING
            i = self.dcount.get(dma, 0)
            self.dcount[dma] = i + 1
            sname = f"{dma}#{i % R}"
            if sname in self.sem and self.cnt[sname] > 0:
                hp_, vp_ = self.sem[sname], self.cnt[sname]
                if kn.get(id(hp_), 0) < vp_:
                    e.wait_ge(hp_, vp_)
                    self.n_wait += 1
                    kn[id(hp_)] = vp_
        inst = fn(*args, **kw)
        if dma is None:
            self.neng[eng] += 1
        if dma is not None:
            h, v = self._bump(sname, 16)
            inst.then_inc(h, 16)
            rec = (h, v, 'dma:' + sname)
        else:
            h, v = self._bump(eng, 1)
            inst.then_inc(h, 1)
            rec = (h, v, eng)
        for w in writes:
            self.last_w[w] = rec
            self.readers[w] = {}
        for r in reads:
            self.readers.setdefault(r, {})[rec[2]] = rec
        self.n_inst += 1
        return inst

    def barrier(self, engs=None, label=""):
        self.marks.append((label, dict(self.neng)))
        for eng in (engs or self.engs):
            e = self.engs[eng]
            kn = self.known[eng]
            for h, v, nm in self.allsems:
                if nm.startswith('bg'):
                    continue
                if v > 0 and kn.get(id(h), 0) < v:
                    e.wait_ge(h, v)
                    kn[id(h)] = v
                    self.n_wait += 1
        if engs is None:
            self.last_w = {k: v for k, v in self.last_w.items() if str(k).startswith('bg:')}
            self.readers = {k: v for k, v in self.readers.items() if str(k).startswith('bg:')}


def _hp():
    return contextlib.ExitStack()


def build(TL, L, do_attn=True, do_peer=True, ctx_update=True):
    T = TL + CTX
    NT = T // 128
    NTL = TL // 128
    blocks = [(i * 512, 512, 0) for i in range(TL // 512)] + [(TL, CTX, 1)]
    nc = bass.Bass("TRN2", target_bir_lowering=False)
    s = Sched(nc)

    def din(name, shape, dt=F32):
        return nc.dram_tensor(name, list(shape), dt, kind="ExternalInput").ap()

    def dscr(name, shape, dt):
        return nc.dram_tensor(name, list(shape), dt).ap()

    xin = din("xin", [T, D])
    cT2 = din("cT2", [128, KC, 2])
    w_mod = din("w_mod", [L, D, 6 * D])
    b_modT = din("b_modT", [L, 128, 48])
    gvec = din("gvec", [L, 128, 21])
    gqk = din("gqk", [L, 64, 4])
    g_finalT = din("g_finalT", [128, KC])
    sinkb = din("sinkb", [128, L * 8])
    w_in = din("w_in", [L, D, 5280])
    w_in_sw = din("w_in_sw", [L, D, 32 + 640 + 640])
    w_uq = din("w_uq", [L, 384, 768])
    w_uq_sw = din("w_uq_sw", [L, 384, 256])
    w_ukv = din("w_ukv", [L, 256, 1024])
    w_oa = din("w_oa", [L, 512, D])
    w_ob = din("w_ob", [L, 512, D])
    w_ow = din("w_ow", [L, 512, D])
    w_o = din("w_o", [L, D, D])
    w_pq = din("w_pq", [L, D, 2048])
    skT = din("skT", [L, 128, 16, 128])
    GI = 2
    NG = 128 // GI
    peer_uG = din("peer_uG", [L, NG * 128, KC * GI * 128])
    peer_vG = din("peer_vG", [L, NG * 128, GI * D])
    ubf_d = dscr("ubf_d", [NG * 128, KC * GI * 128], BF16)
    vbf_d = dscr("vbf_d", [NG * 128, GI * D], BF16)
    rope64 = din("rope64", [2, 64, T])
    rope32 = din("rope32", [2, 32, T])
    consts = din("consts", [128, 8, 128])
    yout = nc.dram_tensor("yout", [TL // 2, D], F32, kind="ExternalOutput").ap()

    xT_d = dscr("xT_d", [D, T], F32)
    hxT_d = dscr("hxT_d", [D, T], BF16)
    qnT_d = dscr("qnT_d", [8, 64, T], BF16)
    qrT_d = dscr("qrT_d", [8, 32, T], BF16)
    knT_d = dscr("knT_d", [8, 64, T], BF16)
    krT_d = dscr("krT_d", [32, T], BF16)
    vA_d = dscr("vA_d", [T, 8, 65], BF16)
    qbT_d = dscr("qbT_d", [8, 64, T], BF16)
    kbT_d = dscr("kbT_d", [2, 64, T], BF16)
    vB_d = dscr("vB_d", [T, 2, 65], BF16)
    qwT_d = dscr("qwT_d", [8, 64, T], BF16)
    kwT_d = dscr("kwT_d", [2, 64, T], BF16)
    vW_d = dscr("vW_d", [T, 2, 65], BF16)
    gT_d = dscr("gT_d", [3 * D, T], BF16)
    oT_d = [dscr(f"oT_d{m}", [512, T], BF16) for m in range(3)]

    xT_v = xT_d.rearrange("(k p) t -> p k t", p=128)
    hxT_v = hxT_d.rearrange("(k p) t -> p k t", p=128)

    with contextlib.ExitStack() as G:
        uid = [0]

        def SB(name, shape, dt, st=G):
            uid[0] += 1
            return st.enter_context(nc.sbuf_tensor(f"{name}_u{uid[0]}", list(shape), dt))

        def PSt(name, shape, dt, st=G):
            return st.enter_context(nc.psum_tensor(name, list(shape), dt))

        PS = [PSt(f"ps{i}", [128, 512], F32) for i in range(7)]
        PSB = PSt("psb", [128, 1024], BF16)

        def dma(out, in_, reads, writes, q='ld', eng='sp'):
            fn = nc.sync.dma_start if eng == 'sp' else nc.gpsimd.dma_start
            return s.op(eng, fn, out=out, in_=in_, reads=reads, writes=writes, dma=q + eng)

        def mm(out, lhsT, rhs, start, stop, reads, writes):
            return s.op('pe', nc.tensor.matmul, out, lhsT=lhsT, rhs=rhs, start=start, stop=stop,
                        reads=reads, writes=writes)

        def act(out, in_, func, reads, writes, **kw):
            return s.op('act', nc.scalar.activation, out=out, in_=in_, func=func, reads=reads, writes=writes, **kw)

        def tt(eng, out, in0, in1, op, reads, writes):
            fn = nc.vector.tensor_tensor if eng == 'dve' else nc.gpsimd.tensor_tensor
            return s.op(eng, fn, out=out, in0=in0, in1=in1, op=op, reads=reads, writes=writes)

        def ts(eng, out, in0, s1, op0, reads, writes, s2=None, op1=None):
            fn = nc.vector.tensor_scalar if eng == 'dve' else nc.gpsimd.tensor_scalar
            kw = {}
            if op1 is not None:
                kw['op1'] = op1
            return s.op(eng, fn, out=out, in0=in0, scalar1=s1, scalar2=s2, op0=op0, reads=reads, writes=writes, **kw)

        def stt(out, in0, scalar, in1, op0, op1, reads, writes):
            return s.op('dve', nc.vector.scalar_tensor_tensor, out=out, in0=in0, scalar=scalar, in1=in1,
                        op0=op0, op1=op1, reads=reads, writes=writes)

        def cp(eng, out, in_, reads, writes):
            if eng == 'act':
                return s.op('act', nc.scalar.copy, out=out, in_=in_, reads=reads, writes=writes)
            fn = nc.vector.tensor_copy if eng == 'dve' else nc.gpsimd.tensor_copy
            return s.op(eng, fn, out=out, in_=in_, reads=reads, writes=writes)

        cst_f = SB("cst_f", [128, 8, 128], F32)
        cst_b = SB("cst_b", [128, 8, 128], BF16)
        dma(cst_f[:], consts, ['consts'], ['cst_f'])
        cp('dve', cst_b[:], cst_f[:], ['cst_f'], ['cst_b'])
        ident_f = cst_f[:, 0, :]
        ones_f = cst_f[:, 1, :]
        ident_b = cst_b[:, 0, :]
        nm_lo = cst_b[:, 2, :]
        nm_hi = cst_b[:, 3, :]
        seam_wrapL, seam_wrapR, seam_midL, seam_midR = (cst_b[:, 4 + i, :] for i in range(4))
        epsc = SB("epsc", [128, 1], F32)
        s.op('dve', nc.vector.memset, epsc[:], EPS, writes=['epsc'])
        scT = SB("scT", [128, KC, 2], F32)
        dma(scT[:], cT2, ['cT2'], ['scT'])
        act(scT[:], scT[:], AF.Silu, ['scT'], ['scT'])
        esink = SB("esink", [128, L * 8], F32)
        dma(esink[:], sinkb, ['sinkb'], ['esink'])
        act(esink[:], esink[:], AF.Exp, ['esink'], ['esink'])
        gfin = SB("gfin", [128, KC], F32)
        dma(gfin[:], g_finalT, ['gfin_d'], ['gfin'])
        modT = SB("modT", [128, 48, 2], F32)
        gsc = SB("gsc", [128, 2, KC, 2], F32)
        gv = SB("gv", [128, 21], F32)
        gq = SB("gq", [64, 4], F32)
        bmod = SB("bmod", [128, 48], F32)

        def rstd_from_ps(ps_ap, out_ap, n, rkeys, wkeys, np_=128):
            act(out_ap, ps_ap, AF.Sqrt, rkeys, wkeys, scale=1.0 / n, bias=epsc[0:np_, :])
            s.op('dve', nc.vector.reciprocal, out=out_ap, in_=out_ap, reads=wkeys, writes=wkeys)

        with _hp() as P:
            xin_t = [SB(f"xin_t{i}", [128, D], F32, P) for i in range(2)]
            xo_t = [SB(f"xo_t{i}", [128, KC, 128], F32, P) for i in range(2)]
            for j in range(NT):
                a = j % 2
                dma(xin_t[a][:], xin[j * 128:(j + 1) * 128, :], ['xin'], [f'xin_t{a}'])
                for hk in range(2):
                    pst = PS[2 * a + hk]
                    for kk in range(4):
                        k = hk * 4 + kk
                        s.op('pe', nc.tensor.transpose, pst[:, kk * 128:(kk + 1) * 128],
                             xin_t[a][:, k * 128:(k + 1) * 128], ident_f,
                             reads=[f'xin_t{a}', 'cst_f'], writes=[f'ps{2 * a + hk}'])
                    cp('dve' if hk == 0 else 'act', xo_t[a][:, hk * 4:(hk + 1) * 4, :],
                       pst[:].rearrange("p (k t) -> p k t", k=4), [f'ps{2 * a + hk}'], [f'xo_t{a}'])
                dma(xT_v[:, :, j * 128:(j + 1) * 128], xo_t[a][:], [f'xo_t{a}'], ['xT_d'], q='st', eng='pool')
        s.barrier()

        def load_x_and_norm(P_tiles, blk, sub, hx_out, hx_key):
            t0, nt, isc = blk
            xs, sq, rstd, tmpf = P_tiles['xs'], P_tiles['sq'], P_tiles['rstd'], P_tiles['tmpf']
            dma(xs[:, :, 0:nt], xT_v[:, :, t0:t0 + nt], ['xT_d'], ['xs'])
            for k in range(KC):
                a = k % 2
                act(sq[a][:, 0:nt], xs[:, k, 0:nt], AF.Square, ['xs'], [f'sq{a}'])
                mm(PS[6][:, 0:nt], ones_f, sq[a][:, 0:nt], k == 0, k == KC - 1, [f'sq{a}', 'cst_f'], ['ps6'])
            rstd_from_ps(PS[6][:, 0:nt], rstd[:, 0:nt], D, ['ps6', 'epsc'], ['rstd'])
            shoff = 0 if sub == 0 else 24
            for k in range(KC):
                a = k % 2
                stt(tmpf[a][:, 0:nt], xs[:, k, 0:nt], gsc[:, sub, k, isc:isc + 1], rstd[:, 0:nt], ALU.mult, ALU.mult,
                    ['xs', 'gsc', 'rstd'], [f'tmpf{a}'])
                act(hx_out[:, k, 0:nt], tmpf[a][:, 0:nt], AF.Identity, [f'tmpf{a}', 'modT'], [hx_key],
                    bias=modT[:, shoff + k, isc:isc + 1])

        def norm_tiles(P):
            return {'xs': SB("xs", [128, KC, 512], F32, P),
                    'sq': [SB(f"sq{i}", [128, 512], F32, P) for i in range(2)],
                    'rstd': SB("rstd", [128, 512], F32, P),
                    'tmpf': [SB(f"tmpf{i}", [128, 512], F32, P) for i in range(2)]}

        def load_w_bf16(dst, src, key, srckey):
            dma(dst, src, [srckey], [key], q='w', eng='pool')

        for l in range(L):
            last = (l == L - 1)
            with _hp() as P:
                wm = [SB(f"wm{i}", [128, KC, 512], F32, P) for i in range(2)]
                if do_peer:
                    nrow = NG * 128
                    for q8 in range(8):
                        r0, r1 = q8 * nrow // 8, (q8 + 1) * nrow // 8
                        dma(ubf_d[r0:r1, :], peer_uG[l][r0:r1, :], ['peer_uG', 'bg:ld'], ['bg:ubf'], q='bg', eng='pool')
                        dma(vbf_d[r0:r1, :], peer_vG[l][r0:r1, :], ['peer_vG', 'bg:ld'], ['bg:vbf'], q='bg', eng='pool')
                dma(bmod[:], b_modT[l], ['b_modT'], ['bmod'])
                dma(gv[:], gvec[l], ['gvec'], ['gv'])
                dma(gq[:], gqk[l], ['gqk'], ['gq'])
                for nb in range(12):
                    a = nb % 2
                    dma(wm[a][:], w_mod[l].rearrange("(k p) n -> p k n", p=128)[:, :, nb * 512:(nb + 1) * 512],
                        ['w_mod'], [f'wm{a}'])
                    for c in range(4):
                        n = nb * 4 + c
                        for k in range(KC):
                            mm(PS[a][:, c * 2:c * 2 + 2], wm[a][:, k, c * 128:(c + 1) * 128], scT[:, k, :],
                               k == 0, k == KC - 1, [f'wm{a}', 'scT'], [f'ps{a}'])
                        ts('dve', modT[:, n, :], PS[a][:, c * 2:c * 2 + 2], bmod[:, n:n + 1], ALU.add,
                           [f'ps{a}', 'bmod'], ['modT'])
                for sub in range(2):
                    scoff = 8 if sub == 0 else 32
                    goff = 0 if sub == 0 else 8
                    for v in range(2):
                        ts('dve', gsc[:, sub, :, v], modT[:, scoff:scoff + 8, v], 1.0, ALU.add,
                           ['modT'], ['gsc'])
                        tt('dve', gsc[:, sub, :, v], gsc[:, sub, :, v], gv[:, goff:goff + 8], ALU.mult,
                           ['gsc', 'gv'], ['gsc'])
            s.barrier()

            if do_attn:
                with _hp() as P:
                    NTl = norm_tiles(P)
                    hxs = [SB(f"hxs{i}", [128, KC, 512], BF16, P) for i in range(2)]
                    for bi, blk in enumerate(blocks):
                        t0, nt, isc = blk
                        a = bi % 2
                        load_x_and_norm(NTl, blk, 0, hxs[a], f'hxs{a}')
                        dma(hxT_v[:, :, t0:t0 + nt], hxs[a][:, :, 0:nt], [f'hxs{a}'], ['hxT_d'], q='st', eng='pool')
                s.barrier()

                with _hp() as P:
                    wA = SB("wA", [128, KC, 704], BF16, P)
                    wuq = SB("wuq", [128, 3, 1024], BF16, P)
                    wukv = SB("wukv", [128, 2, 1024], BF16, P)
                    wiv = w_in[l].rearrange("(k p) n -> p k n", p=128)
                    load_w_bf16(wA[:, :, 0:672], wiv[:, :, 0:672], 'wA', 'w_in')
                    load_w_bf16(wA[:, :, 672:704], w_in_sw[l].rearrange("(k p) n -> p k n", p=128)[:, :, 0:32], 'wA', 'w_in_sw')
                    load_w_bf16(wuq[:, :, 0:768], w_uq[l].rearrange("(k p) n -> p k n", p=128), 'wuq', 'w_uq')
                    load_w_bf16(wuq[:, :, 768:1024], w_uq_sw[l].rearrange("(k p) n -> p k n", p=128), 'wuq', 'w_uq_sw')
                    load_w_bf16(wukv[:], w_ukv[l].rearrange("(k p) n -> p k n", p=128), 'wukv', 'w_ukv')
                    hxb = [SB(f"hxb{i}", [128, KC, 512], BF16, P) for i in range(2)]
                    r32 = [SB(f"r32_{i}", [32, 2, 512], F32, P) for i in range(2)]
                    cqf = SB("cqf", [128, 5, 512], F32, P)
                    cqn = SB("cqn", [128, 5, 512], BF16, P)
                    sqa = [SB(f"sqa{i}", [128, 512], F32, P) for i in range(2)]
                    rsa = SB("rsa", [128, 2, 512], F32, P)
                    t32 = [SB(f"t32_{i}", [32, 512], F32, P) for i in range(2)]
                    o32 = [SB(f"o32_{i}", [32, 512], BF16, P) for i in range(2)]
                    o64 = [SB(f"o64_{i}", [64, 512], BF16, P) for i in range(2)]
                    vst = [SB(f"vst{i}", [128, 8, 65], BF16, P) for i in range(2)]
                    for i in range(2):
                        s.op('pool', nc.gpsimd.memset, vst[i][:], 1.0, writes=[f'vst{i}'])
                    ev = 0
                    for bi, blk in enumerate(blocks):
                        t0, nt, isc = blk
                        a = bi % 2
                        dma(hxb[a][:, :, 0:nt], hxT_v[:, :, t0:t0 + nt], ['hxT_d'], [f'hxb{a}'])
                        dma(r32[a][:, :, 0:nt], rope32.rearrange("c p t -> p c t")[:, :, t0:t0 + nt], ['rope32'], [f'r32_{a}'])
                        for c in range(5):
                            pb = c % 2
                            for k in range(KC):
                                mm(PS[pb][:, 0:nt], wA[:, k, c * 128:(c + 1) * 128], hxb[a][:, k, 0:nt], k == 0, k == KC - 1,
                                   ['wA', f'hxb{a}'], [f'ps{pb}'])
                            cp('dve', cqf[:, c, 0:nt], PS[pb][:, 0:nt], [f'ps{pb}'], ['cqf'])
                        for grp, (c0, c1, n) in enumerate([(0, 3, 384), (3, 5, 256)]):
                            for c in range(c0, c1):
                                sa = c % 2
                                act(sqa[sa][:, 0:nt], cqf[:, c, 0:nt], AF.Square, ['cqf'], [f'sqa{sa}'])
                                mm(PS[2 + grp][:, 0:nt], ones_f, sqa[sa][:, 0:nt], c == c0, c == c1 - 1,
                                   [f'sqa{sa}', 'cst_f'], [f'ps{2 + grp}'])
                            rstd_from_ps(PS[2 + grp][:, 0:nt], rsa[:, grp, 0:nt], n, [f'ps{2 + grp}', 'epsc'], ['rsa'])
                            for c in range(c0, c1):
                                stt(cqn[:, c, 0:nt], cqf[:, c, 0:nt], gv[:, 16 + c:17 + c], rsa[:, grp, 0:nt], ALU.mult, ALU.mult,
                                    ['cqf', 'gv', 'rsa'], ['cqn'])
                        for k in range(KC):
                            mm(PS[4][0:32, 0:nt], wA[:, k, 640:672], hxb[a][:, k, 0:nt], k == 0, k == KC - 1, ['wA', f'hxb{a}'], ['ps4'])
                        for k in range(KC):
                            mm(PS[5][0:32, 0:nt], wA[:, k, 672:704], hxb[a][:, k, 0:nt], k == 0, k == KC - 1, ['wA', f'hxb{a}'], ['ps5'])
                        e = ev % 2; ev += 1
                        tt('dve', t32[0][:, 0:nt], PS[4][0:32, 0:nt], r32[a][:, 0, 0:nt], ALU.mult, ['ps4', f'r32_{a}'], ['t32_0'])
                        tt('dve', t32[1][:, 0:nt], PS[5][0:32, 0:nt], r32[a][:, 1, 0:nt], ALU.mult, ['ps5', f'r32_{a}'], ['t32_1'])
                        tt('pool', o32[e][:, 0:nt], t32[0][:, 0:nt], t32[1][:, 0:nt], ALU.add, ['t32_0', 't32_1'], [f'o32_{e}'])
                        dma(krT_d[:, t0:t0 + nt], o32[e][:, 0:nt], [f'o32_{e}'], ['krT_d'], q='st', eng='pool')
                        for h in range(8):
                            pb = h % 2
                            for c in range(3):
                                mm(PS[pb][0:64, 0:nt], wuq[:, c, h * 96:h * 96 + 64], cqn[:, c, 0:nt], c == 0, c == 2, ['wuq', 'cqn'], [f'ps{pb}'])
                            e = h % 2
                            cp('act', o64[e][:, 0:nt], PS[pb][0:64, 0:nt], [f'ps{pb}'], [f'o64_{e}'])
                            dma(qnT_d[h, :, t0:t0 + nt], o64[e][:, 0:nt], [f'o64_{e}'], ['qnT_d'], q='st', eng='pool')
                            for c in range(3):
                                mm(PS[4][0:32, 0:nt], wuq[:, c, h * 96 + 64:h * 96 + 96], cqn[:, c, 0:nt], c == 0, c == 2, ['wuq', 'cqn'], ['ps4'])
                            for c in range(3):
                                mm(PS[5][0:32, 0:nt], wuq[:, c, 768 + h * 32:768 + h * 32 + 32], cqn[:, c, 0:nt], c == 0, c == 2, ['wuq', 'cqn'], ['ps5'])
                            e = ev % 2; ev += 1
                            tt('dve', t32[0][:, 0:nt], PS[4][0:32, 0:nt], r32[a][:, 0, 0:nt], ALU.mult, ['ps4', f'r32_{a}'], ['t32_0'])
                            tt('dve', t32[1][:, 0:nt], PS[5][0:32, 0:nt], r32[a][:, 1, 0:nt], ALU.mult, ['ps5', f'r32_{a}'], ['t32_1'])
                            tt('pool', o32[e][:, 0:nt], t32[0][:, 0:nt], t32[1][:, 0:nt], ALU.add, ['t32_0', 't32_1'], [f'o32_{e}'])
                            dma(qrT_d[h, :, t0:t0 + nt], o32[e][:, 0:nt], [f'o32_{e}'], ['qrT_d'], q='st', eng='pool')
                            pb = 2 + h % 2
                            for c in range(2):
                                mm(PS[pb][0:64, 0:nt], wukv[:, c, h * 128:h * 128 + 64], cqn[:, 3 + c, 0:nt], c == 0, c == 1, ['wukv', 'cqn'], [f'ps{pb}'])
                            e2 = (h + 1) % 2
                            cp('act', o64[e2][:, 0:nt], PS[pb][0:64, 0:nt], [f'ps{pb}'], [f'o64_{e2}'])
                            dma(knT_d[h, :, t0:t0 + nt], o64[e2][:, 0:nt], [f'o64_{e2}'], ['knT_d'], q='st', eng='pool')
                        for j in range(nt // 128):
                            va = j % 2
                            for c in range(2):
                                mm(PS[6][:, :].rearrange("p (h d) -> p h d", h=8), cqn[:, 3 + c, j * 128:(j + 1) * 128],
                                   wukv[:, c, :].rearrange("p (h d) -> p h d", h=8)[:, :, 64:128], c == 0, c == 1, ['wukv', 'cqn'], ['ps6'])
                            cp('dve', vst[va][:, :, 0:64], PS[6][:, :].rearrange("p (h d) -> p h d", h=8), ['ps6'], [f'vst{va}'])
                            dma(vA_d[t0 + j * 128:t0 + (j + 1) * 128, :, :], vst[va][:], [f'vst{va}'], ['vA_d'], q='st', eng='pool')
                s.barrier()

                for mix in range(2):
                    with _hp() as P:
                        col0 = 672 if mix == 0 else 1440
                        sw0 = 32 if mix == 0 else 672
                        qT_dst, kT_dst, v_dst = (qbT_d, kbT_d, vB_d) if mix == 0 else (qwT_d, kwT_d, vW_d)
                        wq = SB("wq", [128, KC, 1408], BF16, P)
                        wiv = w_in[l].rearrange("(k p) n -> p k n", p=128)
                        load_w_bf16(wq[:, :, 0:768], wiv[:, :, col0:col0 + 768], 'wq', 'w_in')
                        load_w_bf16(wq[:, :, 768:1408], w_in_sw[l].rearrange("(k p) n -> p k n", p=128)[:, :, sw0:sw0 + 640], 'wq', 'w_in_sw')
                        hxb = [SB(f"hxb{i}", [128, KC, 512], BF16, P) for i in range(2)]
                        r64 = [SB(f"r64_{i}", [64, 2, 512], F32, P) for i in range(2)]
                        qf = [SB(f"qf{i}", [64, 512], F32, P) for i in range(2)]
                        sq6 = [SB(f"sq6_{i}", [64, 512], F32, P) for i in range(2)]
                        rs6 = [SB(f"rs6_{i}", [64, 512], F32, P) for i in range(2)]
                        t64 = [SB(f"t64_{i}", [64, 512], F32, P) for i in range(2)]
                        o64 = [SB(f"o64_{i}", [64, 512], BF16, P) for i in range(2)]
                        vst = [SB(f"vst{i}", [128, 2, 65], BF16, P) for i in range(2)]
                        for i in range(2):
                            s.op('pool', nc.gpsimd.memset, vst[i][:], 1.0, writes=[f'vst{i}'])
                        ev = 0
                        for bi, blk in enumerate(blocks):
                            t0, nt, isc = blk
                            a = bi % 2
                            dma(hxb[a][:, :, 0:nt], hxT_v[:, :, t0:t0 + nt], ['hxT_d'], [f'hxb{a}'])
                            dma(r64[a][:, :, 0:nt], rope64.rearrange("c p t -> p c t")[:, :, t0:t0 + nt], ['rope64'], [f'r64_{a}'])
                            for hh in range(10):
                                isk = hh >= 8
                                c_main = (hh * 64) if not isk else (512 + (hh - 8) * 64)
                                c_sw = (768 + hh * 64) if not isk else (768 + 512 + (hh - 8) * 64)
                                e = ev % 2; ev += 1
                                pa, pb = 2 * e, 2 * e + 1
                                for k in range(KC):
                                    mm(PS[pa][0:64, 0:nt], wq[:, k, c_main:c_main + 64], hxb[a][:, k, 0:nt], k == 0, k == KC - 1, ['wq', f'hxb{a}'], [f'ps{pa}'])
                                for k in range(KC):
                                    mm(PS[pb][0:64, 0:nt], wq[:, k, c_sw:c_sw + 64], hxb[a][:, k, 0:nt], k == 0, k == KC - 1, ['wq', f'hxb{a}'], [f'ps{pb}'])
                                if mix == 0:
                                    gcol = 2 if isk else 0
                                    act(sq6[e][:, 0:nt], PS[pa][0:64, 0:nt], AF.Square, [f'ps{pa}'], [f'sq6_{e}'])
                                    mm(PS[4 + e][0:64, 0:nt], ones_f[0:64, 0:64], sq6[e][:, 0:nt], True, True, [f'sq6_{e}', 'cst_f'], [f'ps{4 + e}'])
                                    rstd_from_ps(PS[4 + e][0:64, 0:nt], rs6[e][:, 0:nt], 64, [f'ps{4 + e}', 'epsc'], [f'rs6_{e}'], np_=64)
                                    stt(qf[e][:, 0:nt], PS[pa][0:64, 0:nt], gq[:, gcol:gcol + 1], r64[a][:, 0, 0:nt], ALU.mult, ALU.mult,
                                        [f'ps{pa}', 'gq', f'r64_{a}'], [f'qf{e}'])
                                    stt(t64[e][:, 0:nt], PS[pb][0:64, 0:nt], gq[:, gcol + 1:gcol + 2], r64[a][:, 1, 0:nt], ALU.mult, ALU.mult,
                                        [f'ps{pb}', 'gq', f'r64_{a}'], [f't64_{e}'])
                                    tt('pool', t64[e][:, 0:nt], t64[e][:, 0:nt], qf[e][:, 0:nt], ALU.add, [f't64_{e}', f'qf{e}'], [f't64_{e}'])
                                    tt('pool', o64[e][:, 0:nt], t64[e][:, 0:nt], rs6[e][:, 0:nt], ALU.mult, [f't64_{e}', f'rs6_{e}'], [f'o64_{e}'])
                                else:
                                    tt('dve', qf[e][:, 0:nt], PS[pa][0:64, 0:nt], r64[a][:, 0, 0:nt], ALU.mult, [f'ps{pa}', f'r64_{a}'], [f'qf{e}'])
                                    tt('dve', t64[e][:, 0:nt], PS[pb][0:64, 0:nt], r64[a][:, 1, 0:nt], ALU.mult, [f'ps{pb}', f'r64_{a}'], [f't64_{e}'])
                                    tt('pool', o64[e][:, 0:nt], t64[e][:, 0:nt], qf[e][:, 0:nt], ALU.add, [f't64_{e}', f'qf{e}'], [f'o64_{e}'])
                                dst = qT_dst[hh] if not isk else kT_dst[hh - 8]
                                dma(dst[:, t0:t0 + nt], o64[e][:, 0:nt], [f'o64_{e}'], ['qk_d'], q='st', eng='pool')
                            for j in range(nt // 128):
                                va = j % 2
                                for k in range(KC):
                                    mm(PS[6][:, 0:128], hxb[a][:, k, j * 128:(j + 1) * 128], wq[:, k, 640:768], k == 0, k == KC - 1, ['wq', f'hxb{a}'], ['ps6'])
                                cp('act', vst[va][:, :, 0:64], PS[6][:, 0:128].rearrange("p (h d) -> p h d", h=2), ['ps6'], [f'vst{va}'])
                                dma(v_dst[t0 + j * 128:t0 + (j + 1) * 128, :, :], vst[va][:], [f'vst{va}'], ['v_d'], q='st', eng='pool')
                    s.barrier()

                with _hp() as P:
                    wg = SB("wg", [128, KC, 3072], BF16, P)
                    wiv = w_in[l].rearrange("(k p) n -> p k n", p=128)
                    for q4 in range(4):
                        load_w_bf16(wg[:, :, q4 * 768:(q4 + 1) * 768], wiv[:, :, 2208 + q4 * 768:2208 + (q4 + 1) * 768], 'wg', 'w_in')
                    hxb = [SB(f"hxb{i}", [128, KC, 512], BF16, P) for i in range(2)]
                    og = [SB(f"og{i}", [128, 4, 512], BF16, P) for i in range(2)]
                    gT_v = gT_d.rearrange("(c p) t -> p c t", p=128)
                    ev = 0
                    for bi, blk in enumerate(blocks):
                        t0, nt, isc = blk
                        a = bi % 2
                        dma(hxb[a][:, :, 0:nt], hxT_v[:, :, t0:t0 + nt], ['hxT_d'], [f'hxb{a}'])
                        for c4 in range(6):
                            e = ev % 2; ev += 1
                            for cc in range(4):
                                c = c4 * 4 + cc
                                pb = c % 4
                                for k in range(KC):
                                    mm(PS[pb][:, 0:nt], wg[:, k, c * 128:(c + 1) * 128], hxb[a][:, k, 0:nt], k == 0, k == KC - 1, ['wg', f'hxb{a}'], [f'ps{pb}'])
                                act(og[e][:, cc, 0:nt], PS[pb][:, 0:nt], AF.Sigmoid, [f'ps{pb}'], [f'og{e}'])
                            dma(gT_v[:, c4 * 4:(c4 + 1) * 4, t0:t0 + nt], og[e][:, :, 0:nt], [f'og{e}'], ['gT_d'], q='st', eng='pool')
                s.barrier()

                with _hp() as P:
                    kn_sb = [SB(f"kn_sb{i}", [64, T], BF16, P) for i in range(2)]
                    kr_sb = SB("kr_sb", [32, T], BF16, P)
                    v_sb = [SB(f"v_sb{i}", [128, NT, 65], BF16, P) for i in range(2)]
                    qn_sb = [SB(f"qn_sb{i}", [64, 512], BF16, P) for i in range(2)]
                    qr_sb = [SB(f"qr_sb{i}", [32, 512], BF16, P) for i in range(2)]
                    p_sb = [SB(f"p_sb{i}", [128, 512], BF16, P) for i in range(4)]
                    rz = [SB(f"rz{i}", [128, 4], F32, P) for i in range(2)]
                    o_sb = [SB(f"o_sb{i}", [128, 4, 64], BF16, P) for i in range(2)]
                    oT_sb = [SB(f"oT_sb{i}", [64, 512], BF16, P) for i in range(2)]
                    dma(kr_sb[:], krT_d, ['krT_d'], ['kr_sb'])
                    cnt = {'kv': 0, 'q': 0, 'p': 0, 'o': 0, 's': 0}
                    OB = [PS[2], PS[3], PS[4], PS[5]]

                    def attn_unit(qparts, kparts, vt, nq, ktiles, scale, sink_col, out_dst):
                        nj = nq // 128
                        nk = len(ktiles)
                        pis = {}
                        LA = 2
                        for ki in range(nk + LA):
                            if ki < nk:
                                kt, mask = ktiles[ki]
                                sb_i = (0, 1, 6)[cnt['s'] % 3]; cnt['s'] += 1
                                sps = PS[sb_i]
                                nparts = len(qparts) + (1 if mask is not None else 0)
                                for pi, ((qap, qk), (kap, kk)) in enumerate(zip(qparts, kparts)):
                                    mm(sps[:, 0:nq], kap[:, kt * 128:(kt + 1) * 128], qap[:, 0:nq], pi == 0, pi == nparts - 1,
                                       [qk, kk], [f'ps{sb_i}'])
                                if mask is not None:
                                    mm(sps[:, 0:nq], ident_b, mask, False, True, ['cst_b'], [f'ps{sb_i}'])
                                pi_ = cnt['p'] % 4; cnt['p'] += 1
                                pis[ki] = pi_
                                act(p_sb[pi_][:, 0:nq], sps[:, 0:nq], AF.Exp, [f'ps{sb_i}'], [f'p_sb{pi_}'], scale=scale)
                            if ki >= LA:
                                kp = ki - LA
                                ktp = ktiles[kp][0]
                                pp_ = pis[kp]
                                for j in range(nj):
                                    mm(OB[j][:, 0:65], p_sb[pp_][:, j * 128:(j + 1) * 128], vt[0][:, ktp, :], kp == 0, kp == nk - 1,
                                       [f'p_sb{pp_}', vt[1]], [f'ps{2 + j}'])
                        oi = cnt['o'] % 2; cnt['o'] += 1
                        for j in range(nj):
                            if sink_col is not None:
                                ts('dve', rz[oi][:, j:j + 1], OB[j][:, 64:65], esink[:, sink_col:sink_col + 1], ALU.add,
                                   [f'ps{2 + j}', 'esink'], [f'rz{oi}'])
                                s.op('dve', nc.vector.reciprocal, out=rz[oi][:, j:j + 1], in_=rz[oi][:, j:j + 1], reads=[f'rz{oi}'], writes=[f'rz{oi}'])
                            else:
                                s.op('dve', nc.vector.reciprocal, out=rz[oi][:, j:j + 1], in_=OB[j][:, 64:65], reads=[f'ps{2 + j}'], writes=[f'rz{oi}'])
                            ts('dve', o_sb[oi][:, j, :], OB[j][:, 0:64], rz[oi][:, j:j + 1], ALU.mult, [f'ps{2 + j}', f'rz{oi}'], [f'o_sb{oi}'])
                            s.op('pe', nc.tensor.transpose, PSB[0:64, j * 128:(j + 1) * 128], o_sb[oi][:, j, :], ident_b,
                                 reads=[f'o_sb{oi}', 'cst_b'], writes=['psb'])
                        cp('act', oT_sb[oi][:, 0:nq], PSB[0:64, 0:nq], ['psb'], [f'oT_sb{oi}'])
                        dma(out_dst, oT_sb[oi][:, 0:nq], [f'oT_sb{oi}'], ['oT_d'], q='st', eng='pool')

                    mixers = [(0, qnT_d, knT_d, vA_d, 96 ** -0.5), (1, qbT_d, kbT_d, vB_d, 0.125), (2, qwT_d, kwT_d, vW_d, 0.125)]
                    for (m, qd, kd, vd, scale) in mixers:
                        for h in range(8):
                            kvh = h if m == 0 else h // 4
                            if m == 0 or h % 4 == 0:
                                kvi = cnt['kv'] % 2; cnt['kv'] += 1
                                dma(kn_sb[kvi][:], kd[kvh], ['qk_d', 'knT_d'], [f'kn_sb{kvi}'])
                                dma(v_sb[kvi][:], vd[:, kvh, :].rearrange("(n p) c -> p n c", p=128), ['v_d', 'vA_d'], [f'v_sb{kvi}'])
                            kparts = [(kn_sb[kvi], f'kn_sb{kvi}')]
                            if m == 0:
                                kparts.append((kr_sb, 'kr_sb'))
                            vt = (v_sb[kvi], f'v_sb{kvi}')
                            for bi, blk in enumerate(blocks):
                                t0, nt, isc = blk
                                if last and (isc or t0 >= TL // 2):
                                    continue
                                qi = cnt['q'] % 2; cnt['q'] += 1
                                dma(qn_sb[qi][:, 0:nt], qd[h, :, t0:t0 + nt], ['qk_d', 'qnT_d'], [f'qn_sb{qi}'])
                                qparts = [(qn_sb[qi], f'qn_sb{qi}')]
                                if m == 0:
                                    dma(qr_sb[qi][:, 0:nt], qrT_d[h, :, t0:t0 + nt], ['qrT_d'], [f'qr_sb{qi}'])
                                    qparts.append((qr_sb[qi], f'qr_sb{qi}'))
                                ctx_tiles = [(NTL, None), (NTL + 1, None)]
                                dst = oT_d[m][h * 64:(h + 1) * 64, t0:t0 + nt]
                                sink_col = (l * 8 + h) if m == 2 else None
                                if last and (isc or t0 >= TL // 2):
                                    continue
                                if isc:
                                    attn_unit(qparts, kparts, vt, nt, ctx_tiles, scale, sink_col, dst)
                                elif m < 2:
                                    attn_unit(qparts, kparts, vt, nt, [(kt, None) for kt in range(NT)], scale, sink_col, dst)
                                else:
                                    for j in range(nt // 128):
                                        n = t0 // 128 + j
                                        kts = []
                                        lm = seam_wrapL if n == 0 else (seam_midL if n == NTL // 2 else nm_lo)
                                        rm = seam_wrapR if n == NTL - 1 else (seam_midR if n == NTL // 2 - 1 else nm_hi)
                                        kts.append(((n - 1) % NTL, lm))
                                        kts.append((n, None))
                                        kts.append(((n + 1) % NTL, rm))
                                        kts += ctx_tiles
                                        qp = [(qparts[0][0][:, j * 128:(j + 1) * 128], qparts[0][1])]
                                        attn_unit(qp, kparts, vt, 128, kts, scale, sink_col,
                                                  oT_d[m][h * 64:(h + 1) * 64, t0 + j * 128:t0 + (j + 1) * 128])
                s.barrier()

                with _hp() as P:
                    wo3 = [SB(f"wo3_{m}", [128, 4, D], BF16, P) for m in range(3)]
                    wo = SB("wo", [128, KC, D], BF16, P)
                    for m, wsrc in enumerate([w_oa, w_ob, w_ow]):
                        load_w_bf16(wo3[m][:], wsrc[l].rearrange("(k p) n -> p k n", p=128), f'wo3_{m}', 'w_o3')
                    load_w_bf16(wo[:], w_o[l].rearrange("(k p) n -> p k n", p=128), 'wo', 'w_o')
                    oTb = [[SB(f"oTb{m}_{i}", [128, 4, 512], BF16, P) for i in range(2)] for m in range(3)]
                    gtb = [SB(f"gtb{i}", [128, 24, 512], BF16, P) for i in range(2)]
                    xs2 = [SB(f"xs2_{i}", [128, KC, 512], F32, P) for i in range(2)]
                    mT = SB("mT", [128, KC, 512], BF16, P)
                    acc = [SB(f"acc{i}", [128, 512], F32, P) for i in range(2)]
                    tmp = [SB(f"tmp{i}", [128, 512], F32, P) for i in range(2)]
                    gT_v = gT_d.rearrange("(c p) t -> p c t", p=128)
                    for bi, blk in enumerate(blocks):
                        t0, nt, isc = blk
                        if (isc and (last or not ctx_update)) or (last and t0 >= TL // 2):
                            continue
                        a = bi % 2
                        for m in range(3):
                            dma(oTb[m][a][:, :, 0:nt], oT_d[m].rearrange("(c p) t -> p c t", p=128)[:, :, t0:t0 + nt], ['oT_d'], [f'oTb{m}_{a}'])
                        dma(gtb[a][:, :, 0:nt], gT_v[:, :, t0:t0 + nt], ['gT_d'], [f'gtb{a}'])
                        dma(xs2[a][:, :, 0:nt], xT_v[:, :, t0:t0 + nt], ['xT_d'], [f'xs2_{a}'])
                        for dc in range(KC):
                            e = dc % 2
                            for m in range(3):
                                pb = m
                                for c in range(4):
                                    mm(PS[pb][:, 0:nt], wo3[m][:, c, dc * 128:(dc + 1) * 128], oTb[m][a][:, c, 0:nt], c == 0, c == 3,
                                       [f'wo3_{m}', f'oTb{m}_{a}'], [f'ps{pb}'])
                            tt('dve', acc[e][:, 0:nt], PS[0][:, 0:nt], gtb[a][:, dc, 0:nt], ALU.mult, ['ps0', f'gtb{a}'], [f'acc{e}'])
                            tt('dve', tmp[e][:, 0:nt], PS[1][:, 0:nt], gtb[a][:, 8 + dc, 0:nt], ALU.mult, ['ps1', f'gtb{a}'], [f'tmp{e}'])
                            tt('pool', acc[e][:, 0:nt], acc[e][:, 0:nt], tmp[e][:, 0:nt], ALU.add, [f'acc{e}', f'tmp{e}'], [f'acc{e}'])
                            tt('dve', tmp[e][:, 0:nt], PS[2][:, 0:nt], gtb[a][:, 16 + dc, 0:nt], ALU.mult, ['ps2', f'gtb{a}'], [f'tmp{e}'])
                            tt('pool', mT[:, dc, 0:nt], acc[e][:, 0:nt], tmp[e][:, 0:nt], ALU.add, [f'acc{e}', f'tmp{e}'], ['mT'])
                        for dc in range(KC):
                            pb = 4 + dc % 2
                            for k in range(KC):
                                mm(PS[pb][:, 0:nt], wo[:, k, dc * 128:(dc + 1) * 128], mT[:, k, 0:nt], k == 0, k == KC - 1, ['wo', 'mT'], [f'ps{pb}'])
                            stt(xs2[a][:, dc, 0:nt], PS[pb][:, 0:nt], modT[:, 16 + dc, isc:isc + 1], xs2[a][:, dc, 0:nt], ALU.mult, ALU.add,
                                [f'ps{pb}', 'modT', f'xs2_{a}'], [f'xs2_{a}'])
                        dma(xT_v[:, :, t0:t0 + nt], xs2[a][:, :, 0:nt], [f'xs2_{a}'], ['xT_d'], q='st', eng='pool')
                s.barrier()

            if do_peer:
                with _hp() as P:
                    NTl = norm_tiles(P)
                    xs = NTl['xs']
                    wpq = [SB(f"wpq{i}", [128, KC, 512], BF16, P) for i in range(2)]
                    skt = SB("skt", [128, 16, 128], BF16, P)
                    load_w_bf16(skt[:], skT[l], 'skt', 'skT')
                    hx2 = SB("hx2", [128, KC, 512], BF16, P)
                    qT = xs[:].bitcast(BF16).rearrange("p k (a t) -> p (k a) t", a=2)
                    Ssb = SB("Ssb", [128, 16, 128], F32, P)
                    Stmp = SB("Stmp", [128, 4, 128], F32, P)
                    T16 = SB("T16", [128, 16, 16], F32, P)
                    Eall = SB("Eall", [128, 4, 16, 128], F32, P)
                    ET16 = SB("ET16", [128, 16, 16], F32, P)
                    cand = SB("cand", [128, 8, 256], BF16, P)
                    ctmp = SB("ctmp", [128, 4, 256], BF16, P)
                    CT16 = SB("CT16", [128, 8, 16], F32, P)
                    theta = SB("theta", [128, 4, 8], F32, P)
                    zz = SB("zz", [128, 8], F32, P)
                    Dm = SB("Dm", [128, 4, 8, 128], BF16, P)
                    Pp = [SB(f"Pp{i}", [128, 2, GI, 128], BF16, P) for i in range(4)]
                    Gm = [SB(f"Gm{i}", [128, 4, 8, GI * 128], BF16, P) for i in range(2)]
                    UT = [SB(f"UT{i}", [128, KC, GI * 128], BF16, P) for i in range(2)]
                    Vg = [SB(f"Vg{i}", [128, GI, D], BF16, P) for i in range(3)]
                    a_sb = [SB(f"a_sb{i}", [128, GI, 512], F32, P) for i in range(2)]
                    GaT = [SB(f"GaT{i}", [128, GI, 512], BF16, P) for i in range(2)]
                    accp = SB("accp", [128, KC, 512], F32, P)
                    ubf_v = ubf_d.rearrange("(g p) (k e) -> g p k e", p=128, k=KC)
                    vbf_v = vbf_d.rearrange("(g p) (c d) -> g p c d", p=128, c=GI)
                    gcount = 0
                    pcount = [0]
                    mcount = [0]
                    fcount = [0]
                    ftmp = [SB(f"ftmp{i}", [128, 512], F32, P) for i in range(2)]
                    slot = {}
                    for bi, blk in enumerate(blocks):
                        t0, nt, isc = blk
                        if (isc and (last or not ctx_update)) or (last and t0 >= TL // 2):
                            continue
                        ntl = nt // 128
                        load_x_and_norm(NTl, blk, 1, hx2, 'hx2')
                        for hp in range(16):
                            pb = hp % 2
                            wi = (hp // 4) % 2
                            if hp % 4 == 0:
                                load_w_bf16(wpq[wi][:], w_pq[l].rearrange("(k p) n -> p k n", p=128)[:, :, hp * 128:(hp + 4) * 128], f'wpq{wi}', 'w_pq')
                            for k in range(KC):
                                mm(PS[pb][:, 0:nt], wpq[wi][:, k, (hp % 4) * 128:(hp % 4 + 1) * 128], hx2[:, k, 0:nt], k == 0, k == KC - 1, [f'wpq{wi}', 'hx2'], [f'ps{pb}'])
                            cp('act' if hp % 2 else 'dve', qT[:, hp, 0:nt], PS[pb][:, 0:nt], [f'ps{pb}', 'xs'], ['xs'])
                        for j in range(ntl):
                            for q4 in range(4):
                                pb = 2 + q4 % 2
                                for i4 in range(4):
                                    hp = q4 * 4 + i4
                                    mm(PS[pb][:, i4 * 128:(i4 + 1) * 128], qT[:, hp, j * 128:(j + 1) * 128], skt[:, hp, :], True, True,
                                       ['xs', 'skt'], [f'ps{pb}'])
                                cp('act', Ssb[:, q4 * 4:(q4 + 1) * 4, :], PS[pb][:, :].rearrange("p (a b) -> p a b", a=4), [f'ps{pb}'], ['Ssb'])
                            for hq in range(4):
                                hps = [hq * 4 + i for i in range(4)]
                                for i, hp in enumerate(hps):
                                    s.op('dve', nc.vector.max, out=T16[:, hp, 0:8], in_=Ssb[:, hp, :], reads=['Ssb'], writes=[f'T16a_{hp}'])
                                for i, hp in enumerate(hps):
                                    s.op('dve', nc.vector.match_replace, out=Stmp[:, i, :], in_to_replace=T16[:, hp, 0:8], in_values=Ssb[:, hp, :],
                                         imm_value=-1e30, reads=[f'T16a_{hp}', 'Ssb'], writes=[f'Stmp{i}'])
                                for i, hp in enumerate(hps):
                                    s.op('dve', nc.vector.max, out=T16[:, hp, 8:16], in_=Stmp[:, i, :], reads=[f'Stmp{i}'], writes=[f'T16b_{hp}'])
                            tt('dve', Ssb[:], Ssb[:], T16[:, :, 0:1].to_broadcast([128, 16, 128]), ALU.subtract, ['Ssb'] + [f'T16a_{q}' for q in range(16)] + [f'T16b_{q}' for q in range(16)], ['Ssb'])
                            act(Eall[:, j, :, :], Ssb[:], AF.Exp, ['Ssb'], ['Eall'])
                            tt('dve', ET16[:], T16[:], T16[:, :, 0:1].to_broadcast([128, 16, 16]), ALU.subtract, [f'T16a_{q}' for q in range(16)] + [f'T16b_{q}' for q in range(16)], ['ET16'])
                            act(ET16[:], ET16[:], AF.Exp, ['ET16'], ['ET16'])
                            e4 = ET16[:].rearrange("p (h two) k -> p h two k", two=2)
                            for h in range(8):
                                tt('pool', cand[:, h, :].rearrange("p (a b) -> p a b", a=16),
                                   e4[:, h, 0, :].unsqueeze(2).to_broadcast([128, 16, 16]),
                                   e4[:, h, 1, :].unsqueeze(1).to_broadcast([128, 16, 16]), ALU.mult, ['ET16'], [f'cand{h}'])
                            for hq in range(2):
                                hs = [hq * 4 + i for i in range(4)]
                                for i, h in enumerate(hs):
                                    s.op('dve', nc.vector.max, out=CT16[:, h, 0:8], in_=cand[:, h, :], reads=[f'cand{h}'], writes=[f'CT16a_{h}'])
                                for i, h in enumerate(hs):
                                    s.op('dve', nc.vector.match_replace, out=ctmp[:, i, :], in_to_replace=CT16[:, h, 0:8], in_values=cand[:, h, :],
                                         imm_value=-1e30, reads=[f'CT16a_{h}', f'cand{h}'], writes=[f'ctmp{i}'])
                                for i, h in enumerate(hs):
                                    s.op('dve', nc.vector.max, out=CT16[:, h, 8:16], in_=ctmp[:, i, :], reads=[f'ctmp{i}'], writes=[f'CT16b_{h}'])
                            cp('dve', theta[:, j, :], CT16[:, :, 15], [f'CT16a_{q}' for q in range(8)] + [f'CT16b_{q}' for q in range(8)], ['theta'])
                            s.op('dve', nc.vector.tensor_reduce, out=zz[:], in_=CT16[:], axis=AX.X, op=ALU.add, reads=[f'CT16a_{q}' for q in range(8)] + [f'CT16b_{q}' for q in range(8)], writes=['zz'])
                            s.op('dve', nc.vector.reciprocal, out=zz[:], in_=zz[:], reads=['zz'], writes=['zz'])
                            for h in range(8):
                                ts('dve', Dm[:, j, h, :], ident_f, zz[:, h:h + 1], ALU.mult, ['cst_f', 'zz'], [f'Dm{j}_{h}'])
                        for it in range(NG + 2):
                            g = it
                            if g < NG:
                                ui = gcount % 2
                                vi = gcount % 3
                                gi = gcount % 2
                                gcount += 1
                                slot[g] = (ui, vi, gi)
                                dma(UT[ui][:], ubf_v[g], ['bg:ubf'], [f'UT{ui}'])
                                dma(Vg[vi][:], vbf_v[g], ['bg:vbf'], [f'Vg{vi}'])
                                for j in range(ntl):
                                    for h2 in range(4):
                                        pi_ = pcount[0] % 4; pcount[0] += 1
                                        tt('pool', Pp[pi_][:],
                                           Eall[:, j, 4 * h2:4 * h2 + 3:2, g * GI:(g + 1) * GI].unsqueeze(3).to_broadcast([128, 2, GI, 128]),
                                           Eall[:, j, 4 * h2 + 1:4 * h2 + 4:2, :].unsqueeze(2).to_broadcast([128, 2, GI, 128]),
                                           ALU.mult, ['Eall'], [f'Pp{pi_}'])
                                        on_pool = False
                                        for hh in range(2):
                                            h = 2 * h2 + hh
                                            pv = Pp[pi_][:, hh, :, :].rearrange("p a b -> p (a b)")
                                            if on_pool:
                                                mi = mcount[0] % 2; mcount[0] += 1
                                                tt('pool', mk[mi][:], pv, theta[:, j, h:h + 1].to_broadcast([128, GI * 128]), ALU.is_ge,
                                                   [f'Pp{pi_}', 'theta'], [f'mk{mi}'])
                                                tt('pool', Gm[gi][:, j, h, :], mk[mi][:], pv, ALU.mult, [f'mk{mi}', f'Pp{pi_}'], [f'Gm{gi}_{j}_{h}'])
                                            else:
                                                stt(Gm[gi][:, j, h, :], pv, theta[:, j, h:h + 1], pv, ALU.is_ge, ALU.mult,
                                                    [f'Pp{pi_}', 'theta'], [f'Gm{gi}_{j}_{h}'])
                                for c in range(GI):
                                    ab = c % 2
                                    for k in range(KC):
                                        mm(PS[ab][:, 0:nt], UT[ui][:, k, c * 128:(c + 1) * 128], hx2[:, k, 0:nt], k == 0, k == KC - 1,
                                           [f'UT{ui}', 'hx2'], [f'ps{ab}'])
                                    act(a_sb[gi][:, c, 0:nt], PS[ab][:, 0:nt], AF.Gelu, [f'ps{ab}'], [f'a_sb{gi}_{c}'])
                            g1 = it - 1
                            if 0 <= g1 < NG:
                                ui, vi, gi = slot[g1]
                                for c in range(GI):
                                    gb = 2 + c % 2
                                    for j in range(ntl):
                                        for h in range(8):
                                            mm(PS[gb][:, j * 128:(j + 1) * 128], Gm[gi][:, j, h, c * 128:(c + 1) * 128], Dm[:, j, h, :], h == 0, h == 7,
                                               [f'Gm{gi}_{j}_{h}', f'Dm{j}_{h}'], [f'ps{gb}'])
                                    tt('dve', GaT[gi][:, c, 0:nt], PS[gb][:, 0:nt], a_sb[gi][:, c, 0:nt], ALU.mult,
                                       [f'ps{gb}', f'a_sb{gi}_{c}'], [f'GaT{gi}'])
                            g2 = it - 2
                            if 0 <= g2 < NG:
                                ui, vi, gi = slot[g2]
                                for dc in range(KC):
                                    ob = 4 + dc % 2
                                    for c in range(GI):
                                        mm(PS[ob][:, 0:nt], Vg[vi][:, c, dc * 128:(dc + 1) * 128], GaT[gi][:, c, 0:nt], c == 0, c == GI - 1,
                                           [f'Vg{vi}', f'GaT{gi}'], [f'ps{ob}'])
                                    if g2 == 0:
                                        cp('act', accp[:, dc, 0:nt], PS[ob][:, 0:nt], [f'ps{ob}'], [f'accp{dc}'])
                                    elif dc < 6:
                                        fi = fcount[0] % 2; fcount[0] += 1
                                        cp('act', ftmp[fi][:, 0:nt], PS[ob][:, 0:nt], [f'ps{ob}'], [f'ftmp{fi}'])
                                        tt('pool', accp[:, dc, 0:nt], ftmp[fi][:, 0:nt], accp[:, dc, 0:nt], ALU.add, [f'ftmp{fi}', f'accp{dc}'], [f'accp{dc}'])
                                    else:
                                        tt('dve', accp[:, dc, 0:nt], PS[ob][:, 0:nt], accp[:, dc, 0:nt], ALU.add, [f'ps{ob}', f'accp{dc}'], [f'accp{dc}'])
                        dma(xs[:, :, 0:nt], xT_v[:, :, t0:t0 + nt], ['xT_d'], ['xs'])
                        for dc in range(KC):
                            stt(xs[:, dc, 0:nt], accp[:, dc, 0:nt], modT[:, 40 + dc, isc:isc + 1], xs[:, dc, 0:nt], ALU.mult, ALU.add,
                                [f'accp{dc}', 'modT', 'xs'], ['xs'])
                        dma(xT_v[:, :, t0:t0 + nt], xs[:, :, 0:nt], ['xs'], ['xT_d'], q='st', eng='pool')
                s.barrier()

        with _hp() as P:
            xs = SB("xsf", [128, KC, 512], F32, P)
            sq = [SB(f"sqf{i}", [128, 512], F32, P) for i in range(2)]
            rstd = SB("rstdf", [128, 512], F32, P)
            yT = SB("yT", [128, KC, 512], F32, P)
            yo = [SB(f"yo{i}", [128, D], F32, P) for i in range(2)]
            oc = 0
            for bi, blk in enumerate(blocks):
                t0, nt, isc = blk
                if isc or t0 >= TL // 2:
                    continue
                dma(xs[:, :, 0:nt], xT_v[:, :, t0:t0 + nt], ['xT_d'], ['xsf'])
                for k in range(KC):
                    a = k % 2
                    act(sq[a][:, 0:nt], xs[:, k, 0:nt], AF.Square, ['xsf'], [f'sqf{a}'])
                    mm(PS[6][:, 0:nt], ones_f, sq[a][:, 0:nt], k == 0, k == KC - 1, [f'sqf{a}', 'cst_f'], ['ps6'])
                rstd_from_ps(PS[6][:, 0:nt], rstd[:, 0:nt], D, ['ps6', 'epsc'], ['rstdf'])
                for k in range(KC):
                    stt(yT[:, k, 0:nt], xs[:, k, 0:nt], gfin[:, k:k + 1], rstd[:, 0:nt], ALU.mult, ALU.mult,
                        ['xsf', 'gfin', 'rstdf'], ['yT'])
                for j in range(nt // 128):
                    a = oc % 2; oc += 1
                    for hk in range(2):
                        pb = 2 * a + hk
                        for kk in range(4):
                            k = hk * 4 + kk
                            s.op('pe', nc.tensor.transpose, PS[pb][:, kk * 128:(kk + 1) * 128],
                                 yT[:, k, j * 128:(j + 1) * 128], ident_f, reads=['yT', 'cst_f'], writes=[f'ps{pb}'])
                        cp('dve' if hk == 0 else 'act', yo[a][:, hk * 512:(hk + 1) * 512], PS[pb][:, :], [f'ps{pb}'], [f'yo{a}'])
                    dma(yout[t0 + j * 128:t0 + (j + 1) * 128, :], yo[a][:], [f'yo{a}'], ['yout'], q='st', eng='pool')
        s.barrier(['sp', 'pool'])
    return nc, s


def _rope_tables(TL, rot_dim):
    rows = TL // GRID_W
    r = np.repeat(np.arange(rows, dtype=np.float32), GRID_W)
    col = np.tile(np.arange(GRID_W, dtype=np.float32), rows)
    n_freq = rot_dim // 4
    inv = (np.float32(10000.0) ** (-np.arange(n_freq, dtype=np.float32) / np.float32(n_freq))).astype(np.float32)
    ang = np.concatenate([r[:, None] * inv, col[:, None] * inv], axis=-1).astype(np.float32)
    cos = np.cos(ang).astype(np.float32)
    sin = np.sin(ang).astype(np.float32)
    half = rot_dim // 2
    T = TL + CTX
    tab = np.zeros((2, rot_dim, T), np.float32)
    tab[0, :, TL:] = 1.0
    tab[0, :half, :TL] = cos.T
    tab[0, half:, :TL] = cos.T
    tab[1, :half, :TL] = -sin.T
    tab[1, half:, :TL] = sin.T
    return tab


def _swap_heads(w, hd):
    n = w.shape[-1] // hd
    w4 = w.reshape(w.shape[:-1] + (n, 2, hd // 2))
    return np.ascontiguousarray(w4[..., ::-1, :]).reshape(w.shape)


def _consts(rank):
    c = np.zeros((128, 8, 128), np.float32)
    c[:, 0, :] = np.eye(128, dtype=np.float32)
    c[:, 1, :] = 1.0
    jj = np.arange(128)[:, None]
    ii = np.arange(128)[None, :]
    c[:, 2, :] = np.where(jj < ii, NEG, 0.0)
    c[:, 3, :] = np.where(jj > ii, NEG, 0.0)
    c[:, 4, :] = c[:, 2, :] if rank == 1 else NEG
    c[:, 5, :] = c[:, 3, :] if rank == 1 else NEG
    c[:, 6, :] = c[:, 2, :] if rank == 0 else NEG
    c[:, 7, :] = c[:, 3, :] if rank == 0 else NEG
    return c


def prep_shared(inp, TL, L):
    f = lambda a: np.ascontiguousarray(np.asarray(a, dtype=np.float32))
    w_in = f(inp['w_in'])[:L]
    sh = {}
    sh['w_mod'] = f(inp['w_mod'])[:L]
    sh['b_modT'] = f(f(inp['b_mod'])[:L].reshape(L, 48, 128).transpose(0, 2, 1))
    gvec = np.zeros((L, 128, 21), np.float32)
    gvec[:, :, 0:8] = f(inp['g_attn'])[:L].reshape(L, 8, 128).transpose(0, 2, 1)
    gvec[:, :, 8:16] = f(inp['g_ffn'])[:L].reshape(L, 8, 128).transpose(0, 2, 1)
    gvec[:, :, 16:19] = f(inp['g_cq'])[:L].reshape(L, 3, 128).transpose(0, 2, 1)
    gvec[:, :, 19:21] = f(inp['g_ckv'])[:L].reshape(L, 2, 128).transpose(0, 2, 1)
    sh['gvec'] = gvec
    gqk = np.zeros((L, 64, 4), np.float32)
    gqn = f(inp['g_qn'])[:L]; gkn = f(inp['g_kn'])[:L]
    gqk[:, :, 0] = gqn; gqk[:, :, 1] = _swap_heads(gqn, 64)
    gqk[:, :, 2] = gkn; gqk[:, :, 3] = _swap_heads(gkn, 64)
    sh['gqk'] = gqk
    sh['g_finalT'] = f(f(inp['g_final']).reshape(8, 128).T)
    sh['sinkb'] = f(np.broadcast_to(f(inp['sink'])[:L].reshape(1, L * 8), (128, L * 8)))
    sh['w_in'] = w_in
    kr_sw = _swap_heads(w_in[:, :, 640:672], 32)
    b_sw = _swap_heads(w_in[:, :, 672:672 + 640], 64)
    w_sw = _swap_heads(w_in[:, :, 1440:1440 + 640], 64)
    sh['w_in_sw'] = f(np.concatenate([kr_sw, b_sw, w_sw], axis=-1))
    w_uq = f(inp['w_uq'])[:L]
    sh['w_uq'] = w_uq
    rope_cols = w_uq.reshape(L, 384, 8, 96)[:, :, :, 64:96]
    sh['w_uq_sw'] = f(_swap_heads(f(rope_cols).reshape(L, 384, 256), 32))
    sh['w_ukv'] = f(inp['w_ukv'])[:L]
    for k in ['w_oa', 'w_ob', 'w_ow', 'w_o', 'w_pq']:
        sh[k] = f(inp[k])[:L]
    GI = 2
    NG = 128 // GI
    pu = f(inp['peer_u'])[:L].reshape(L, NG, GI * 128, 8, 128)
    sh['peer_uG'] = f(pu.transpose(0, 1, 4, 3, 2)).reshape(L, NG * 128, 8 * GI * 128)
    pv = f(inp['peer_v'])[:L].reshape(L, NG, GI, 128, 1024)
    sh['peer_vG'] = f(pv.transpose(0, 1, 3, 2, 4)).reshape(L, NG * 128, GI * 1024)
    sk = f(inp['sub_keys'])[:L].reshape(L, 16, 128, 128)
    sh['skT'] = f(sk.transpose(0, 3, 1, 2))
    sh['rope64'] = _rope_tables(TL, 64)
    sh['rope32'] = _rope_tables(TL, 32)
    return sh


def _roll_lat(a, rank, TL, axis):
    if rank == 0:
        return np.ascontiguousarray(a)
    idx = np.concatenate([np.arange(TL // 2, TL), np.arange(0, TL // 2), np.arange(TL, a.shape[axis])])
    return np.ascontiguousarray(np.take(a, idx, axis=axis))


def prep_core(inp, sh, b, rank=0):
    d = dict(sh)
    TL = inp['x'].shape[1]
    xin = np.concatenate([np.asarray(inp['x'][b], np.float32), np.asarray(inp['ctx'][b], np.float32)], axis=0)
    d['xin'] = _roll_lat(xin, rank, TL, 0)
    d['rope64'] = _roll_lat(sh['rope64'], rank, TL, 2)
    d['rope32'] = _roll_lat(sh['rope32'], rank, TL, 2)
    d['consts'] = _consts(rank)
    cT2 = np.zeros((128, 8, 2), np.float32)
    cT2[:, :, 0] = np.asarray(inp['c'][b], np.float32).reshape(8, 128).T
    cT2[:, :, 1] = np.asarray(inp['c_ctx'], np.float32).reshape(8, 128).T
    d['cT2'] = cT2
    return d


_CACHE = {}


def kernel(**inputs):
    B, TL, _ = inputs['x'].shape
    L = inputs['w_mod'].shape[0]
    key = (TL, L)
    if key not in _CACHE:
        _CACHE[key] = build(TL, L)
    nc, _ = _CACHE[key]
    sh = prep_shared(inputs, TL, L)
    in_maps = [prep_core(inputs, sh, c // 2, c % 2) for c in range(8)]
    res = run_bass_kernel_spmd(nc, in_maps, core_ids=list(range(8)))
    out = np.concatenate([np.asarray(res.results[c]['yout'], dtype=np.float32) for c in range(2 * B)], axis=0)
    return out.reshape(B, TL, D)
```

```python
import contextlib
import os
import numpy as np
import ml_dtypes
import concourse.bass as bass
import concourse.mybir as mybir
from concourse.bass_utils import run_bass_kernel_spmd

F32 = mybir.dt.float32
BF16 = mybir.dt.bfloat16
AF = mybir.ActivationFunctionType
ALU = mybir.AluOpType
AX = mybir.AxisListType

D = 1024
KC = 8
CTX = 256
GRID_W = 64
EPS = 1e-6
NEG = -30000.0


class Sched:
    LIMIT = int(os.environ.get("SEM_LIMIT", "30000"))

    def __init__(self, nc):
        self.nc = nc
        self.engs = {'pe': nc.tensor, 'act': nc.scalar, 'dve': nc.vector, 'pool': nc.gpsimd, 'sp': nc.sync}
        self.sem = {}
        self.cnt = {}
        self.ver = {}
        self.allsems = []
        self.known = {e: {} for e in self.engs}
        self.last_w = {}
        self.readers = {}
        self.n_inst = 0
        self.n_wait = 0
        self.neng = {e: 0 for e in self.engs}
        self.dcount = {}
        self.DMA_RING = 8
        self.marks = []

    def _bump(self, name, step):
        if name not in self.sem or self.cnt[name] + step > self.LIMIT:
            self.ver[name] = self.ver.get(name, -1) + 1
            h = self.nc.alloc_semaphore(name=f"s_{name}_{self.ver[name]}")
            self.sem[name] = h
            self.cnt[name] = 0
            self.allsems.append([h, 0, name])
            self._cur = None
        self.cnt[name] += step
        for rec in self.allsems:
            if rec[0] is self.sem[name]:
                rec[1] = self.cnt[name]
        return self.sem[name], self.cnt[name]

    def op(self, eng, fn, *args, reads=(), writes=(), dma=None, **kw):
        e = self.engs[eng]
        deps = {}

        def add(d):
            if d is None:
                return
            k = id(d[0])
            if k not in deps or deps[k][1] < d[1]:
                deps[k] = d
        for r in reads:
            add(self.last_w.get(r))
        for w in writes:
            add(self.last_w.get(w))
            for d in self.readers.get(w, {}).values():
                add(d)
        kn = self.known[eng]
        for k, (h, v, own) in deps.items():
            if own == 'pe' and eng == 'pe' and dma is None:
                continue
            if kn.get(k, 0) >= v:
                continue
            e.wait_ge(h, v)
            self.n_wait += 1
            kn[k] = v
        if dma is not None:
            R = self.DMA_RING
            i = self.dcount.get(dma, 0)
            self.dcount[dma] = i + 1
            sname = f"{dma}#{i % R}"
            if sname in self.sem and self.cnt[sname] > 0:
                hp_, vp_ = self.sem[sname], self.cnt[sname]
                if kn.get(id(hp_), 0) < vp_:
                    e.wait_ge(hp_, vp_)
                    self.n_wait += 1
                    kn[id(hp_)] = vp_
        inst = fn(*args, **kw)
        if dma is None:
            self.neng[eng] += 1
        if dma is not None:
            h, v = self._bump(sname, 16)
            inst.then_inc(h, 16)
            rec = (h, v, 'dma:' + sname)
        else:
            h, v = self._bump(eng, 1)
            inst.then_inc(h, 1)
            rec = (h, v, eng)
        for w in writes:
            self.last_w[w] = rec
            self.readers[w] = {}
        for r in reads:
            self.readers.setdefault(r, {})[rec[2]] = rec
        self.n_inst += 1
        return inst

    def barrier(self, engs=None, label=""):
        self.marks.append((label, dict(self.neng)))
        for eng in (engs or self.engs):
            e = self.engs[eng]
            kn = self.known[eng]
            for h, v, nm in self.allsems:
                if nm.startswith('bg'):
                    continue
                if v > 0 and kn.get(id(h), 0) < v:
                    e.wait_ge(h, v)
                    kn[id(h)] = v
                    self.n_wait += 1
        if engs is None:
            self.last_w = {k: v for k, v in self.last_w.items() if str(k).startswith('bg:')}
            self.readers = {k: v for k, v in self.readers.items() if str(k).startswith('bg:')}


def _hp():
    return contextlib.ExitStack()


def build(TL, L, do_attn=True, do_peer=True, ctx_update=True):
    T = TL + CTX
    NT = T // 128
    NTL = TL // 128
    blocks = [(i * 512, 512, 0) for i in range(TL // 512)] + [(TL, CTX, 1)]
    nc = bass.Bass("TRN2", target_bir_lowering=False)
    s = Sched(nc)

    def din(name, shape, dt=F32):
        return nc.dram_tensor(name, list(shape), dt, kind="ExternalInput").ap()

    def dscr(name, shape, dt):
        return nc.dram_tensor(name, list(shape), dt).ap()

    xin = din("xin", [T, D])
    cT2 = din("cT2", [128, KC, 2])
    w_mod = din("w_mod", [L, D, 6 * D])
    b_modT = din("b_modT", [L, 128, 48])
    gvec = din("gvec", [L, 128, 21])
    gqk = din("gqk", [L, 64, 4])
    g_finalT = din("g_finalT", [128, KC])
    sinkb = din("sinkb", [128, L * 8])
    w_in = din("w_in", [L, D, 5280])
    w_in_sw = din("w_in_sw", [L, D, 32 + 640 + 640])
    w_uq = din("w_uq", [L, 384, 768])
    w_uq_sw = din("w_uq_sw", [L, 384, 256])
    w_ukv = din("w_ukv", [L, 256, 1024])
    w_oa = din("w_oa", [L, 512, D])
    w_ob = din("w_ob", [L, 512, D])
    w_ow = din("w_ow", [L, 512, D])
    w_o = din("w_o", [L, D, D])
    w_pq = din("w_pq", [L, D, 2048])
    skT = din("skT", [L, 128, 16, 128])
    GI = 2
    NG = 128 // GI
    peer_uG = din("peer_uG", [L, NG * 128, KC * GI * 128])
    peer_vG = din("peer_vG", [L, NG * 128, GI * D])
    ubf_d = dscr("ubf_d", [NG * 128, KC * GI * 128], BF16)
    vbf_d = dscr("vbf_d", [NG * 128, GI * D], BF16)
    rope64 = din("rope64", [2, 64, T])
    rope32 = din("rope32", [2, 32, T])
    consts = din("consts", [128, 8, 128])
    yout = nc.dram_tensor("yout", [TL // 2, D], F32, kind="ExternalOutput").ap()

    xT_d = dscr("xT_d", [D, T], F32)
    hxT_d = dscr("hxT_d", [D, T], BF16)
    qnT_d = dscr("qnT_d", [8, 64, T], BF16)
    qrT_d = dscr("qrT_d", [8, 32, T], BF16)
    knT_d = dscr("knT_d", [8, 64, T], BF16)
    krT_d = dscr("krT_d", [32, T], BF16)
    vA_d = dscr("vA_d", [T, 8, 65], BF16)
    qbT_d = dscr("qbT_d", [8, 64, T], BF16)
    kbT_d = dscr("kbT_d", [2, 64, T], BF16)
    vB_d = dscr("vB_d", [T, 2, 65], BF16)
    qwT_d = dscr("qwT_d", [8, 64, T], BF16)
    kwT_d = dscr("kwT_d", [2, 64, T], BF16)
    vW_d = dscr("vW_d", [T, 2, 65], BF16)
    gT_d = dscr("gT_d", [3 * D, T], BF16)
    oT_d = [dscr(f"oT_d{m}", [512, T], BF16) for m in range(3)]

    xT_v = xT_d.rearrange("(k p) t -> p k t", p=128)
    hxT_v = hxT_d.rearrange("(k p) t -> p k t", p=128)

    with contextlib.ExitStack() as G:
        uid = [0]

        def SB(name, shape, dt, st=G):
            uid[0] += 1
            return st.enter_context(nc.sbuf_tensor(f"{name}_u{uid[0]}", list(shape), dt))

        def PSt(name, shape, dt, st=G):
            return st.enter_context(nc.psum_tensor(name, list(shape), dt))

        PS = [PSt(f"ps{i}", [128, 512], F32) for i in range(7)]
        PSB = PSt("psb", [128, 1024], BF16)

        def dma(out, in_, reads, writes, q='ld', eng='sp'):
            fn = nc.sync.dma_start if eng == 'sp' else nc.gpsimd.dma_start
            return s.op(eng, fn, out=out, in_=in_, reads=reads, writes=writes, dma=q + eng)

        def mm(out, lhsT, rhs, start, stop, reads, writes):
            return s.op('pe', nc.tensor.matmul, out, lhsT=lhsT, rhs=rhs, start=start, stop=stop,
                        reads=reads, writes=writes)

        def act(out, in_, func, reads, writes, **kw):
            return s.op('act', nc.scalar.activation, out=out, in_=in_, func=func, reads=reads, writes=writes, **kw)

        def tt(eng, out, in0, in1, op, reads, writes):
            fn = nc.vector.tensor_tensor if eng == 'dve' else nc.gpsimd.tensor_tensor
            return s.op(eng, fn, out=out, in0=in0, in1=in1, op=op, reads=reads, writes=writes)

        def ts(eng, out, in0, s1, op0, reads, writes, s2=None, op1=None):
            fn = nc.vector.tensor_scalar if eng == 'dve' else nc.gpsimd.tensor_scalar
            kw = {}
            if op1 is not None:
                kw['op1'] = op1
            return s.op(eng, fn, out=out, in0=in0, scalar1=s1, scalar2=s2, op0=op0, reads=reads, writes=writes, **kw)

        def stt(out, in0, scalar, in1, op0, op1, reads, writes):
            return s.op('dve', nc.vector.scalar_tensor_tensor, out=out, in0=in0, scalar=scalar, in1=in1,
                        op0=op0, op1=op1, reads=reads, writes=writes)

        def cp(eng, out, in_, reads, writes):
            if eng == 'act':
                return s.op('act', nc.scalar.copy, out=out, in_=in_, reads=reads, writes=writes)
            fn = nc.vector.tensor_copy if eng == 'dve' else nc.gpsimd.tensor_copy
            return s.op(eng, fn, out=out, in_=in_, reads=reads, writes=writes)

        cst_f = SB("cst_f", [128, 8, 128], F32)
        cst_b = SB("cst_b", [128, 8, 128], BF16)
        dma(cst_f[:], consts, ['consts'], ['cst_f'])
        cp('dve', cst_b[:], cst_f[:], ['cst_f'], ['cst_b'])
        ident_f = cst_f[:, 0, :]
        ones_f = cst_f[:, 1, :]
        ident_b = cst_b[:, 0, :]
        nm_lo = cst_b[:, 2, :]
        nm_hi = cst_b[:, 3, :]
        seam_wrapL, seam_wrapR, seam_midL, seam_midR = (cst_b[:, 4 + i, :] for i in range(4))
        epsc = SB("epsc", [128, 1], F32)
        s.op('dve', nc.vector.memset, epsc[:], EPS, writes=['epsc'])
        scT = SB("scT", [128, KC, 2], F32)
        dma(scT[:], cT2, ['cT2'], ['scT'])
        act(scT[:], scT[:], AF.Silu, ['scT'], ['scT'])
        esink = SB("esink", [128, L * 8], F32)
        dma(esink[:], sinkb, ['sinkb'], ['esink'])
        act(esink[:], esink[:], AF.Exp, ['esink'], ['esink'])
        gfin = SB("gfin", [128, KC], F32)
        dma(gfin[:], g_finalT, ['gfin_d'], ['gfin'])
        modT = SB("modT", [128, 48, 2], F32)
        gsc = SB("gsc", [128, 2, KC, 2], F32)
        gv = SB("gv", [128, 21], F32)
        gq = SB("gq", [64, 4], F32)
        bmod = SB("bmod", [128, 48], F32)

        def rstd_from_ps(ps_ap, out_ap, n, rkeys, wkeys, np_=128):
            act(out_ap, ps_ap, AF.Sqrt, rkeys, wkeys, scale=1.0 / n, bias=epsc[0:np_, :])
            s.op('dve', nc.vector.reciprocal, out=out_ap, in_=out_ap, reads=wkeys, writes=wkeys)

        with _hp() as P:
            xin_t = [SB(f"xin_t{i}", [128, D], F32, P) for i in range(2)]
            xo_t = [SB(f"xo_t{i}", [128, KC, 128], F32, P) for i in range(2)]
            for j in range(NT):
                a = j % 2
                dma(xin_t[a][:], xin[j * 128:(j + 1) * 128, :], ['xin'], [f'xin_t{a}'])
                for hk in range(2):
                    pst = PS[2 * a + hk]
                    for kk in range(4):
                        k = hk * 4 + kk
                        s.op('pe', nc.tensor.transpose, pst[:, kk * 128:(kk + 1) * 128],
                             xin_t[a][:, k * 128:(k + 1) * 128], ident_f,
                             reads=[f'xin_t{a}', 'cst_f'], writes=[f'ps{2 * a + hk}'])
                    cp('dve' if hk == 0 else 'act', xo_t[a][:, hk * 4:(hk + 1) * 4, :],
                       pst[:].rearrange("p (k t) -> p k t", k=4), [f'ps{2 * a + hk}'], [f'xo_t{a}'])
                dma(xT_v[:, :, j * 128:(j + 1) * 128], xo_t[a][:], [f'xo_t{a}'], ['xT_d'], q='st', eng='pool')
        s.barrier()

        def load_x_and_norm(P_tiles, blk, sub, hx_out, hx_key):
            t0, nt, isc = blk
            xs, sq, rstd, tmpf = P_tiles['xs'], P_tiles['sq'], P_tiles['rstd'], P_tiles['tmpf']
            dma(xs[:, :, 0:nt], xT_v[:, :, t0:t0 + nt], ['xT_d'], ['xs'])
            for k in range(KC):
                a = k % 2
                act(sq[a][:, 0:nt], xs[:, k, 0:nt], AF.Square, ['xs'], [f'sq{a}'])
                mm(PS[6][:, 0:nt], ones_f, sq[a][:, 0:nt], k == 0, k == KC - 1, [f'sq{a}', 'cst_f'], ['ps6'])
            rstd_from_ps(PS[6][:, 0:nt], rstd[:, 0:nt], D, ['ps6', 'epsc'], ['rstd'])
            shoff = 0 if sub == 0 else 24
            for k in range(KC):
                a = k % 2
                stt(tmpf[a][:, 0:nt], xs[:, k, 0:nt], gsc[:, sub, k, isc:isc + 1], rstd[:, 0:nt], ALU.mult, ALU.mult,
                    ['xs', 'gsc', 'rstd'], [f'tmpf{a}'])
                act(hx_out[:, k, 0:nt], tmpf[a][:, 0:nt], AF.Identity, [f'tmpf{a}', 'modT'], [hx_key],
                    bias=modT[:, shoff + k, isc:isc + 1])

        def norm_tiles(P):
            return {'xs': SB("xs", [128, KC, 512], F32, P),
                    'sq': [SB(f"sq{i}", [128, 512], F32, P) for i in range(2)],
                    'rstd': SB("rstd", [128, 512], F32, P),
                    'tmpf': [SB(f"tmpf{i}", [128, 512], F32, P) for i in range(2)]}

        def load_w_bf16(dst, src, key, srckey):
            dma(dst, src, [srckey], [key], q='w', eng='pool')

        for l in range(L):
            last = (l == L - 1)
            with _hp() as P:
                wm = [SB(f"wm{i}", [128, KC, 512], F32, P) for i in range(2)]
                if do_peer:
                    nrow = NG * 128
                    for q8 in range(8):
                        r0, r1 = q8 * nrow // 8, (q8 + 1) * nrow // 8
                        dma(ubf_d[r0:r1, :], peer_uG[l][r0:r1, :], ['peer_uG', 'bg:ld'], ['bg:ubf'], q='bg', eng='pool')
                        dma(vbf_d[r0:r1, :], peer_vG[l][r0:r1, :], ['peer_vG', 'bg:ld'], ['bg:vbf'], q='bg', eng='pool')
                dma(bmod[:], b_modT[l], ['b_modT'], ['bmod'])
                dma(gv[:], gvec[l], ['gvec'], ['gv'])
                dma(gq[:], gqk[l], ['gqk'], ['gq'])
                for nb in range(12):
                    a = nb % 2
                    dma(wm[a][:], w_mod[l].rearrange("(k p) n -> p k n", p=128)[:, :, nb * 512:(nb + 1) * 512],
                        ['w_mod'], [f'wm{a}'])
                    for c in range(4):
                        n = nb * 4 + c
                        for k in range(KC):
                            mm(PS[a][:, c * 2:c * 2 + 2], wm[a][:, k, c * 128:(c + 1) * 128], scT[:, k, :],
                               k == 0, k == KC - 1, [f'wm{a}', 'scT'], [f'ps{a}'])
                        ts('dve', modT[:, n, :], PS[a][:, c * 2:c * 2 + 2], bmod[:, n:n + 1], ALU.add,
                           [f'ps{a}', 'bmod'], ['modT'])
                for sub in range(2):
                    scoff = 8 if sub == 0 else 32
                    goff = 0 if sub == 0 else 8
                    for v in range(2):
                        ts('dve', gsc[:, sub, :, v], modT[:, scoff:scoff + 8, v], 1.0, ALU.add,
                           ['modT'], ['gsc'])
                        tt('dve', gsc[:, sub, :, v], gsc[:, sub, :, v], gv[:, goff:goff + 8], ALU.mult,
                           ['gsc', 'gv'], ['gsc'])
            s.barrier()

            if do_attn:
                with _hp() as P:
                    NTl = norm_tiles(P)
                    hxs = [SB(f"hxs{i}", [128, KC, 512], BF16, P) for i in range(2)]
                    for bi, blk in enumerate(blocks):
                        t0, nt, isc = blk
                        a = bi % 2
                        load_x_and_norm(NTl, blk, 0, hxs[a], f'hxs{a}')
                        dma(hxT_v[:, :, t0:t0 + nt], hxs[a][:, :, 0:nt], [f'hxs{a}'], ['hxT_d'], q='st', eng='pool')
                s.barrier()

                with _hp() as P:
                    wA = SB("wA", [128, KC, 704], BF16, P)
                    wuq = SB("wuq", [128, 3, 1024], BF16, P)
                    wukv = SB("wukv", [128, 2, 1024], BF16, P)
                    wiv = w_in[l].rearrange("(k p) n -> p k n", p=128)
                    load_w_bf16(wA[:, :, 0:672], wiv[:, :, 0:672], 'wA', 'w_in')
                    load_w_bf16(wA[:, :, 672:704], w_in_sw[l].rearrange("(k p) n -> p k n", p=128)[:, :, 0:32], 'wA', 'w_in_sw')
                    load_w_bf16(wuq[:, :, 0:768], w_uq[l].rearrange("(k p) n -> p k n", p=128), 'wuq', 'w_uq')
                    load_w_bf16(wuq[:, :, 768:1024], w_uq_sw[l].rearrange("(k p) n -> p k n", p=128), 'wuq', 'w_uq_sw')
                    load_w_bf16(wukv[:], w_ukv[l].rearrange("(k p) n -> p k n", p=128), 'wukv', 'w_ukv')
                    hxb = [SB(f"hxb{i}", [128, KC, 512], BF16, P) for i in range(2)]
                    r32 = [SB(f"r32_{i}", [32, 2, 512], F32, P) for i in range(2)]
                    cqf = SB("cqf", [128, 5, 512], F32, P)
                    cqn = SB("cqn", [128, 5, 512], BF16, P)
                    sqa = [SB(f"sqa{i}", [128, 512], F32, P) for i in range(2)]
                    rsa = SB("rsa", [128, 2, 512], F32, P)
                    t32 = [SB(f"t32_{i}", [32, 512], F32, P) for i in range(2)]
                    o32 = [SB(f"o32_{i}", [32, 512], BF16, P) for i in range(2)]
                    o64 = [SB(f"o64_{i}", [64, 512], BF16, P) for i in range(2)]
                    vst = [SB(f"vst{i}", [128, 8, 65], BF16, P) for i in range(2)]
                    for i in range(2):
                        s.op('pool', nc.gpsimd.memset, vst[i][:], 1.0, writes=[f'vst{i}'])
                    ev = 0
                    for bi, blk in enumerate(blocks):
                        t0, nt, isc = blk
                        a = bi % 2
                        dma(hxb[a][:, :, 0:nt], hxT_v[:, :, t0:t0 + nt], ['hxT_d'], [f'hxb{a}'])
                        dma(r32[a][:, :, 0:nt], rope32.rearrange("c p t -> p c t")[:, :, t0:t0 + nt], ['rope32'], [f'r32_{a}'])
                        for c in range(5):
                            pb = c % 2
                            for k in range(KC):
                                mm(PS[pb][:, 0:nt], wA[:, k, c * 128:(c + 1) * 128], hxb[a][:, k, 0:nt], k == 0, k == KC - 1,
                                   ['wA', f'hxb{a}'], [f'ps{pb}'])
                            cp('dve', cqf[:, c, 0:nt], PS[pb][:, 0:nt], [f'ps{pb}'], ['cqf'])
                        for grp, (c0, c1, n) in enumerate([(0, 3, 384), (3, 5, 256)]):
                            for c in range(c0, c1):
                                sa = c % 2
                                act(sqa[sa][:, 0:nt], cqf[:, c, 0:nt], AF.Square, ['cqf'], [f'sqa{sa}'])
                                mm(PS[2 + grp][:, 0:nt], ones_f, sqa[sa][:, 0:nt], c == c0, c == c1 - 1,
                                   [f'sqa{sa}', 'cst_f'], [f'ps{2 + grp}'])
                            rstd_from_ps(PS[2 + grp][:, 0:nt], rsa[:, grp, 0:nt], n, [f'ps{2 + grp}', 'epsc'], ['rsa'])
                            for c in range(c0, c1):
                                stt(cqn[:, c, 0:nt], cqf[:, c, 0:nt], gv[:, 16 + c:17 + c], rsa[:, grp, 0:nt], ALU.mult, ALU.mult,
                                    ['cqf', 'gv', 'rsa'], ['cqn'])
                        for k in range(KC):
                            mm(PS[4][0:32, 0:nt], wA[:, k, 640:672], hxb[a][:, k, 0:nt], k == 0, k == KC - 1, ['wA', f'hxb{a}'], ['ps4'])
                        for k in range(KC):
                            mm(PS[5][0:32, 0:nt], wA[:, k, 672:704], hxb[a][:, k, 0:nt], k == 0, k == KC - 1, ['wA', f'hxb{a}'], ['ps5'])
                        e = ev % 2; ev += 1
                        tt('dve', t32[0][:, 0:nt], PS[4][0:32, 0:nt], r32[a][:, 0, 0:nt], ALU.mult, ['ps4', f'r32_{a}'], ['t32_0'])
                        tt('dve', t32[1][:, 0:nt], PS[5][0:32, 0:nt], r32[a][:, 1, 0:nt], ALU.mult, ['ps5', f'r32_{a}'], ['t32_1'])
                        tt('pool', o32[e][:, 0:nt], t32[0][:, 0:nt], t32[1][:, 0:nt], ALU.add, ['t32_0', 't32_1'], [f'o32_{e}'])
                        dma(krT_d[:, t0:t0 + nt], o32[e][:, 0:nt], [f'o32_{e}'], ['krT_d'], q='st', eng='pool')
                        for h in range(8):
                            pb = h % 2
                            for c in range(3):
                                mm(PS[pb][0:64, 0:nt], wuq[:, c, h * 96:h * 96 + 64], cqn[:, c, 0:nt], c == 0, c == 2, ['wuq', 'cqn'], [f'ps{pb}'])
                            e = h % 2
                            cp('act', o64[e][:, 0:nt], PS[pb][0:64, 0:nt], [f'ps{pb}'], [f'o64_{e}'])
                            dma(qnT_d[h, :, t0:t0 + nt], o64[e][:, 0:nt], [f'o64_{e}'], ['qnT_d'], q='st', eng='pool')
                            for c in range(3):
                                mm(PS[4][0:32, 0:nt], wuq[:, c, h * 96 + 64:h * 96 + 96], cqn[:, c, 0:nt], c == 0, c == 2, ['wuq', 'cqn'], ['ps4'])
                            for c in range(3):
                                mm(PS[5][0:32, 0:nt], wuq[:, c, 768 + h * 32:768 + h * 32 + 32], cqn[:, c, 0:nt], c == 0, c == 2, ['wuq', 'cqn'], ['ps5'])
                            e = ev % 2; ev += 1
                            tt('dve', t32[0][:, 0:nt], PS[4][0:32, 0:nt], r32[a][:, 0, 0:nt], ALU.mult, ['ps4', f'r32_{a}'], ['t32_0'])
                            tt('dve', t32[1][:, 0:nt], PS[5][0:32, 0:nt], r32[a][:, 1, 0:nt], ALU.mult, ['ps5', f'r32_{a}'], ['t32_1'])
                            tt('pool', o32[e][:, 0:nt], t32[0][:, 0:nt], t32[1][:, 0:nt], ALU.add, ['t32_0', 't32_1'], [f'o32_{e}'])
                            dma(qrT_d[h, :, t0:t0 + nt], o32[e][:, 0:nt], [f'o32_{e}'], ['qrT_d'], q='st', eng='pool')
                            pb = 2 + h % 2
                            for c in range(2):
                                mm(PS[pb][0:64, 0:nt], wukv[:, c, h * 128:h * 128 + 64], cqn[:, 3 + c, 0:nt], c == 0, c == 1, ['wukv', 'cqn'], [f'ps{pb}'])
                            e2 = (h + 1) % 2
                            cp('act', o64[e2][:, 0:nt], PS[pb][0:64, 0:nt], [f'ps{pb}'], [f'o64_{e2}'])
                            dma(knT_d[h, :, t0:t0 + nt], o64[e2][:, 0:nt], [f'o64_{e2}'], ['knT_d'], q='st', eng='pool')
                        for j in range(nt // 128):
                            va = j % 2
                            for c in range(2):
                                mm(PS[6][:, :].rearrange("p (h d) -> p h d", h=8), cqn[:, 3 + c, j * 128:(j + 1) * 128],
                                   wukv[:, c, :].rearrange("p (h d) -> p h d", h=8)[:, :, 64:128], c == 0, c == 1, ['wukv', 'cqn'], ['ps6'])
                            cp('dve', vst[va][:, :, 0:64], PS[6][:, :].rearrange("p (h d) -> p h d", h=8), ['ps6'], [f'vst{va}'])
                            dma(vA_d[t0 + j * 128:t0 + (j + 1) * 128, :, :], vst[va][:], [f'vst{va}'], ['vA_d'], q='st', eng='pool')
                s.barrier()

                for mix in range(2):
                    with _hp() as P:
                        col0 = 672 if mix == 0 else 1440
                        sw0 = 32 if mix == 0 else 672
                        qT_dst, kT_dst, v_dst = (qbT_d, kbT_d, vB_d) if mix == 0 else (qwT_d, kwT_d, vW_d)
                        wq = SB("wq", [128, KC, 1408], BF16, P)
                        wiv = w_in[l].rearrange("(k p) n -> p k n", p=128)
                        load_w_bf16(wq[:, :, 0:768], wiv[:, :, col0:col0 + 768], 'wq', 'w_in')
                        load_w_bf16(wq[:, :, 768:1408], w_in_sw[l].rearrange("(k p) n -> p k n", p=128)[:, :, sw0:sw0 + 640], 'wq', 'w_in_sw')
                        hxb = [SB(f"hxb{i}", [128, KC, 512], BF16, P) for i in range(2)]
                        r64 = [SB(f"r64_{i}", [64, 2, 512], F32, P) for i in range(2)]
                        qf = [SB(f"qf{i}", [64, 512], F32, P) for i in range(2)]
                        sq6 = [SB(f"sq6_{i}", [64, 512], F32, P) for i in range(2)]
                        rs6 = [SB(f"rs6_{i}", [64, 512], F32, P) for i in range(2)]
                        t64 = [SB(f"t64_{i}", [64, 512], F32, P) for i in range(2)]
                        o64 = [SB(f"o64_{i}", [64, 512], BF16, P) for i in range(2)]
                        vst = [SB(f"vst{i}", [128, 2, 65], BF16, P) for i in range(2)]
                        for i in range(2):
                            s.op('pool', nc.gpsimd.memset, vst[i][:], 1.0, writes=[f'vst{i}'])
                        ev = 0
                        for bi, blk in enumerate(blocks):
                            t0, nt, isc = blk
                            a = bi % 2
                            dma(hxb[a][:, :, 0:nt], hxT_v[:, :, t0:t0 + nt], ['hxT_d'], [f'hxb{a}'])
                            dma(r64[a][:, :, 0:nt], rope64.rearrange("c p t -> p c t")[:, :, t0:t0 + nt], ['rope64'], [f'r64_{a}'])
                            for hh in range(10):
                                isk = hh >= 8
                                c_main = (hh * 64) if not isk else (512 + (hh - 8) * 64)
                                c_sw = (768 + hh * 64) if not isk else (768 + 512 + (hh - 8) * 64)
                                e = ev % 2; ev += 1
                                pa, pb = 2 * e, 2 * e + 1
                                for k in range(KC):
                                    mm(PS[pa][0:64, 0:nt], wq[:, k, c_main:c_main + 64], hxb[a][:, k, 0:nt], k == 0, k == KC - 1, ['wq', f'hxb{a}'], [f'ps{pa}'])
                                for k in range(KC):
                                    mm(PS[pb][0:64, 0:nt], wq[:, k, c_sw:c_sw + 64], hxb[a][:, k, 0:nt], k == 0, k == KC - 1, ['wq', f'hxb{a}'], [f'ps{pb}'])
                                if mix == 0:
                                    gcol = 2 if isk else 0
                                    act(sq6[e][:, 0:nt], PS[pa][0:64, 0:nt], AF.Square, [f'ps{pa}'], [f'sq6_{e}'])
                                    mm(PS[4 + e][0:64, 0:nt], ones_f[0:64, 0:64], sq6[e][:, 0:nt], True, True, [f'sq6_{e}', 'cst_f'], [f'ps{4 + e}'])
                                    rstd_from_ps(PS[4 + e][0:64, 0:nt], rs6[e][:, 0:nt], 64, [f'ps{4 + e}', 'epsc'], [f'rs6_{e}'], np_=64)
                                    stt(qf[e][:, 0:nt], PS[pa][0:64, 0:nt], gq[:, gcol:gcol + 1], r64[a][:, 0, 0:nt], ALU.mult, ALU.mult,
                                        [f'ps{pa}', 'gq', f'r64_{a}'], [f'qf{e}'])
                                    stt(t64[e][:, 0:nt], PS[pb][0:64, 0:nt], gq[:, gcol + 1:gcol + 2], r64[a][:, 1, 0:nt], ALU.mult, ALU.mult,
                                        [f'ps{pb}', 'gq', f'r64_{a}'], [f't64_{e}'])
                                    tt('pool', t64[e][:, 0:nt], t64[e][:, 0:nt], qf[e][:, 0:nt], ALU.add, [f't64_{e}', f'qf{e}'], [f't64_{e}'])
                                    tt('pool', o64[e][:, 0:nt], t64[e][:, 0:nt], rs6[e][:, 0:nt], ALU.mult, [f't64_{e}', f'rs6_{e}'], [f'o64_{e}'])
                                else:
                                    tt('dve', qf[e][:, 0:nt], PS[pa][0:64, 0:nt], r64[a][:, 0, 0:nt], ALU.mult, [f'ps{pa}', f'r64_{a}'], [f'qf{e}'])
                                    tt('dve', t64[e][:, 0:nt], PS[pb][0:64, 0:nt], r64[a][:, 1, 0:nt], ALU.mult, [f'ps{pb}', f'r64_{a}'], [f't64_{e}'])
                                    tt('pool', o64[e][:, 0:nt], t64[e][:, 0:nt], qf[e][:, 0:nt], ALU.add, [f't64_{e}', f'qf{e}'], [f'o64_{e}'])
                                dst = qT_dst[hh] if not isk else kT_dst[hh - 8]
                                dma(dst[:, t0:t0 + nt], o64[e][:, 0:nt], [f'o64_{e}'], ['qk_d'], q='st', eng='pool')
                            for j in range(nt // 128):
                                va = j % 2
                                for k in range(KC):
                                    mm(PS[6][:, 0:128], hxb[a][:, k, j * 128:(j + 1) * 128], wq[:, k, 640:768], k == 0, k == KC - 1, ['wq', f'hxb{a}'], ['ps6'])
                                cp('act', vst[va][:, :, 0:64], PS[6][:, 0:128].rearrange("p (h d) -> p h d", h=2), ['ps6'], [f'vst{va}'])
                                dma(v_dst[t0 + j * 128:t0 + (j + 1) * 128, :, :], vst[va][:], [f'vst{va}'], ['v_d'], q='st', eng='pool')
                    s.barrier()

                with _hp() as P:
                    wg = SB("wg", [128, KC, 3072], BF16, P)
                    wiv = w_in[l].rearrange("(k p) n -> p k n", p=128)
                    for q4 in range(4):
                        load_w_bf16(wg[:, :, q4 * 768:(q4 + 1) * 768], wiv[:, :, 2208 + q4 * 768:2208 + (q4 + 1) * 768], 'wg', 'w_in')
                    hxb = [SB(f"hxb{i}", [128, KC, 512], BF16, P) for i in range(2)]
                    og = [SB(f"og{i}", [128, 4, 512], BF16, P) for i in range(2)]
                    gT_v = gT_d.rearrange("(c p) t -> p c t", p=128)
                    ev = 0
                    for bi, blk in enumerate(blocks):
                        t0, nt, isc = blk
                        a = bi % 2
                        dma(hxb[a][:, :, 0:nt], hxT_v[:, :, t0:t0 + nt], ['hxT_d'], [f'hxb{a}'])
                        for c4 in range(6):
                            e = ev % 2; ev += 1
                            for cc in range(4):
                                c = c4 * 4 + cc
                                pb = c % 4
                                for k in range(KC):
                                    mm(PS[pb][:, 0:nt], wg[:, k, c * 128:(c + 1) * 128], hxb[a][:, k, 0:nt], k == 0, k == KC - 1, ['wg', f'hxb{a}'], [f'ps{pb}'])
                                act(og[e][:, cc, 0:nt], PS[pb][:, 0:nt], AF.Sigmoid, [f'ps{pb}'], [f'og{e}'])
                            dma(gT_v[:, c4 * 4:(c4 + 1) * 4, t0:t0 + nt], og[e][:, :, 0:nt], [f'og{e}'], ['gT_d'], q='st', eng='pool')
                s.barrier()

                with _hp() as P:
                    kn_sb = [SB(f"kn_sb{i}", [96, T], BF16, P) for i in range(2)]
                    kr_sb = SB("kr_sb", [32, T], BF16, P)
                    v_sb = [SB(f"v_sb{i}", [128, NT, 65], BF16, P) for i in range(2)]
                    qn_sb = [SB(f"qn_sb{i}", [96, 512], BF16, P) for i in range(2)]
                    qr_sb = [SB(f"qr_sb{i}", [32, 512], BF16, P) for i in range(2)]
                    p_sb = [SB(f"p_sb{i}", [128, 512], BF16, P) for i in range(4)]
                    rz = [SB(f"rz{i}", [128, 4], F32, P) for i in range(2)]
                    o_sb = [SB(f"o_sb{i}", [128, 4, 64], BF16, P) for i in range(2)]
                    oT_sb = [SB(f"oT_sb{i}", [64, 512], BF16, P) for i in range(2)]
                    dma(kr_sb[:], krT_d, ['krT_d'], ['kr_sb'])
                    cnt = {'kv': 0, 'q': 0, 'p': 0, 'o': 0, 's': 0}
                    OB = [PS[2], PS[3], PS[4], PS[5]]

                    def attn_unit(qparts, kparts, vt, nq, ktiles, scale, sink_col, out_dst):
                        nj = nq // 128
                        nk = len(ktiles)
                        pis = {}
                        LA = 2
                        for ki in range(nk + LA):
                            if ki < nk:
                                kt, mask = ktiles[ki]
                                sb_i = (0, 1, 6)[cnt['s'] % 3]; cnt['s'] += 1
                                sps = PS[sb_i]
                                nparts = len(qparts) + (1 if mask is not None else 0)
                                for pi, ((qap, qk), (kap, kk)) in enumerate(zip(qparts, kparts)):
                                    mm(sps[:, 0:nq], kap[:, kt * 128:(kt + 1) * 128], qap[:, 0:nq], pi == 0, pi == nparts - 1,
                                       [qk, kk], [f'ps{sb_i}'])
                                if mask is not None:
                                    mm(sps[:, 0:nq], ident_b, mask, False, True, ['cst_b'], [f'ps{sb_i}'])
                                pi_ = cnt['p'] % 4; cnt['p'] += 1
                                pis[ki] = pi_
                                act(p_sb[pi_][:, 0:nq], sps[:, 0:nq], AF.Exp, [f'ps{sb_i}'], [f'p_sb{pi_}'], scale=scale)
                            if ki >= LA:
                                kp = ki - LA
                                ktp = ktiles[kp][0]
                                pp_ = pis[kp]
                                for j in range(nj):
                                    mm(OB[j][:, 0:65], p_sb[pp_][:, j * 128:(j + 1) * 128], vt[0][:, ktp, :], kp == 0, kp == nk - 1,
                                       [f'p_sb{pp_}', vt[1]], [f'ps{2 + j}'])
                        oi = cnt['o'] % 2; cnt['o'] += 1
                        for j in range(nj):
                            if sink_col is not None:
                                ts('dve', rz[oi][:, j:j + 1], OB[j][:, 64:65], esink[:, sink_col:sink_col + 1], ALU.add,
                                   [f'ps{2 + j}', 'esink'], [f'rz{oi}'])
                                s.op('dve', nc.vector.reciprocal, out=rz[oi][:, j:j + 1], in_=rz[oi][:, j:j + 1], reads=[f'rz{oi}'], writes=[f'rz{oi}'])
                            else:
                                s.op('dve', nc.vector.reciprocal, out=rz[oi][:, j:j + 1], in_=OB[j][:, 64:65], reads=[f'ps{2 + j}'], writes=[f'rz{oi}'])
                            ts('dve', o_sb[oi][:, j, :], OB[j][:, 0:64], rz[oi][:, j:j + 1], ALU.mult, [f'ps{2 + j}', f'rz{oi}'], [f'o_sb{oi}'])
                            s.op('pe', nc.tensor.transpose, PSB[0:64, j * 128:(j + 1) * 128], o_sb[oi][:, j, :], ident_b,
                                 reads=[f'o_sb{oi}', 'cst_b'], writes=['psb'])
                        cp('act', oT_sb[oi][:, 0:nq], PSB[0:64, 0:nq], ['psb'], [f'oT_sb{oi}'])
                        dma(out_dst, oT_sb[oi][:, 0:nq], [f'oT_sb{oi}'], ['oT_d'], q='st', eng='pool')

                    mixers = [(0, qnT_d, knT_d, vA_d, 96 ** -0.5), (1, qbT_d, kbT_d, vB_d, 0.125), (2, qwT_d, kwT_d, vW_d, 0.125)]
                    for (m, qd, kd, vd, scale) in mixers:
                        for h in range(8):
                            kvh = h if m == 0 else h // 4
                            dk = 96 if m == 0 else 64
                            if m == 0 or h % 4 == 0:
                                kvi = cnt['kv'] % 2; cnt['kv'] += 1
                                dma(kn_sb[kvi][0:64, :], kd[kvh], ['qk_d', 'knT_d'], [f'kn_sb{kvi}'])
                                if m == 0:
                                    dma(kn_sb[kvi][64:96, :], krT_d, ['krT_d'], [f'kn_sb{kvi}'])
                                dma(v_sb[kvi][:], vd[:, kvh, :].rearrange("(n p) c -> p n c", p=128), ['v_d', 'vA_d'], [f'v_sb{kvi}'])
                            kparts = [(kn_sb[kvi][0:dk, :], f'kn_sb{kvi}')]
                            vt = (v_sb[kvi], f'v_sb{kvi}')
                            for bi, blk in enumerate(blocks):
                                t0, nt, isc = blk
                                if last and (isc or t0 >= TL // 2):
                                    continue
                                qi = cnt['q'] % 2; cnt['q'] += 1
                                dma(qn_sb[qi][0:64, 0:nt], qd[h, :, t0:t0 + nt], ['qk_d', 'qnT_d'], [f'qn_sb{qi}'])
                                if m == 0:
                                    dma(qn_sb[qi][64:96, 0:nt], qrT_d[h, :, t0:t0 + nt], ['qrT_d'], [f'qn_sb{qi}'])
                                qparts = [(qn_sb[qi][0:dk, :], f'qn_sb{qi}')]
                                ctx_tiles = [(NTL, None), (NTL + 1, None)]
                                dst = oT_d[m][h * 64:(h + 1) * 64, t0:t0 + nt]
                                sink_col = (l * 8 + h) if m == 2 else None
                                if last and (isc or t0 >= TL // 2):
                                    continue
                                if isc:
                                    attn_unit(qparts, kparts, vt, nt, ctx_tiles, scale, sink_col, dst)
                                elif m < 2:
                                    attn_unit(qparts, kparts, vt, nt, [(kt, None) for kt in range(NT)], scale, sink_col, dst)
                                else:
                                    for j in range(nt // 128):
                                        n = t0 // 128 + j
                                        kts = []
                                        lm = seam_wrapL if n == 0 else (seam_midL if n == NTL // 2 else nm_lo)
                                        rm = seam_wrapR if n == NTL - 1 else (seam_midR if n == NTL // 2 - 1 else nm_hi)
                                        kts.append(((n - 1) % NTL, lm))
                                        kts.append((n, None))
                                        kts.append(((n + 1) % NTL, rm))
                                        kts += ctx_tiles
                                        qp = [(qparts[0][0][:, j * 128:(j + 1) * 128], qparts[0][1])]
                                        attn_unit(qp, kparts, vt, 128, kts, scale, sink_col,
                                                  oT_d[m][h * 64:(h + 1) * 64, t0 + j * 128:t0 + (j + 1) * 128])
                s.barrier()

                with _hp() as P:
                    wo3 = [SB(f"wo3_{m}", [128, 4, D], BF16, P) for m in range(3)]
                    wo = SB("wo", [128, KC, D], BF16, P)
                    for m, wsrc in enumerate([w_oa, w_ob, w_ow]):
                        load_w_bf16(wo3[m][:], wsrc[l].rearrange("(k p) n -> p k n", p=128), f'wo3_{m}', 'w_o3')
                    load_w_bf16(wo[:], w_o[l].rearrange("(k p) n -> p k n", p=128), 'wo', 'w_o')
                    oTb = [[SB(f"oTb{m}_{i}", [128, 4, 512], BF16, P) for i in range(2)] for m in range(3)]
                    gtb = [SB(f"gtb{i}", [128, 24, 512], BF16, P) for i in range(2)]
                    xs2 = [SB(f"xs2_{i}", [128, KC, 512], F32, P) for i in range(2)]
                    mT = SB("mT", [128, KC, 512], BF16, P)
                    acc = [SB(f"acc{i}", [128, 512], F32, P) for i in range(2)]
                    tmp = [SB(f"tmp{i}", [128, 512], F32, P) for i in range(2)]
                    gT_v = gT_d.rearrange("(c p) t -> p c t", p=128)
                    for bi, blk in enumerate(blocks):
                        t0, nt, isc = blk
                        if (isc and (last or not ctx_update)) or (last and t0 >= TL // 2):
                            continue
                        a = bi % 2
                        for m in range(3):
                            dma(oTb[m][a][:, :, 0:nt], oT_d[m].rearrange("(c p) t -> p c t", p=128)[:, :, t0:t0 + nt], ['oT_d'], [f'oTb{m}_{a}'])
                        dma(gtb[a][:, :, 0:nt], gT_v[:, :, t0:t0 + nt], ['gT_d'], [f'gtb{a}'])
                        dma(xs2[a][:, :, 0:nt], xT_v[:, :, t0:t0 + nt], ['xT_d'], [f'xs2_{a}'])
                        for dc in range(KC):
                            e = dc % 2
                            for m in range(3):
                                pb = m
                                for c in range(4):
                                    mm(PS[pb][:, 0:nt], wo3[m][:, c, dc * 128:(dc + 1) * 128], oTb[m][a][:, c, 0:nt], c == 0, c == 3,
                                       [f'wo3_{m}', f'oTb{m}_{a}'], [f'ps{pb}'])
                            tt('dve', acc[e][:, 0:nt], PS[0][:, 0:nt], gtb[a][:, dc, 0:nt], ALU.mult, ['ps0', f'gtb{a}'], [f'acc{e}'])
                            tt('dve', tmp[e][:, 0:nt], PS[1][:, 0:nt], gtb[a][:, 8 + dc, 0:nt], ALU.mult, ['ps1', f'gtb{a}'], [f'tmp{e}'])
                            tt('pool', acc[e][:, 0:nt], acc[e][:, 0:nt], tmp[e][:, 0:nt], ALU.add, [f'acc{e}', f'tmp{e}'], [f'acc{e}'])
                            tt('dve', tmp[e][:, 0:nt], PS[2][:, 0:nt], gtb[a][:, 16 + dc, 0:nt], ALU.mult, ['ps2', f'gtb{a}'], [f'tmp{e}'])
                            tt('pool', mT[:, dc, 0:nt], acc[e][:, 0:nt], tmp[e][:, 0:nt], ALU.add, [f'acc{e}', f'tmp{e}'], ['mT'])
                        for dc in range(KC):
                            pb = 4 + dc % 2
                            for k in range(KC):
                                mm(PS[pb][:, 0:nt], wo[:, k, dc * 128:(dc + 1) * 128], mT[:, k, 0:nt], k == 0, k == KC - 1, ['wo', 'mT'], [f'ps{pb}'])
                            stt(xs2[a][:, dc, 0:nt], PS[pb][:, 0:nt], modT[:, 16 + dc, isc:isc + 1], xs2[a][:, dc, 0:nt], ALU.mult, ALU.add,
                                [f'ps{pb}', 'modT', f'xs2_{a}'], [f'xs2_{a}'])
                        dma(xT_v[:, :, t0:t0 + nt], xs2[a][:, :, 0:nt], [f'xs2_{a}'], ['xT_d'], q='st', eng='pool')
                s.barrier()

            if do_peer:
                with _hp() as P:
                    NTl = norm_tiles(P)
                    xs = NTl['xs']
                    wpq = [SB(f"wpq{i}", [128, KC, 512], BF16, P) for i in range(2)]
                    skt = SB("skt", [128, 16, 128], BF16, P)
                    load_w_bf16(skt[:], skT[l], 'skt', 'skT')
                    hx2 = SB("hx2", [128, KC, 512], BF16, P)
                    qT = xs[:].bitcast(BF16).rearrange("p k (a t) -> p (k a) t", a=2)
                    Ssb = SB("Ssb", [128, 16, 128], F32, P)
                    Stmp = SB("Stmp", [128, 4, 128], F32, P)
                    T16 = SB("T16", [128, 16, 16], F32, P)
                    Eall = SB("Eall", [128, 4, 16, 128], F32, P)
                    ET16 = SB("ET16", [128, 16, 16], F32, P)
                    cand = SB("cand", [128, 8, 256], BF16, P)
                    ctmp = SB("ctmp", [128, 4, 256], BF16, P)
                    CT16 = SB("CT16", [128, 8, 16], F32, P)
                    theta = SB("theta", [128, 4, 8], F32, P)
                    zz = SB("zz", [128, 8], F32, P)
                    Dm = SB("Dm", [128, 4, 8, 128], BF16, P)
                    Pp = [SB(f"Pp{i}", [128, 2, GI, 128], BF16, P) for i in range(4)]
                    Gm = [SB(f"Gm{i}", [128, 4, 8, GI * 128], BF16, P) for i in range(2)]
                    UT = [SB(f"UT{i}", [128, KC, GI * 128], BF16, P) for i in range(2)]
                    Vg = [SB(f"Vg{i}", [128, GI, D], BF16, P) for i in range(3)]
                    a_sb = [SB(f"a_sb{i}", [128, GI, 512], F32, P) for i in range(2)]
                    GaT = [SB(f"GaT{i}", [128, GI, 512], BF16, P) for i in range(2)]
                    accp = SB("accp", [128, KC, 512], F32, P)
                    ubf_v = ubf_d.rearrange("(g p) (k e) -> g p k e", p=128, k=KC)
                    vbf_v = vbf_d.rearrange("(g p) (c d) -> g p c d", p=128, c=GI)
                    gcount = 0
                    pcount = [0]
                    mcount = [0]
                    fcount = [0]
                    ftmp = [SB(f"ftmp{i}", [128, 512], F32, P) for i in range(2)]
                    slot = {}
                    for bi, blk in enumerate(blocks):
                        t0, nt, isc = blk
                        if (isc and (last or not ctx_update)) or (last and t0 >= TL // 2):
                            continue
                        ntl = nt // 128
                        load_x_and_norm(NTl, blk, 1, hx2, 'hx2')
                        for hp in range(16):
                            pb = hp % 2
                            wi = (hp // 4) % 2
                            if hp % 4 == 0:
                                load_w_bf16(wpq[wi][:], w_pq[l].rearrange("(k p) n -> p k n", p=128)[:, :, hp * 128:(hp + 4) * 128], f'wpq{wi}', 'w_pq')
                            for k in range(KC):
                                mm(PS[pb][:, 0:nt], wpq[wi][:, k, (hp % 4) * 128:(hp % 4 + 1) * 128], hx2[:, k, 0:nt], k == 0, k == KC - 1, [f'wpq{wi}', 'hx2'], [f'ps{pb}'])
                            cp('act' if hp % 2 else 'dve', qT[:, hp, 0:nt], PS[pb][:, 0:nt], [f'ps{pb}', 'xs'], ['xs'])
                        for j in range(ntl):
                            for q4 in range(4):
                                pb = 2 + q4 % 2
                                for i4 in range(4):
                                    hp = q4 * 4 + i4
                                    mm(PS[pb][:, i4 * 128:(i4 + 1) * 128], qT[:, hp, j * 128:(j + 1) * 128], skt[:, hp, :], True, True,
                                       ['xs', 'skt'], [f'ps{pb}'])
                                cp('act', Ssb[:, q4 * 4:(q4 + 1) * 4, :], PS[pb][:, :].rearrange("p (a b) -> p a b", a=4), [f'ps{pb}'], ['Ssb'])
                            for hq in range(4):
                                hps = [hq * 4 + i for i in range(4)]
                                for i, hp in enumerate(hps):
                                    s.op('dve', nc.vector.max, out=T16[:, hp, 0:8], in_=Ssb[:, hp, :], reads=['Ssb'], writes=[f'T16a_{hp}'])
                                for i, hp in enumerate(hps):
                                    s.op('dve', nc.vector.match_replace, out=Stmp[:, i, :], in_to_replace=T16[:, hp, 0:8], in_values=Ssb[:, hp, :],
                                         imm_value=-1e30, reads=[f'T16a_{hp}', 'Ssb'], writes=[f'Stmp{i}'])
                                for i, hp in enumerate(hps):
                                    s.op('dve', nc.vector.max, out=T16[:, hp, 8:16], in_=Stmp[:, i, :], reads=[f'Stmp{i}'], writes=[f'T16b_{hp}'])
                            tt('dve', Ssb[:], Ssb[:], T16[:, :, 0:1].to_broadcast([128, 16, 128]), ALU.subtract, ['Ssb'] + [f'T16a_{q}' for q in range(16)] + [f'T16b_{q}' for q in range(16)], ['Ssb'])
                            act(Eall[:, j, :, :], Ssb[:], AF.Exp, ['Ssb'], ['Eall'])
                            tt('dve', ET16[:], T16[:], T16[:, :, 0:1].to_broadcast([128, 16, 16]), ALU.subtract, [f'T16a_{q}' for q in range(16)] + [f'T16b_{q}' for q in range(16)], ['ET16'])
                            act(ET16[:], ET16[:], AF.Exp, ['ET16'], ['ET16'])
                            e4 = ET16[:].rearrange("p (h two) k -> p h two k", two=2)
                            for h in range(8):
                                tt('pool', cand[:, h, :].rearrange("p (a b) -> p a b", a=16),
                                   e4[:, h, 0, :].unsqueeze(2).to_broadcast([128, 16, 16]),
                                   e4[:, h, 1, :].unsqueeze(1).to_broadcast([128, 16, 16]), ALU.mult, ['ET16'], [f'cand{h}'])
                            for hq in range(2):
                                hs = [hq * 4 + i for i in range(4)]
                                for i, h in enumerate(hs):
                                    s.op('dve', nc.vector.max, out=CT16[:, h, 0:8], in_=cand[:, h, :], reads=[f'cand{h}'], writes=[f'CT16a_{h}'])
                                for i, h in enumerate(hs):
                                    s.op('dve', nc.vector.match_replace, out=ctmp[:, i, :], in_to_replace=CT16[:, h, 0:8], in_values=cand[:, h, :],
                                         imm_value=-1e30, reads=[f'CT16a_{h}', f'cand{h}'], writes=[f'ctmp{i}'])
                                for i, h in enumerate(hs):
                                    s.op('dve', nc.vector.max, out=CT16[:, h, 8:16], in_=ctmp[:, i, :], reads=[f'ctmp{i}'], writes=[f'CT16b_{h}'])
                            cp('dve', theta[:, j, :], CT16[:, :, 15], [f'CT16a_{q}' for q in range(8)] + [f'CT16b_{q}' for q in range(8)], ['theta'])
                            s.op('dve', nc.vector.tensor_reduce, out=zz[:], in_=CT16[:], axis=AX.X, op=ALU.add, reads=[f'CT16a_{q}' for q in range(8)] + [f'CT16b_{q}' for q in range(8)], writes=['zz'])
                            s.op('dve', nc.vector.reciprocal, out=zz[:], in_=zz[:], reads=['zz'], writes=['zz'])
                            for h in range(8):
                                ts('dve', Dm[:, j, h, :], ident_f, zz[:, h:h + 1], ALU.mult, ['cst_f', 'zz'], [f'Dm{j}_{h}'])
                        for it in range(NG + 2):
                            g = it
                            if g < NG:
                                ui = gcount % 2
                                vi = gcount % 3
                                gi = gcount % 2
                                gcount += 1
                                slot[g] = (ui, vi, gi)
                                dma(UT[ui][:], ubf_v[g], ['bg:ubf'], [f'UT{ui}'])
                                dma(Vg[vi][:], vbf_v[g], ['bg:vbf'], [f'Vg{vi}'])
                                for j in range(ntl):
                                    for h2 in range(4):
                                        pi_ = pcount[0] % 4; pcount[0] += 1
                                        tt('pool', Pp[pi_][:],
                                           Eall[:, j, 4 * h2:4 * h2 + 3:2, g * GI:(g + 1) * GI].unsqueeze(3).to_broadcast([128, 2, GI, 128]),
                                           Eall[:, j, 4 * h2 + 1:4 * h2 + 4:2, :].unsqueeze(2).to_broadcast([128, 2, GI, 128]),
                                           ALU.mult, ['Eall'], [f'Pp{pi_}'])
                                        on_pool = False
                                        for hh in range(2):
                                            h = 2 * h2 + hh
                                            pv = Pp[pi_][:, hh, :, :].rearrange("p a b -> p (a b)")
                                            if on_pool:
                                                mi = mcount[0] % 2; mcount[0] += 1
                                                tt('pool', mk[mi][:], pv, theta[:, j, h:h + 1].to_broadcast([128, GI * 128]), ALU.is_ge,
                                                   [f'Pp{pi_}', 'theta'], [f'mk{mi}'])
                                                tt('pool', Gm[gi][:, j, h, :], mk[mi][:], pv, ALU.mult, [f'mk{mi}', f'Pp{pi_}'], [f'Gm{gi}_{j}_{h}'])
                                            else:
                                                stt(Gm[gi][:, j, h, :], pv, theta[:, j, h:h + 1], pv, ALU.is_ge, ALU.mult,
                                                    [f'Pp{pi_}', 'theta'], [f'Gm{gi}_{j}_{h}'])
                                for c in range(GI):
                                    ab = c % 2
                                    for k in range(KC):
                                        mm(PS[ab][:, 0:nt], UT[ui][:, k, c * 128:(c + 1) * 128], hx2[:, k, 0:nt], k == 0, k == KC - 1,
                                           [f'UT{ui}', 'hx2'], [f'ps{ab}'])
                                    act(a_sb[gi][:, c, 0:nt], PS[ab][:, 0:nt], AF.Gelu, [f'ps{ab}'], [f'a_sb{gi}_{c}'])
                            g1 = it - 1
                            if 0 <= g1 < NG:
                                ui, vi, gi = slot[g1]
                                for c in range(GI):
                                    gb = 2 + c % 2
                                    for j in range(ntl):
                                        for h in range(8):
                                            mm(PS[gb][:, j * 128:(j + 1) * 128], Gm[gi][:, j, h, c * 128:(c + 1) * 128], Dm[:, j, h, :], h == 0, h == 7,
                                               [f'Gm{gi}_{j}_{h}', f'Dm{j}_{h}'], [f'ps{gb}'])
                                    tt('dve', GaT[gi][:, c, 0:nt], PS[gb][:, 0:nt], a_sb[gi][:, c, 0:nt], ALU.mult,
                                       [f'ps{gb}', f'a_sb{gi}_{c}'], [f'GaT{gi}'])
                            g2 = it - 2
                            if 0 <= g2 < NG:
                                ui, vi, gi = slot[g2]
                                for dc in range(KC):
                                    ob = 4 + dc % 2
                                    for c in range(GI):
                                        mm(PS[ob][:, 0:nt], Vg[vi][:, c, dc * 128:(dc + 1) * 128], GaT[gi][:, c, 0:nt], c == 0, c == GI - 1,
                                           [f'Vg{vi}', f'GaT{gi}'], [f'ps{ob}'])
                                    if g2 == 0:
                                        cp('act', accp[:, dc, 0:nt], PS[ob][:, 0:nt], [f'ps{ob}'], [f'accp{dc}'])
                                    elif dc < 6:
                                        fi = fcount[0] % 2; fcount[0] += 1
                                        cp('act', ftmp[fi][:, 0:nt], PS[ob][:, 0:nt], [f'ps{ob}'], [f'ftmp{fi}'])
                                        tt('pool', accp[:, dc, 0:nt], ftmp[fi][:, 0:nt], accp[:, dc, 0:nt], ALU.add, [f'ftmp{fi}', f'accp{dc}'], [f'accp{dc}'])
                                    else:
                                        tt('dve', accp[:, dc, 0:nt], PS[ob][:, 0:nt], accp[:, dc, 0:nt], ALU.add, [f'ps{ob}', f'accp{dc}'], [f'accp{dc}'])
                        dma(xs[:, :, 0:nt], xT_v[:, :, t0:t0 + nt], ['xT_d'], ['xs'])
                        for dc in range(KC):
                            stt(xs[:, dc, 0:nt], accp[:, dc, 0:nt], modT[:, 40 + dc, isc:isc + 1], xs[:, dc, 0:nt], ALU.mult, ALU.add,
                                [f'accp{dc}', 'modT', 'xs'], ['xs'])
                        dma(xT_v[:, :, t0:t0 + nt], xs[:, :, 0:nt], ['xs'], ['xT_d'], q='st', eng='pool')
                s.barrier()

        with _hp() as P:
            xs = SB("xsf", [128, KC, 512], F32, P)
            sq = [SB(f"sqf{i}", [128, 512], F32, P) for i in range(2)]
            rstd = SB("rstdf", [128, 512], F32, P)
            yT = SB("yT", [128, KC, 512], F32, P)
            yo = [SB(f"yo{i}", [128, D], F32, P) for i in range(2)]
            oc = 0
            for bi, blk in enumerate(blocks):
                t0, nt, isc = blk
                if isc or t0 >= TL // 2:
                    continue
                dma(xs[:, :, 0:nt], xT_v[:, :, t0:t0 + nt], ['xT_d'], ['xsf'])
                for k in range(KC):
                    a = k % 2
                    act(sq[a][:, 0:nt], xs[:, k, 0:nt], AF.Square, ['xsf'], [f'sqf{a}'])
                    mm(PS[6][:, 0:nt], ones_f, sq[a][:, 0:nt], k == 0, k == KC - 1, [f'sqf{a}', 'cst_f'], ['ps6'])
                rstd_from_ps(PS[6][:, 0:nt], rstd[:, 0:nt], D, ['ps6', 'epsc'], ['rstdf'])
                for k in range(KC):
                    stt(yT[:, k, 0:nt], xs[:, k, 0:nt], gfin[:, k:k + 1], rstd[:, 0:nt], ALU.mult, ALU.mult,
                        ['xsf', 'gfin', 'rstdf'], ['yT'])
                for j in range(nt // 128):
                    a = oc % 2; oc += 1
                    for hk in range(2):
                        pb = 2 * a + hk
                        for kk in range(4):
                            k = hk * 4 + kk
                            s.op('pe', nc.tensor.transpose, PS[pb][:, kk * 128:(kk + 1) * 128],
                                 yT[:, k, j * 128:(j + 1) * 128], ident_f, reads=['yT', 'cst_f'], writes=[f'ps{pb}'])
                        cp('dve' if hk == 0 else 'act', yo[a][:, hk * 512:(hk + 1) * 512], PS[pb][:, :], [f'ps{pb}'], [f'yo{a}'])
                    dma(yout[t0 + j * 128:t0 + (j + 1) * 128, :], yo[a][:], [f'yo{a}'], ['yout'], q='st', eng='pool')
        s.barrier(['sp', 'pool'])
    return nc, s


def _rope_tables(TL, rot_dim):
    rows = TL // GRID_W
    r = np.repeat(np.arange(rows, dtype=np.float32), GRID_W)
    col = np.tile(np.arange(GRID_W, dtype=np.float32), rows)
    n_freq = rot_dim // 4
    inv = (np.float32(10000.0) ** (-np.arange(n_freq, dtype=np.float32) / np.float32(n_freq))).astype(np.float32)
    ang = np.concatenate([r[:, None] * inv, col[:, None] * inv], axis=-1).astype(np.float32)
    cos = np.cos(ang).astype(np.float32)
    sin = np.sin(ang).astype(np.float32)
    half = rot_dim // 2
    T = TL + CTX
    tab = np.zeros((2, rot_dim, T), np.float32)
    tab[0, :, TL:] = 1.0
    tab[0, :half, :TL] = cos.T
    tab[0, half:, :TL] = cos.T
    tab[1, :half, :TL] = -sin.T
    tab[1, half:, :TL] = sin.T
    return tab


def _swap_heads(w, hd):
    n = w.shape[-1] // hd
    w4 = w.reshape(w.shape[:-1] + (n, 2, hd // 2))
    return np.ascontiguousarray(w4[..., ::-1, :]).reshape(w.shape)


def _consts(rank):
    c = np.zeros((128, 8, 128), np.float32)
    c[:, 0, :] = np.eye(128, dtype=np.float32)
    c[:, 1, :] = 1.0
    jj = np.arange(128)[:, None]
    ii = np.arange(128)[None, :]
    c[:, 2, :] = np.where(jj < ii, NEG, 0.0)
    c[:, 3, :] = np.where(jj > ii, NEG, 0.0)
    c[:, 4, :] = c[:, 2, :] if rank == 1 else NEG
    c[:, 5, :] = c[:, 3, :] if rank == 1 else NEG
    c[:, 6, :] = c[:, 2, :] if rank == 0 else NEG
    c[:, 7, :] = c[:, 3, :] if rank == 0 else NEG
    return c


def prep_shared(inp, TL, L):
    f = lambda a: np.ascontiguousarray(np.asarray(a, dtype=np.float32))
    w_in = f(inp['w_in'])[:L]
    sh = {}
    sh['w_mod'] = f(inp['w_mod'])[:L]
    sh['b_modT'] = f(f(inp['b_mod'])[:L].reshape(L, 48, 128).transpose(0, 2, 1))
    gvec = np.zeros((L, 128, 21), np.float32)
    gvec[:, :, 0:8] = f(inp['g_attn'])[:L].reshape(L, 8, 128).transpose(0, 2, 1)
    gvec[:, :, 8:16] = f(inp['g_ffn'])[:L].reshape(L, 8, 128).transpose(0, 2, 1)
    gvec[:, :, 16:19] = f(inp['g_cq'])[:L].reshape(L, 3, 128).transpose(0, 2, 1)
    gvec[:, :, 19:21] = f(inp['g_ckv'])[:L].reshape(L, 2, 128).transpose(0, 2, 1)
    sh['gvec'] = gvec
    gqk = np.zeros((L, 64, 4), np.float32)
    gqn = f(inp['g_qn'])[:L]; gkn = f(inp['g_kn'])[:L]
    gqk[:, :, 0] = gqn; gqk[:, :, 1] = _swap_heads(gqn, 64)
    gqk[:, :, 2] = gkn; gqk[:, :, 3] = _swap_heads(gkn, 64)
    sh['gqk'] = gqk
    sh['g_finalT'] = f(f(inp['g_final']).reshape(8, 128).T)
    sh['sinkb'] = f(np.broadcast_to(f(inp['sink'])[:L].reshape(1, L * 8), (128, L * 8)))
    sh['w_in'] = w_in
    kr_sw = _swap_heads(w_in[:, :, 640:672], 32)
    b_sw = _swap_heads(w_in[:, :, 672:672 + 640], 64)
    w_sw = _swap_heads(w_in[:, :, 1440:1440 + 640], 64)
    sh['w_in_sw'] = f(np.concatenate([kr_sw, b_sw, w_sw], axis=-1))
    w_uq = f(inp['w_uq'])[:L]
    sh['w_uq'] = w_uq
    rope_cols = w_uq.reshape(L, 384, 8, 96)[:, :, :, 64:96]
    sh['w_uq_sw'] = f(_swap_heads(f(rope_cols).reshape(L, 384, 256), 32))
    sh['w_ukv'] = f(inp['w_ukv'])[:L]
    for k in ['w_oa', 'w_ob', 'w_ow', 'w_o', 'w_pq']:
        sh[k] = f(inp[k])[:L]
    GI = 2
    NG = 128 // GI
    pu = f(inp['peer_u'])[:L].reshape(L, NG, GI * 128, 8, 128)
    sh['peer_uG'] = f(pu.transpose(0, 1, 4, 3, 2)).reshape(L, NG * 128, 8 * GI * 128)
    pv = f(inp['peer_v'])[:L].reshape(L, NG, GI, 128, 1024)
    sh['peer_vG'] = f(pv.transpose(0, 1, 3, 2, 4)).reshape(L, NG * 128, GI * 1024)
    sk = f(inp['sub_keys'])[:L].reshape(L, 16, 128, 128)
    sh['skT'] = f(sk.transpose(0, 3, 1, 2))
    sh['rope64'] = _rope_tables(TL, 64)
    sh['rope32'] = _rope_tables(TL, 32)
    return sh


def _roll_lat(a, rank, TL, axis):
    if rank == 0:
        return np.ascontiguousarray(a)
    idx = np.concatenate([np.arange(TL // 2, TL), np.arange(0, TL // 2), np.arange(TL, a.shape[axis])])
    return np.ascontiguousarray(np.take(a, idx, axis=axis))


def prep_core(inp, sh, b, rank=0):
    d = dict(sh)
    TL = inp['x'].shape[1]
    xin = np.concatenate([np.asarray(inp['x'][b], np.float32), np.asarray(inp['ctx'][b], np.float32)], axis=0)
    d['xin'] = _roll_lat(xin, rank, TL, 0)
    d['rope64'] = _roll_lat(sh['rope64'], rank, TL, 2)
    d['rope32'] = _roll_lat(sh['rope32'], rank, TL, 2)
    d['consts'] = _consts(rank)
    cT2 = np.zeros((128, 8, 2), np.float32)
    cT2[:, :, 0] = np.asarray(inp['c'][b], np.float32).reshape(8, 128).T
    cT2[:, :, 1] = np.asarray(inp['c_ctx'], np.float32).reshape(8, 128).T
    d['cT2'] = cT2
    return d


_CACHE = {}


def kernel(**inputs):
    B, TL, _ = inputs['x'].shape
    L = inputs['w_mod'].shape[0]
    key = (TL, L)
    if key not in _CACHE:
        _CACHE[key] = build(TL, L)
    nc, _ = _CACHE[key]
    sh = prep_shared(inputs, TL, L)
    in_maps = [prep_core(inputs, sh, c // 2, c % 2) for c in range(8)]
    res = run_bass_kernel_spmd(nc, in_maps, core_ids=list(range(8)))
    out = np.concatenate([np.asarray(res.results[c]['yout'], dtype=np.float32) for c in range(2 * B)], axis=0)
    return out.reshape(B, TL, D)
```
